# Optimizing a Trainium2 kernel written in Bass

```python
import jax, jax.numpy as jnp
from jax import lax
import numpy as np

D_MODEL = 1024
BATCH = 16
SEQ = 2048
DEPTH = 4

CTX_LEN = 256
GRID_W = 64
EPS = 1e-6
ROPE_BASE = 10000.0
Q_BLOCK = 128

MLA_HEADS = 8
MLA_Q_RANK = 384
MLA_KV_RANK = 256
MLA_NOPE = 64
MLA_ROPE = 32
MLA_V = 64
MLA_SCALE = (MLA_NOPE + MLA_ROPE) ** -0.5

GLA_HEADS = 4
GLA_DK = 64
GLA_DV = 128
GLA_GATE_RANK = 16
GLA_GATE_TAU = 16.0
GLA_CHUNK = 64
GLA_QK_W = GLA_HEADS * GLA_DK
GLA_V_W = GLA_HEADS * GLA_DV

NA_HEADS = 8
NA_HEAD_DIM = 64
NA_KH = 8
NA_KW = 16
NA_W = NA_HEADS * NA_HEAD_DIM
NA_SCALE = NA_HEAD_DIM ** -0.5

FFN_HIDDEN = ((8 * D_MODEL + 3 * 256 - 1) // (3 * 256)) * 256

N_BRANCH = 3
MLA_OUT_W = MLA_HEADS * MLA_V
A_SPLITS = (MLA_Q_RANK, MLA_KV_RANK, MLA_ROPE)
B_SPLITS = (GLA_QK_W, GLA_QK_W, GLA_V_W, GLA_V_W, GLA_GATE_RANK, GLA_GATE_RANK)
C_SPLITS = (NA_W, NA_W, NA_W)
GROUP_SPLITS = (sum(A_SPLITS), sum(B_SPLITS), sum(C_SPLITS), N_BRANCH * D_MODEL)
IN_WIDTH = sum(GROUP_SPLITS)

kernel_name = "hybrid_mla_gla_natten_adaln_trunk"


def split_cols(z, sizes):
    idx = np.cumsum(sizes)[:-1].tolist()
    return jnp.split(z, idx, axis=-1)


def rms_norm(x, w):
    xf = x.astype(jnp.float32)
    y = xf * lax.rsqrt(jnp.mean(xf * xf, axis=-1, keepdims=True) + EPS)
    return (y * w.astype(jnp.float32)).astype(x.dtype)


def modulate(h, shift, scale):
    return h * (1 + scale) + shift


def to_heads(t, n_heads):
    b, l, w = t.shape
    return t.reshape(b, l, n_heads, w // n_heads).transpose(0, 2, 1, 3)


def from_heads(t):
    b, h, l, d = t.shape
    return t.transpose(0, 2, 1, 3).reshape(b, l, h * d)


def axial_rope_tables(length):
    t = jnp.arange(length)
    rows = (t // GRID_W).astype(jnp.float32)
    cols = (t % GRID_W).astype(jnp.float32)
    n_freq = MLA_ROPE // 4
    inv_freq = ROPE_BASE ** (-jnp.arange(n_freq, dtype=jnp.float32) / n_freq)
    ang = jnp.concatenate([rows[:, None] * inv_freq, cols[:, None] * inv_freq], axis=-1)
    return jnp.cos(ang), jnp.sin(ang)


def apply_rope(x, cos, sin):
    half = x.shape[-1] // 2
    x1, x2 = x[..., :half], x[..., half:]
    cos = cos.astype(x.dtype)
    sin = sin.astype(x.dtype)
    return jnp.concatenate([x1 * cos - x2 * sin, x1 * sin + x2 * cos], axis=-1)


def dense_attend(q, k, v):
    s = jnp.einsum('bhqd,bhkd->bhqk', q, k).astype(jnp.float32)
    p = jax.nn.softmax(s, axis=-1).astype(v.dtype)
    return jnp.einsum('bhqk,bhkd->bhqd', p, v)


def mla_queries(a, q_norm_w, w_q_up):
    q_down = a[..., :MLA_Q_RANK]
    q = to_heads(rms_norm(q_down, q_norm_w) @ w_q_up, MLA_HEADS)
    return q[..., :MLA_NOPE], q[..., MLA_NOPE:]


def mla_keys(a, kv_norm_w, w_kv_up):
    _, kv_down, k_rope = split_cols(a, A_SPLITS)
    kv = to_heads(rms_norm(kv_down, kv_norm_w) @ w_kv_up, MLA_HEADS)
    return kv[..., :MLA_NOPE], k_rope, kv[..., MLA_NOPE:]


def mla_attend(q_nope, q_rope, k_nope, k_rope, v):
    s = (jnp.einsum('bhqd,bhkd->bhqk', q_nope, k_nope)
         + jnp.einsum('bhqr,bkr->bhqk', q_rope, k_rope)).astype(jnp.float32) * MLA_SCALE
    p = jax.nn.softmax(s, axis=-1).astype(v.dtype)
    return jnp.einsum('bhqk,bhkd->bhqd', p, v)


def mla_mixer(a_lat, a_ctx, q_norm_w, kv_norm_w, w_q_up, w_kv_up, cos, sin, need_ctx_out):
    qn, qr = mla_queries(a_lat, q_norm_w, w_q_up)
    qr = apply_rope(qr, cos, sin)
    kn, kr, v = mla_keys(a_lat, kv_norm_w, w_kv_up)
    kr = apply_rope(kr, cos, sin)
    kn_c, kr_c, v_c = mla_keys(a_ctx, kv_norm_w, w_kv_up)
    kn_all = jnp.concatenate([kn, kn_c], axis=2)
    kr_all = jnp.concatenate([kr, kr_c], axis=1)
    v_all = jnp.concatenate([v, v_c], axis=2)
    b, h, l, _ = qn.shape
    nb = l // Q_BLOCK

    def blocks(t):
        return t.reshape(b, h, nb, Q_BLOCK, t.shape[-1]).transpose(2, 0, 1, 3, 4)

    o = lax.map(lambda qb: mla_attend(qb[0], qb[1], kn_all, kr_all, v_all), (blocks(qn), blocks(qr)))
    o = o.transpose(1, 2, 0, 3, 4).reshape(b, h, l, MLA_V)
    y_lat = from_heads(o)
    y_ctx = None
    if need_ctx_out:
        qn_c, qr_c = mla_queries(a_ctx, q_norm_w, w_q_up)
        y_ctx = from_heads(mla_attend(qn_c, qr_c, kn_c, kr_c, v_c))
    return y_lat, y_ctx


def gla_chunked(q, k, v, log_a, s0):
    b, h, l, _ = q.shape
    dv = v.shape[-1]
    n = l // GLA_CHUNK

    def ch(t):
        return t.reshape(b, h, n, GLA_CHUNK, t.shape[-1])

    q, k, v, log_a = ch(q), ch(k), ch(v), ch(log_a)
    cum = jnp.cumsum(log_a, axis=3)
    last = cum[:, :, :, -1:, :]
    q_dec = q * jnp.exp(cum)
    k_inv = k * jnp.exp(-cum)
    k_end = k * jnp.exp(last - cum)
    lower = jnp.tril(jnp.ones((GLA_CHUNK, GLA_CHUNK), dtype=bool))
    att = jnp.where(lower, jnp.einsum('bhncd,bhnsd->bhncs', q_dec, k_inv), 0.0)
    o_intra = jnp.einsum('bhncs,bhnse->bhnce', att, v)
    kv_chunk = jnp.einsum('bhncd,bhnce->bhnde', k_end, v)
    decay = jnp.exp(last[:, :, :, 0, :])

    def step(state, inp):
        dec, kv = inp
        return dec[..., None] * state + kv, state

    s_final, s_prev = lax.scan(step, s0, (jnp.moveaxis(decay, 2, 0), jnp.moveaxis(kv_chunk, 2, 0)))
    s_prev = jnp.moveaxis(s_prev, 0, 2)
    o_inter = jnp.einsum('bhncd,bhnde->bhnce', q_dec, s_prev)
    return (o_intra + o_inter).reshape(b, h, l, dv), s_final


def gla_bidir(q, k, v, la_f, la_b, s0_f, s0_b):
    o_f, s_f = gla_chunked(q, k, v, la_f, s0_f)
    flip = lambda t: jnp.flip(t, axis=2)
    o_b, s_b = gla_chunked(flip(q), flip(k), flip(v), flip(la_b), s0_b)
    return o_f + flip(o_b), s_f, s_b


def gla_log_decay(lr, w_up, b_up):
    z = (lr @ w_up + b_up).astype(jnp.float32)
    return to_heads(jax.nn.log_sigmoid(z) / GLA_GATE_TAU, GLA_HEADS)


def gla_mixer(b_lat, b_ctx, w_gate_f, b_gate_f, w_gate_b, b_gate_b, norm_w, need_ctx_out):
    def prep(t):
        q, k, v, g, lr_f, lr_b = split_cols(t, B_SPLITS)
        q = to_heads(q, GLA_HEADS).astype(jnp.float32) * (GLA_DK ** -0.5)
        k = to_heads(k, GLA_HEADS).astype(jnp.float32)
        v = to_heads(v, GLA_HEADS).astype(jnp.float32)
        return q, k, v, g, gla_log_decay(lr_f, w_gate_f, b_gate_f), gla_log_decay(lr_b, w_gate_b, b_gate_b)

    def finish(o, g):
        o = rms_norm(o.transpose(0, 2, 1, 3), norm_w)
        b, l = o.shape[0], o.shape[1]
        return o.reshape(b, l, GLA_V_W).astype(g.dtype) * jax.nn.silu(g)

    q_c, k_c, v_c, g_c, laf_c, lab_c = prep(b_ctx)
    s0 = jnp.zeros((q_c.shape[0], GLA_HEADS, GLA_DK, GLA_DV), jnp.float32)
    o_c, s_f, s_b = gla_bidir(q_c, k_c, v_c, laf_c, lab_c, s0, s0)
    q, k, v, g, laf, lab = prep(b_lat)
    o_l, _, _ = gla_bidir(q, k, v, laf, lab, s_f, s_b)
    y_ctx = finish(o_c, g_c) if need_ctx_out else None
    return finish(o_l, g), y_ctx


def na_mixer(c_lat, c_ctx, rpb, need_ctx_out):
    q, k, v = [to_heads(t, NA_HEADS) for t in split_cols(c_lat, C_SPLITS)]
    q_c, k_c, v_c = [to_heads(t, NA_HEADS) for t in split_cols(c_ctx, C_SPLITS)]
    b, h, l, d = q.shape
    rows = l // GRID_W
    kh = min(NA_KH, rows)
    grid = lambda t: t.reshape(b, h, rows, GRID_W, d)
    qg, kg, vg = grid(q * NA_SCALE), grid(k), grid(v)
    col = jnp.arange(GRID_W)
    col_start = jnp.clip(col - NA_KW // 2, 0, GRID_W - NA_KW)
    col_mask = (col[None, :] >= col_start[:, None]) & (col[None, :] < col_start[:, None] + NA_KW)
    col_off = jnp.clip(col[None, :] - col[:, None] + NA_KW - 1, 0, 2 * NA_KW - 2)
    band = kh * GRID_W

    def row_attend(r):
        r0 = jnp.clip(r - kh // 2, 0, rows - kh)
        kb = lax.dynamic_slice_in_dim(kg, r0, kh, axis=2)
        vb = lax.dynamic_slice_in_dim(vg, r0, kh, axis=2)
        qr = lax.dynamic_index_in_dim(qg, r, axis=2, keepdims=False)
        row_off = r0 + jnp.arange(kh) - r + NA_KH - 1
        bias = rpb[:, row_off[:, None, None], col_off[None, :, :]].transpose(0, 2, 1, 3)
        s_band = jnp.einsum('bhqd,bhrkd->bhqrk', qr, kb).astype(jnp.float32) + bias.astype(jnp.float32)
        s_band = jnp.where(col_mask[:, None, :], s_band, -jnp.inf)
        s_ctx = jnp.einsum('bhqd,bhkd->bhqk', qr, k_c).astype(jnp.float32)
        s = jnp.concatenate([s_band.reshape(b, h, GRID_W, band), s_ctx], axis=-1)
        p = jax.nn.softmax(s, axis=-1).astype(v.dtype)
        p_band = p[..., :band].reshape(b, h, GRID_W, kh, GRID_W)
        return (jnp.einsum('bhqrk,bhrkd->bhqd', p_band, vb)
                + jnp.einsum('bhqk,bhkd->bhqd', p[..., band:], v_c))

    o = lax.map(row_attend, jnp.arange(rows))
    y_lat = o.transpose(1, 0, 3, 2, 4).reshape(b, l, h * d)
    y_ctx = from_heads(dense_attend(q_c * NA_SCALE, k_c, v_c)) if need_ctx_out else None
    return y_lat, y_ctx


def merge_branches(ya, yb, yc, gates, w_a_o, w_b_o, w_c_o, w_out):
    ga, gb, gc = jnp.split(jax.nn.sigmoid(gates), N_BRANCH, axis=-1)
    return (ga * (ya @ w_a_o) + gb * (yb @ w_b_o) + gc * (yc @ w_c_o)) @ w_out


def swiglu(u, w_ffn_in, w_ffn_out):
    gate, up = jnp.split(u @ w_ffn_in, 2, axis=-1)
    return (jax.nn.silu(gate) * up) @ w_ffn_out


def hybrid_layer(h, hc, mod, modc, norm1_w, w_in, mla_q_norm_w, mla_kv_norm_w, mla_w_q_up, mla_w_kv_up,
                 gla_w_gate_f, gla_b_gate_f, gla_w_gate_b, gla_b_gate_b, gla_norm_w, na_rpb,
                 w_a_o, w_b_o, w_c_o, w_out, norm2_w, w_ffn_in, w_ffn_out, cos, sin, need_ctx_out):
    sh1, sc1, g1, sh2, sc2, g2 = mod
    csh1, csc1, cg1, csh2, csc2, cg2 = modc
    z = modulate(rms_norm(h, norm1_w), sh1, sc1) @ w_in
    zc = modulate(rms_norm(hc, norm1_w), csh1, csc1) @ w_in
    a, bb, cc, gates = split_cols(z, GROUP_SPLITS)
    a_c, bb_c, cc_c, gates_c = split_cols(zc, GROUP_SPLITS)
    ya, ya_c = mla_mixer(a, a_c, mla_q_norm_w, mla_kv_norm_w, mla_w_q_up, mla_w_kv_up, cos, sin, need_ctx_out)
    yb, yb_c = gla_mixer(bb, bb_c, gla_w_gate_f, gla_b_gate_f, gla_w_gate_b, gla_b_gate_b, gla_norm_w, need_ctx_out)
    yc, yc_c = na_mixer(cc, cc_c, na_rpb, need_ctx_out)
    h = h + g1 * merge_branches(ya, yb, yc, gates, w_a_o, w_b_o, w_c_o, w_out)
    h = h + g2 * swiglu(modulate(rms_norm(h, norm2_w), sh2, sc2), w_ffn_in, w_ffn_out)
    if need_ctx_out:
        hc = hc + cg1 * merge_branches(ya_c, yb_c, yc_c, gates_c, w_a_o, w_b_o, w_c_o, w_out)
        hc = hc + cg2 * swiglu(modulate(rms_norm(hc, norm2_w), csh2, csc2), w_ffn_in, w_ffn_out)
    return h, hc


def setup_inputs(seed: int = 0) -> dict:
    key = jax.random.key(seed)
    ks = iter(jax.random.split(key, 32))

    def nrm(shape, scale):
        return jax.random.normal(next(ks), shape, jnp.float32) * scale

    L, D = DEPTH, D_MODEL
    return {
        "x": nrm((BATCH, SEQ, D), 1.0),
        "c": nrm((BATCH, D), 1.0),
        "ctx": nrm((BATCH, CTX_LEN, D), 1.0),
        "c_ctx": nrm((D,), 1.0),
        "w_mod": nrm((L, D, 6 * D), 0.5 * D ** -0.5),
        "b_mod": nrm((L, 6 * D), 0.02),
        "norm1_w": 1.0 + nrm((L, D), 0.02),
        "w_in": nrm((L, D, IN_WIDTH), D ** -0.5),
        "mla_q_norm_w": 1.0 + nrm((L, MLA_Q_RANK), 0.02),
        "mla_kv_norm_w": 1.0 + nrm((L, MLA_KV_RANK), 0.02),
        "mla_w_q_up": nrm((L, MLA_Q_RANK, MLA_HEADS * (MLA_NOPE + MLA_ROPE)), MLA_Q_RANK ** -0.5),
        "mla_w_kv_up": nrm((L, MLA_KV_RANK, MLA_HEADS * (MLA_NOPE + MLA_V)), MLA_KV_RANK ** -0.5),
        "gla_w_gate_f": nrm((L, GLA_GATE_RANK, GLA_QK_W), GLA_GATE_RANK ** -0.5),
        "gla_b_gate_f": nrm((L, GLA_QK_W), 0.1),
        "gla_w_gate_b": nrm((L, GLA_GATE_RANK, GLA_QK_W), GLA_GATE_RANK ** -0.5),
        "gla_b_gate_b": nrm((L, GLA_QK_W), 0.1),
        "gla_norm_w": 1.0 + nrm((L, GLA_DV), 0.02),
        "na_rpb": nrm((L, NA_HEADS, 2 * NA_KH - 1, 2 * NA_KW - 1), 0.1),
        "w_a_o": nrm((L, MLA_OUT_W, D), MLA_OUT_W ** -0.5),
        "w_b_o": nrm((L, GLA_V_W, D), GLA_V_W ** -0.5),
        "w_c_o": nrm((L, NA_W, D), NA_W ** -0.5),
        "w_out": nrm((L, D, D), D ** -0.5),
        "norm2_w": 1.0 + nrm((L, D), 0.02),
        "w_ffn_in": nrm((L, D, 2 * FFN_HIDDEN), D ** -0.5),
        "w_ffn_out": nrm((L, FFN_HIDDEN, D), FFN_HIDDEN ** -0.5),
        "final_norm_w": 1.0 + nrm((D,), 0.02),
    }


def reference(x, c, ctx, c_ctx, w_mod, b_mod, norm1_w, w_in, mla_q_norm_w, mla_kv_norm_w, mla_w_q_up,
              mla_w_kv_up, gla_w_gate_f, gla_b_gate_f, gla_w_gate_b, gla_b_gate_b, gla_norm_w, na_rpb,
              w_a_o, w_b_o, w_c_o, w_out, norm2_w, w_ffn_in, w_ffn_out, final_norm_w):
    cos, sin = axial_rope_tables(x.shape[1])
    c_act = jax.nn.silu(c)[:, None, :]
    cc_act = jax.nn.silu(c_ctx)
    h, hc = x, ctx
    for i in range(DEPTH):
        mod = jnp.split(c_act @ w_mod[i] + b_mod[i], 6, axis=-1)
        modc = jnp.split(cc_act @ w_mod[i] + b_mod[i], 6, axis=-1)
        h, hc = hybrid_layer(h, hc, mod, modc, norm1_w[i], w_in[i], mla_q_norm_w[i], mla_kv_norm_w[i],
                             mla_w_q_up[i], mla_w_kv_up[i], gla_w_gate_f[i], gla_b_gate_f[i],
                             gla_w_gate_b[i], gla_b_gate_b[i], gla_norm_w[i], na_rpb[i],
                             w_a_o[i], w_b_o[i], w_c_o[i], w_out[i], norm2_w[i], w_ffn_in[i],
                             w_ffn_out[i], cos, sin, i < DEPTH - 1)
    return rms_norm(h, final_norm_w)
```

```python
import numpy as np
from contextlib import ExitStack
import concourse.bass as bass
import concourse.mybir as mybir
from concourse.bass_utils import run_bass_kernel_spmd

F32 = mybir.dt.float32
BF16 = mybir.dt.bfloat16
AF = mybir.ActivationFunctionType
ALU = mybir.AluOpType

D = 1024
DEPTH = 4
SEQ = 2048
CTX = 256
T = SEQ + CTX
GRID_W = 64
EPS = 1e-6
IN_W = 6848
A0, B0, C0, G0 = 0, 672, 2240, 3776
FFH = 2816
MLA_SCALE = 96 ** -0.5
NA_SCALE = 0.125
BLOCKS = [(0, 512), (512, 512), (1024, 512), (1536, 512), (2048, 256)]
NEG = -30000.0
SEM_CAP = 30000
SAME_ENG_SYNC = True


class Tok:
    __slots__ = ("name", "w", "r", "dsem", "dcnt")

    def __init__(self, name):
        self.name = name
        self.w = None
        self.r = {}
        self.dsem = None
        self.dcnt = 0


class _CountProxy:
    def __init__(self, h, prog):
        self._h, self._p = h, prog

    def matmul(self, *a, **k):
        c = self._p.mm_counts
        c[self._p._cur_phase] = c.get(self._p._cur_phase, 0) + 1
        return self._h.matmul(*a, **k)

    def __getattr__(self, name):
        return getattr(self._h, name)


INSTRUMENT = False


class Op:
    __slots__ = ("eng", "fn", "deps", "ev", "signal", "signo", "waits", "dma", "phase", "raw")

    def __init__(self, eng, fn):
        self.eng = eng
        self.fn = fn
        self.deps = set()
        self.raw = set()
        self.ev = None
        self.signal = False
        self.signo = 0
        self.waits = []
        self.dma = None


class Prog:
    ENGS = ["pe", "act", "dve", "pool", "sp"]

    def __init__(self, nc, stack):
        self.nc = nc
        self.stack = stack
        self.ops = {e: [] for e in self.ENGS}
        self.anchors = []
        self.active_anchors = set()
        self.phase = "init"
        self.mm_counts = {}

    def tok(self, name):
        return Tok(name)

    def toks(self, name, n):
        return [Tok(f"{name}{i}") for i in range(n)]

    def _deps(self, op, reads, writes, anchor=None):
        for t in reads:
            if t.w is not None:
                op.deps.add(t.w)
                op.raw.add(t.w)
        for t in writes:
            if t.w is not None and not (anchor is not None and t.w[0] == "d" and t.w[1] is anchor):
                op.deps.add(t.w)
            for k, v in t.r.items():
                op.deps.add((k[0], k[1], v))

    def _commit(self, ev, reads, writes):
        for t in reads:
            key = (ev[0], ev[1])
            if t.r.get(key, 0) < ev[2]:
                t.r[key] = ev[2]
        for t in writes:
            t.w = ev
            t.r = {}

    def add(self, eng, fn, reads=(), writes=()):
        op = Op(eng, fn)
        op.phase = self.phase
        self._deps(op, reads, writes)
        self.ops[eng].append(op)
        ev = ("c", eng, len(self.ops[eng]))
        op.ev = ev
        self._commit(ev, reads, writes)
        return op

    def dma(self, q, out, in_, reads, writes, anchor, exempt=False):
        op = Op(q, None)
        op.dma = (out, in_, anchor)
        self._deps(op, reads, writes, anchor)
        if anchor.dsem is None:
            anchor.dsem = self.stack.enter_context(self.nc.semaphore(f"d{len(self.anchors)}"))
            self.anchors.append(anchor)
        anchor.dcnt += 16
        if not exempt:
            self.active_anchors.add(anchor)
        self.ops[q].append(op)
        ev = ("d", anchor, anchor.dcnt)
        op.ev = ev
        self._commit(ev, reads, writes)
        return op

    def barrier(self):
        evs = []
        for e in self.ENGS:
            if self.ops[e]:
                last = None
                for o in reversed(self.ops[e]):
                    if o.ev[0] == "c" and o.fn is not None:
                        last = o
                        break
                if last is not None:
                    evs.append(last.ev)
        devs = [("d", a, a.dcnt) for a in self.active_anchors]
        self.active_anchors = set()
        for e in self.ENGS:
            op = Op(e, None)
            for ev in evs:
                if ev[1] != e:
                    op.deps.add(ev)
            for ev in devs:
                op.deps.add(ev)
            self.ops[e].append(op)
            op.ev = ("c", e, len(self.ops[e]))

    def finalize(self):
        for e in self.ENGS:
            waited = {}
            for op in self.ops[e]:
                best = {}
                for ev in op.deps:
                    if ev[0] == "c":
                        if ev[1] == e and (e == "pe" or not SAME_ENG_SYNC):
                            continue
                        if ev[1] == e and ev not in op.raw:
                            continue
                        if ev[1] == e and ev[2] >= op.ev[2]:
                            continue
                    key = (ev[0], ev[1])
                    if ev[2] > best.get(key, 0):
                        best[key] = ev[2]
                for key, v in best.items():
                    if waited.get(key, 0) >= v:
                        continue
                    waited[key] = v
                    if key[0] == "c":
                        src = self.ops[key[1]][v - 1]
                        idx = v - 1
                        while src.fn is None and src.dma is None:
                            idx -= 1
                            if idx < 0:
                                src = None
                                break
                            src = self.ops[key[1]][idx]
                        if src is None:
                            continue
                        if src.dma is not None:
                            op.waits.append(("d", src.dma[2], src.ev[2]))
                            continue
                        src.signal = True
                        op.waits.append(("c", key[1], src))
                    else:
                        op.waits.append(("d", key[1], v))
        self.csems = {}
        for e in self.ENGS:
            n = 0
            for op in self.ops[e]:
                if op.signal:
                    n += 1
                    op.signo = n
            nsem = (n + SEM_CAP - 1) // SEM_CAP
            self.csems[e] = [self.stack.enter_context(self.nc.semaphore(f"c_{e}{i}")) for i in range(max(nsem, 1))]

    def _semval(self, e, signo):
        return self.csems[e][(signo - 1) // SEM_CAP], (signo - 1) % SEM_CAP + 1

    def emit_engine(self, e, h):
        if INSTRUMENT and e == "pe":
            h = _CountProxy(h, self)
        for op in self.ops[e]:
            self._cur_phase = getattr(op, "phase", None)
            for w in op.waits:
                if w[0] == "c":
                    sem, val = self._semval(w[1], w[2].signo)
                    h.wait_ge(sem, val)
                else:
                    h.wait_ge(w[1].dsem, w[2])
            if op.dma is not None:
                out, in_, anchor = op.dma
                h.dma_start(out=out, in_=in_).then_inc(anchor.dsem, 16)
            elif op.fn is not None:
                ins = op.fn(h)
                if op.signal:
                    sem, _ = self._semval(e, op.signo)
                    ins.then_inc(sem, 1)

    def emit(self):
        self.finalize()
        nc = self.nc
        with nc.Block() as block:
            @block.tensor
            def _(h):
                self.emit_engine("pe", h)

            @block.scalar
            def _(h):
                self.emit_engine("act", h)

            @block.vector
            def _(h):
                self.emit_engine("dve", h)

            @block.gpsimd
            def _(h):
                self.emit_engine("pool", h)

            @block.sync
            def _(h):
                self.emit_engine("sp", h)


def na_tables():
    rows = SEQ // GRID_W
    pats = {}
    plan = {}
    for r in range(rows):
        r0 = min(max(r - 4, 0), rows - 8)
        tiles = list(range(r0 // 2, (r0 + 7) // 2 + 1))
        lst = []
        for tt in tiles:
            j0 = 2 * tt
            key = (j0 - r, r0 <= j0 < r0 + 8, r0 <= j0 + 1 < r0 + 8)
            if key not in pats:
                pats[key] = len(pats)
            lst.append((tt, pats[key]))
        plan[r] = lst
    return pats, plan


def build_na_gather_index(pats):
    NU = len(pats)
    ia = np.zeros((NU, 128, 8, 64), np.int64)
    ib = np.zeros((NU, 128, 8, 64), np.int64)
    neg = np.zeros((NU, 128, 8, 64), np.float32)
    col = np.arange(64)
    cs = np.clip(col - 8, 0, 48)
    for (dr0, v0, v1), u in pats.items():
        for jj in range(2):
            valid = (v0, v1)[jj]
            dr = dr0 + jj
            for kc in range(64):
                p = jj * 64 + kc
                for qc in range(64):
                    ok = valid and (kc >= cs[qc]) and (kc < cs[qc] + 16) and (-7 <= dr <= 7)
                    a = min(max(dr + 7, 0), 14)
                    b = min(max(kc - qc + 15, 0), 30)
                    ia[u, p, :, qc] = a
                    ib[u, p, :, qc] = b
                    if not ok:
                        neg[u, p, :, qc] = NEG
    return ia, ib, neg


def rope_tables():
    t = np.arange(SEQ)
    rows = (t // GRID_W).astype(np.float32)
    cols = (t % GRID_W).astype(np.float32)
    inv = (10000.0 ** (-np.arange(8, dtype=np.float32) / 8)).astype(np.float32)
    ang = np.concatenate([rows[:, None] * inv, cols[:, None] * inv], -1)
    cos, sin = np.cos(ang).astype(np.float32), np.sin(ang).astype(np.float32)
    C = np.ones((96, T), np.float32)
    S = np.zeros((96, T), np.float32)
    C[64:80, :SEQ] = cos.T
    C[80:96, :SEQ] = cos.T
    S[64:80, :SEQ] = sin.T
    S[80:96, :SEQ] = sin.T
    return C, S


def gla_masks():
    s = np.arange(128)[:, None]
    t = np.arange(128)[None, :]
    m = np.zeros((4, 128, 128), np.float32)
    m[0] = (s <= t)
    m[1] = (s >= t)
    m[2] = (s > t)
    m[3] = (s < t)
    return m


_PATS, _PLAN = na_tables()
_NU = len(_PATS)


PHASES = ("norm1", "mla", "na", "gla", "merge", "norm2", "ffn")


def build_program(n_layers=DEPTH, n_elems=2, taps=(), stop_after=None):
    nc = bass.Bass("TRN2", target_bir_lowering=False)
    stack = ExitStack()
    P = Prog(nc, stack)
    L = DEPTH

    def din(name, shape, dt=F32):
        return nc.dram_tensor(name, list(shape), dt, kind="ExternalInput").ap()

    h0 = din("h0", [2, 128, 8, T])
    cT = din("cT", [128, 8, 3])
    w_mod = din("w_mod", [L, D, 6 * D])
    b_modT = din("b_modT", [128, L, 48])
    n1T = din("n1T", [128, L, 8])
    n2T = din("n2T", [128, L, 8])
    fnT = din("fnT", [128, 8])
    w_in = din("w_in", [L, D, IN_W])
    qnT = din("qnT", [128, L, 3])
    kvnT = din("kvnT", [128, L, 2])
    w_q_up = din("w_q_up", [L, 384, 768])
    w_kv_up = din("w_kv_up", [L, 256, 1024])
    gla_gate = din("gla_gate", [L, 2, 17, 256])
    gnT = din("gnT", [128, L])
    na_tab = din("na_tab", [L, 128, _NU, 512])
    na_neg = din("na_neg", [128, _NU, 512])
    w_a_o = din("w_a_o", [L, 512, D])
    w_b_o = din("w_b_o", [L, 512, D])
    w_c_o = din("w_c_o", [L, 512, D])
    w_out = din("w_out", [L, D, D])
    w_ffn_in = din("w_ffn_in", [L, D, 2 * FFH])
    w_ffn_out = din("w_ffn_out", [L, FFH, D])
    ropeC = din("ropeC", [96, T])
    ropeS = din("ropeS", [96, T])
    gmask = din("gmask", [4, 128, 128])
    ident_in = din("ident", [128, 128])
    outT = nc.dram_tensor("outT", [2, 128, 8, SEQ], F32, kind="ExternalOutput").ap()
    h_scr = nc.dram_tensor("h_scr", [2, 128, 8, T], F32, kind="Internal").ap()
    tap_out = {}
    for name, shape, dt_ in taps:
        tap_out[name] = nc.dram_tensor("tap_" + name, list(shape), dt_, kind="ExternalOutput").ap()

    def sb(name, shape, dt=F32):
        return stack.enter_context(nc.sbuf_tensor(name, list(shape), dt))

    ps = [stack.enter_context(nc.psum_tensor(f"ps{i}", [128, 512], F32)) for i in range(8)]
    pst = P.toks("ps", 8)

    ones_bf = sb("ones_bf", [128, 128], BF16)
    ones_f = sb("ones_f", [128, 64], F32)
    eps_col = sb("eps_col", [128, 1])
    one_col = sb("one_col", [128, 1])
    modT = sb("modT", [128, L, 48, 3])
    t_modT = P.tok("modT")
    cact = sb("cact", [128, 8, 3], BF16)
    cin = sb("cin", [128, 8, 3])
    bmod = sb("bmod", [128, L, 48])
    n1s = sb("n1s", [128, L, 8])
    n2s = sb("n2s", [128, L, 8])
    fns = sb("fns", [128, 8])
    qns = sb("qns", [128, L, 3])
    kvns = sb("kvns", [128, L, 2])
    gns = sb("gns", [128, L])
    t_const = P.tok("const")
    Wc = sb("Wc", [128, 8, 2])
    Sc = sb("Sc", [128, 8, 2])
    Gc = sb("Gc", [128, 8, 2])
    t_coef = P.tok("coef")
    xnT = sb("xnT", [128, 8, T], BF16)
    t_xn = P.toks("xn", len(BLOCKS))
    Y = sb("Y", [128, 3 * 4 * T], BF16)
    yaT = Y[:, 0:4 * T].rearrange("p (j t) -> p j t", j=4)
    ybT = Y[:, 4 * T:8 * T].rearrange("p (j t) -> p j t", j=4)
    ycT = Y[:, 8 * T:12 * T].rearrange("p (j t) -> p j t", j=4)
    t_ya = P.toks("ya", len(BLOCKS))
    t_yb = P.tok("yb")
    t_yc = P.tok("yc")
    Fb = [sb(f"F{i}", [128, 512]) for i in range(7)]
    tF = P.toks("F", 7)
    Rb, t_Rb = Fb[6], tF[6]
    PWN = 13824
    PW = sb("PW", [128, PWN], BF16)
    t_PW = P.tok("PW")
    ring = [PW[:, i * 4608:(i + 1) * 4608] for i in range(3)]
    t_ring = P.toks("ring", 3)
    BIGN = 34816
    BIG = sb("BIG", [128, BIGN], BF16)
    t_big = P.tok("BIG")

    cnt = {"ring": 0, "ps": 0, "f": 0}
    _ptoks = {}

    def ptok(name):
        if name not in _ptoks:
            _ptoks[name] = P.tok(name)
        return _ptoks[name]

    def next_ring():
        i = cnt["ring"] % 3
        cnt["ring"] += 1
        return ring[i], t_ring[i]

    def ring_load(dst, src, tb):
        load_w(dst, src, tb, extra_writes=[t_PW, ptok("wA"), ptok("wq"), ptok("wkv"), ptok("wAr"), ptok("wqr")], exempt=True)

    def next_ps(lo=0, hi=8):
        n = hi - lo
        i = lo + cnt["ps"] % n
        cnt["ps"] += 1
        return ps[i], pst[i]

    def next_f(lo, hi):
        n = hi - lo
        i = lo + cnt["f"] % n
        cnt["f"] += 1
        return Fb[i], tF[i]

    def act(fn, reads, writes):
        return P.add("act", fn, reads, writes)

    def dve(fn, reads, writes):
        return P.add("dve", fn, reads, writes)

    def pe(fn, reads, writes):
        return P.add("pe", fn, reads, writes)

    class Carver:
        def __init__(self, base, limit):
            self.base, self.o, self.limit = base, 0, limit

        def take(self, nel, dt=BF16):
            n = nel if dt == BF16 else 2 * nel
            ap = self.base[:, self.o:self.o + n]
            self.o += n
            assert self.o <= self.limit, (self.o, self.limit)
            return ap if dt == BF16 else ap.bitcast(F32)

    def rstd_from_ss(ps_ss, t_ss, n, dim, Rout, t_R):
        act(lambda h: h.activation(out=Rout[:, 0:n], in_=ps_ss[:, 0:n], func=AF.Ln, scale=1.0 / dim, bias=eps_col[:, 0:1]),
            [t_ss, t_const], [t_R])
        act(lambda h: h.activation(out=Rout[:, 0:n], in_=Rout[:, 0:n], func=AF.Exp, scale=-0.5), [t_R], [t_R])

    def load_w(dst_ap, src_ap, tok, reads=(), extra_writes=(), exempt=False):
        P.dma("pool", dst_ap, src_ap, list(reads), [tok] + list(extra_writes), tok, exempt=exempt)

    def pw_users():
        return [t_PW, ptok("wA"), ptok("wq"), ptok("wkv"), ptok("wAr"), ptok("wqr")] + t_ring

    def w_in_cols(l, c0, c1):
        return w_in[l].rearrange("(k p) c -> p k c", p=128)[:, :, c0:c1]

    def prologue():
        dve(lambda h: h.memset(ones_bf[:], 1.0), [], [t_const])
        dve(lambda h: h.memset(ones_f[:], 1.0), [], [t_const])
        dve(lambda h: h.memset(eps_col[:], EPS), [], [t_const])
        dve(lambda h: h.memset(one_col[:], 1.0), [], [t_const])
        tl = P.tok("ld_small")
        for dst, src in ((cin, cT), (bmod, b_modT), (n1s, n1T), (n2s, n2T), (fns, fnT), (qns, qnT), (kvns, kvnT), (gns, gnT)):
            P.dma("sp", dst[:], src, [], [tl, t_const], P.tok("a"))
        act(lambda h: h.activation(out=cin[:], in_=cin[:], func=AF.Silu), [tl, t_const], [tl])
        dve(lambda h: h.tensor_copy(out=cact[:], in_=cin[:]), [tl], [t_const])
        for l in range(n_layers):
            for cch in range(12):
                buf, tb = next_ring()
                bv = buf[:, 0:4096].rearrange("p (k c) -> p k c", k=8)
                load_w(bv, w_mod[l].rearrange("(k p) c -> p k c", p=128)[:, :, cch * 512:(cch + 1) * 512], tb)
                pb, tp = next_ps()

                def mm(h, bv=bv, pb=pb):
                    ins = None
                    for j in range(4):
                        for k in range(8):
                            ins = h.matmul(pb[:, j * 4:j * 4 + 3], bv[:, k, j * 128:(j + 1) * 128], cact[:, k, :], start=(k == 0), stop=(k == 7))
                    return ins
                pe(mm, [tb, t_const], [tp])

                def ev(h, pb=pb, l=l, cch=cch):
                    ins = None
                    for j in range(4):
                        ins = h.tensor_scalar(out=modT[:, l, cch * 4 + j, :], in0=pb[:, j * 4:j * 4 + 3], scalar1=bmod[:, l, cch * 4 + j:cch * 4 + j + 1],
                                              scalar2=None, op0=ALU.add)
                    return ins
                dve(ev, [tp, t_const], [t_modT])

    def coef_cols(l, e, which):
        base = 3 * which
        nw = n1s if which == 0 else n2s
        for j, colm in ((0, e), (1, 2)):
            dve(lambda h, j=j, colm=colm: h.scalar_tensor_tensor(out=Wc[:, :, j], in0=modT[:, l, (base + 1) * 8:(base + 2) * 8, colm], scalar=1.0,
                                                               in1=nw[:, l, :], op0=ALU.add, op1=ALU.mult), [t_modT, t_const], [t_coef])
            dve(lambda h, j=j, colm=colm: h.tensor_copy(out=Sc[:, :, j], in_=modT[:, l, base * 8:(base + 1) * 8, colm]), [t_modT], [t_coef])
            dve(lambda h, j=j, colm=colm: h.tensor_copy(out=Gc[:, :, j], in_=modT[:, l, (base + 2) * 8:(base + 3) * 8, colm]), [t_modT], [t_coef])

    def norm_phase(e, final=False):
        cv = Carver(BIG, BIGN)
        hb = [cv.take(4096, F32).rearrange("p (k t) -> p k t", k=8) for _ in range(2)]
        sqbs = [cv.take(4096).rearrange("p (k t) -> p k t", k=8) for _ in range(2)]
        Rbs = [cv.take(512, F32) for _ in range(2)]
        ob = cv.take(4096, F32).rearrange("p (k t) -> p k t", k=8) if final else None
        t_hb, t_sqbs, t_ob, t_Rbs = [ptok("hb0"), ptok("hb1")], P.toks("sqb", 2), ptok("ob"), P.toks("nRb", 2)
        for bi, (t0, n) in enumerate(BLOCKS):
            if final and t0 >= SEQ:
                continue
            j = 1 if t0 >= SEQ else 0
            hbuf, thb = hb[bi % 2], t_hb[bi % 2]
            sqb, t_sqb = sqbs[bi % 2], t_sqbs[bi % 2]
            Rb, t_Rb = Rbs[bi % 2], t_Rbs[bi % 2]
            P.dma("sp", hbuf[:, :, 0:n], hcur[e][:, :, t0:t0 + n], [t_h[e][bi]], [thb], thb)
            act(lambda h, hbuf=hbuf, n=n, sqb=sqb: h.activation(out=sqb[:, :, 0:n], in_=hbuf[:, :, 0:n], func=AF.Square), [thb], [t_sqb])
            pb, tp = next_ps()

            def mm(h, pb=pb, n=n, sqb=sqb):
                ins = None
                for k in range(8):
                    ins = h.matmul(pb[:, 0:n], ones_bf[:], sqb[:, k, 0:n], start=(k == 0), stop=(k == 7))
                return ins
            pe(mm, [t_sqb, t_const], [tp])
            rstd_from_ss(pb, tp, n, D, Rb, t_Rb)
            if not final:
                for k in range(8):
                    tf, ttf = next_f(0, 3)
                    dve(lambda h, k=k, tf=tf, hbuf=hbuf, n=n, j=j, Rb=Rb: h.scalar_tensor_tensor(
                        out=tf[:, 0:n], in0=hbuf[:, k, 0:n], scalar=Wc[:, k, j:j + 1], in1=Rb[:, 0:n], op0=ALU.mult, op1=ALU.mult),
                        [thb, t_Rb, t_coef], [ttf])
                    act(lambda h, k=k, tf=tf, n=n, t0=t0, j=j: h.activation(
                        out=xnT[:, k, t0:t0 + n], in_=tf[:, 0:n], func=AF.Identity, bias=Sc[:, k, j:j + 1], scale=1.0),
                        [ttf, t_coef], [t_xn[bi]])
            else:
                for k in range(8):
                    dve(lambda h, k=k, hbuf=hbuf, n=n, Rb=Rb: h.scalar_tensor_tensor(
                        out=ob[:, k, 0:n], in0=hbuf[:, k, 0:n], scalar=fns[:, k:k + 1], in1=Rb[:, 0:n], op0=ALU.mult, op1=ALU.mult),
                        [thb, t_Rb, t_const], [t_ob])
                P.dma("sp", outT[e][:, :, t0:t0 + n], ob[:, :, 0:n], [t_ob], [t_out], t_ob)

    mla_w = {}

    def mla_weights(l):
        wA = PW[:, 0:5376].rearrange("p (k c) -> p k c", k=8)
        wAr = PW[:, 5376:6144].rearrange("p (k c) -> p k c", k=8)
        wq = PW[:, 6144:8448].rearrange("p (k c) -> p k c", k=3)
        wqr = PW[:, 8448:10752].rearrange("p (k c) -> p k c", k=3)
        wkv = PW[:, 10752:12800].rearrange("p (k c) -> p k c", k=2)
        t_wA, t_wq, t_wkv, t_wAr, t_wqr = ptok("wA"), ptok("wq"), ptok("wkv"), ptok("wAr"), ptok("wqr")
        load_w(wA, w_in_cols(l, A0, A0 + 672), t_wA, extra_writes=pw_users(), exempt=True)
        load_w(wq, w_q_up[l].rearrange("(k p) c -> p k c", p=128), t_wq, exempt=True)
        load_w(wkv, w_kv_up[l].rearrange("(k p) c -> p k c", p=128), t_wkv, exempt=True)
        mla_w.update(wA=wA, wAr=wAr, wq=wq, wqr=wqr, wkv=wkv)

    def mla_weights_prep(l):
        wA, wAr, wq, wqr, wkv = mla_w["wA"], mla_w["wAr"], mla_w["wq"], mla_w["wqr"], mla_w["wkv"]
        t_wA, t_wq, t_wkv, t_wAr, t_wqr = ptok("wA"), ptok("wq"), ptok("wkv"), ptok("wAr"), ptok("wqr")

        def scale_q(h):
            ins = None
            for k in range(3):
                ins = h.tensor_scalar(out=wq[:, k, :], in0=wq[:, k, :], scalar1=qns[:, l, k:k + 1], scalar2=None, op0=ALU.mult)
            return ins
        dve(scale_q, [t_wq, t_const], [t_wq])

        def scale_kv(h):
            ins = None
            for k in range(2):
                ins = h.tensor_scalar(out=wkv[:, k, :], in0=wkv[:, k, :], scalar1=kvns[:, l, k:k + 1], scalar2=None, op0=ALU.mult)
            return ins
        dve(scale_kv, [t_wkv, t_const], [t_wkv])
        dve(lambda h: h.memset(wAr[:], 0.0), [t_wA], [t_wAr])

        def rotA(h):
            h.tensor_scalar(out=wAr[:, :, 64:80], in0=wA[:, :, 656:672], scalar1=-1.0, scalar2=None, op0=ALU.mult)
            return h.tensor_copy(out=wAr[:, :, 80:96], in_=wA[:, :, 640:656])
        dve(rotA, [t_wA, t_wAr], [t_wAr])
        dve(lambda h: h.memset(wqr[:], 0.0), [t_wA], [t_wqr])

        def rotQ(h):
            wq4 = wq.rearrange("p k (h c) -> p k h c", h=8)
            wqr4 = wqr.rearrange("p k (h c) -> p k h c", h=8)
            ins = None
            for k in range(3):
                h.tensor_scalar(out=wqr4[:, k, :, 64:80], in0=wq4[:, k, :, 80:96], scalar1=-1.0, scalar2=None, op0=ALU.mult)
                ins = h.tensor_copy(out=wqr4[:, k, :, 80:96], in_=wq4[:, k, :, 64:80])
            return ins
        dve(rotQ, [t_wq, t_wqr], [t_wqr])

    def na_weights(l):
        wC = PW[:, 0:12288].rearrange("p (k c) -> p k c", k=8)
        load_w(wC, w_in_cols(l, C0, C0 + 1536), t_PW, extra_writes=pw_users(), exempt=True)

    def gla_weights(l):
        wB = PW[:, 0:12544].rearrange("p (k c) -> p k c", k=8)
        load_w(wB, w_in_cols(l, B0, B0 + 1568), t_PW, extra_writes=pw_users(), exempt=True)

    def mla_phase(l, e):
        KT = Y[:, 4 * T:12 * T].rearrange("p (h t) -> p h t", h=8)
        cv = Carver(BIG, BIGN)
        VA = cv.take(18 * 8 * 128).rearrange("p (t h c) -> p t h c", t=18, h=8)
        QTb = [cv.take(4096).rearrange("p (h t) -> p h t", h=8) for _ in range(2)]
        qdn = cv.take(1536).rearrange("p (c t) -> p c t", c=3)
        kvdn = cv.take(1024).rearrange("p (c t) -> p c t", c=2)
        sqb = cv.take(1536).rearrange("p (c t) -> p c t", c=3)
        NPT = 5
        PT = [cv.take(512) for _ in range(NPT)]
        t1b, t2b, bcs, rsb, rC, rS = Fb[0], Fb[1], Fb[2], Fb[3], Fb[4], Fb[5]
        t_t1, t_t2, t_bcs, t_rs, t_rC, t_rS = tF[0], tF[1], tF[2], tF[3], tF[4], tF[5]
        t_KT = P.toks("KT", len(BLOCKS))
        t_VA = P.toks("VA", len(BLOCKS))
        t_QT = P.toks("QT", 2)
        t_qdn, t_kvdn, t_sqb = P.tok("qdn"), P.tok("kvdn"), P.tok("msq")
        t_PT = P.toks("PT", NPT)
        t_vones = P.tok("vones")
        wA, wAr, wq, wqr, wkv = mla_w["wA"], mla_w["wAr"], mla_w["wq"], mla_w["wqr"], mla_w["wkv"]
        t_wA, t_wq, t_wkv, t_wAr, t_wqr = ptok("wA"), ptok("wq"), ptok("wkv"), ptok("wAr"), ptok("wqr")
        dve(lambda h: h.memset(VA[:, :, :, 64:128], 1.0), [], [t_vones])

        def down_norm(c0, nchunk, dim, dst, t_dst, bi, t0, n):
            banks = []
            for c in range(nchunk):
                pb, tp = next_ps(0, 4)
                banks.append((pb, tp))

                def mm(h, pb=pb, c=c):
                    ins = None
                    for k in range(8):
                        ins = h.matmul(pb[:, 0:n], wA[:, k, c0 + c * 128:c0 + (c + 1) * 128], xnT[:, k, t0:t0 + n], start=(k == 0), stop=(k == 7))
                    return ins
                pe(mm, [t_wA, t_xn[bi]], [tp])
                act(lambda h, pb=pb, c=c: h.activation(out=sqb[:, c, 0:n], in_=pb[:, 0:n], func=AF.Square), [tp], [t_sqb])
            pss, tss = next_ps(4, 6)

            def mm2(h):
                ins = None
                for c in range(nchunk):
                    ins = h.matmul(pss[:, 0:n], ones_bf[:], sqb[:, c, 0:n], start=(c == 0), stop=(c == nchunk - 1))
                return ins
            pe(mm2, [t_sqb, t_const], [tss])
            rstd_from_ss(pss, tss, n, dim, Rb, t_Rb)
            for c, (pb, tp) in enumerate(banks):
                dve(lambda h, pb=pb, c=c: h.tensor_tensor(out=dst[:, c, 0:n], in0=pb[:, 0:n], in1=Rb[:, 0:n], op=ALU.mult), [tp, t_Rb], [t_dst])

        def load_rope(t0, n):
            P.dma("sp", rC[0:96, 0:n], ropeC[:, t0:t0 + n], [], [t_rC], t_rC)
            P.dma("sp", rS[0:96, 0:n], ropeS[:, t0:t0 + n], [], [t_rS], t_rS)

        def stageK(bi, t0, n):
            down_norm(384, 2, 256, kvdn, t_kvdn, bi, t0, n)
            load_rope(t0, n)
            for hd in range(8):
                pb, tp = next_ps(0, 4)

                def mm(h, pb=pb, hd=hd):
                    ins = None
                    for c in range(2):
                        ins = h.matmul(pb[0:64, 0:n], wkv[:, c, hd * 128:hd * 128 + 64], kvdn[:, c, 0:n], start=(c == 0), stop=(c == 1))
                    return ins
                pe(mm, [t_wkv, t_kvdn], [tp])
                dve(lambda h, pb=pb, hd=hd: h.tensor_copy(out=KT[0:64, hd, t0:t0 + n], in_=pb[0:64, 0:n]), [tp], [t_KT[bi]])
            pa, tpa = next_ps(0, 4)
            pb2, tpb2 = next_ps(0, 4)

            def mmr(h, pa=pa, pb2=pb2):
                ins = None
                for k in range(8):
                    ins = h.matmul(pa[0:96, 0:n], wA[:, k, 576:672], xnT[:, k, t0:t0 + n], start=(k == 0), stop=(k == 7))
                for k in range(8):
                    ins = h.matmul(pb2[0:96, 0:n], wAr[:, k, 0:96], xnT[:, k, t0:t0 + n], start=(k == 0), stop=(k == 7))
                return ins
            pe(mmr, [t_wA, t_wAr, t_xn[bi]], [tpa, tpb2])
            dve(lambda h, pa=pa: h.tensor_tensor(out=t1b[64:96, 0:n], in0=pa[64:96, 0:n], in1=rC[64:96, 0:n], op=ALU.mult), [tpa, t_rC], [t_t1])
            dve(lambda h, pb2=pb2: h.tensor_tensor(out=t2b[64:96, 0:n], in0=pb2[64:96, 0:n], in1=rS[64:96, 0:n], op=ALU.mult), [tpb2, t_rS], [t_t2])
            dve(lambda h: h.tensor_tensor(out=t1b[64:96, 0:n], in0=t1b[64:96, 0:n], in1=t2b[64:96, 0:n], op=ALU.add), [t_t1, t_t2], [t_t1])
            for hd in range(8):
                if hd % 2 == 0:
                    act(lambda h, hd=hd: h.activation(out=KT[64:96, hd, t0:t0 + n], in_=t1b[64:96, 0:n], func=AF.Copy), [t_t1], [t_KT[bi]])
                else:
                    dve(lambda h, hd=hd: h.tensor_copy(out=KT[64:96, hd, t0:t0 + n], in_=t1b[64:96, 0:n]), [t_t1], [t_KT[bi]])
            for ti in range(n // 128):
                tt = (t0 + ti * 128) // 128
                pb, tp = next_ps(0, 4)

                def mmv(h, pb=pb, ti=ti):
                    ins = None
                    for hd in range(8):
                        for c in range(2):
                            ins = h.matmul(pb[:, hd * 64:(hd + 1) * 64], kvdn[:, c, ti * 128:(ti + 1) * 128], wkv[:, c, hd * 128 + 64:(hd + 1) * 128], start=(c == 0), stop=(c == 1))
                    return ins
                pe(mmv, [t_wkv, t_kvdn], [tp])
                act(lambda h, pb=pb, tt=tt: h.activation(out=VA[:, tt, :, 0:64], in_=pb[:, 0:512].rearrange("p (h c) -> p h c", h=8), func=AF.Copy),
                    [tp, t_vones], [t_VA[bi]])

        for bi, (t0, n) in enumerate(BLOCKS):
            stageK(bi, t0, n)

        def stageQproj(bi, t0, n):
            down_norm(0, 3, 384, qdn, t_qdn, bi, t0, n)
            load_rope(t0, n)
            QT, tQ = QTb[bi % 2], t_QT[bi % 2]
            for hd in range(8):
                pa, tpa = next_ps(0, 4)
                pb2, tpb2 = next_ps(0, 4)

                def mmq(h, pa=pa, pb2=pb2, hd=hd):
                    ins = None
                    for c in range(3):
                        ins = h.matmul(pa[0:96, 0:n], wq[:, c, hd * 96:(hd + 1) * 96], qdn[:, c, 0:n], start=(c == 0), stop=(c == 2))
                    for c in range(3):
                        ins = h.matmul(pb2[0:96, 0:n], wqr[:, c, hd * 96:(hd + 1) * 96], qdn[:, c, 0:n], start=(c == 0), stop=(c == 2))
                    return ins
                pe(mmq, [t_wq, t_wqr, t_qdn], [tpa, tpb2])
                dve(lambda h, pa=pa, hd=hd, QT=QT: h.tensor_copy(out=QT[0:64, hd, 0:n], in_=pa[0:64, 0:n]), [tpa], [tQ])
                dve(lambda h, pa=pa: h.tensor_tensor(out=t1b[64:96, 0:n], in0=pa[64:96, 0:n], in1=rC[64:96, 0:n], op=ALU.mult), [tpa, t_rC], [t_t1])
                dve(lambda h, pb2=pb2: h.tensor_tensor(out=t2b[64:96, 0:n], in0=pb2[64:96, 0:n], in1=rS[64:96, 0:n], op=ALU.mult), [tpb2, t_rS], [t_t2])
                dve(lambda h, hd=hd, QT=QT: h.tensor_tensor(out=QT[64:96, hd, 0:n], in0=t1b[64:96, 0:n], in1=t2b[64:96, 0:n], op=ALU.add), [t_t1, t_t2], [tQ])

        def stageAttn(bi, t0, n):
            ctxq = t0 >= SEQ
            ktiles = [16, 17] if ctxq else list(range(18))
            kdeps = ([t_KT[4], t_VA[4]] if ctxq else t_KT + t_VA)
            QT, tQ = QTb[bi % 2], t_QT[bi % 2]
            items = [(hd, ki, kt) for hd in range(8) for ki, kt in enumerate(ktiles)]
            nk = len(ktiles)
            pos, pts = {}, {}
            LOOK = 4

            def issue_S(idx):
                hd, ki, kt = items[idx]
                psS, tps = next_ps(0, 6)
                pe(lambda h: h.matmul(psS[:, 0:n], KT[0:96, hd, kt * 128:(kt + 1) * 128], QT[0:96, hd, 0:n], start=True, stop=True), kdeps + [tQ], [tps])
                pt, tpt = PT[idx % NPT], t_PT[idx % NPT]
                act(lambda h: h.activation(out=pt[:, 0:n], in_=psS[:, 0:n], func=AF.Exp, scale=MLA_SCALE), [tps], [tpt])
                pts[idx] = (pt, tpt)

            def issue_PV(idx):
                hd, ki, kt = items[idx]
                if ki == 0:
                    pos[hd] = next_ps(6, 8)
                po, tpo = pos[hd]
                pt, tpt = pts.pop(idx)
                pe(lambda h: h.matmul(po[:, 0:n], VA[:, kt, hd, :], pt[:, 0:n], start=(ki == 0), stop=(ki == nk - 1)), kdeps + [tpt, t_vones], [tpo])
                if ki == nk - 1:
                    dve(lambda h: h.reciprocal(out=bcs[0:64, 0:n], in_=po[64:128, 0:n]), [tpo], [t_bcs])
                    p0 = (hd % 2) * 64
                    dve(lambda h: h.tensor_tensor(out=yaT[p0:p0 + 64, hd // 2, t0:t0 + n], in0=po[0:64, 0:n], in1=bcs[0:64, 0:n], op=ALU.mult),
                        [tpo, t_bcs], [t_ya[bi]])
            for idx in range(len(items) + LOOK):
                if idx < len(items):
                    issue_S(idx)
                if idx >= LOOK:
                    issue_PV(idx - LOOK)

        stageQproj(0, *BLOCKS[0])
        for bi, (t0, n) in enumerate(BLOCKS):
            if bi + 1 < len(BLOCKS):
                stageQproj(bi + 1, *BLOCKS[bi + 1])
            stageAttn(bi, t0, n)

    def na_phase(l, e):
        cv = Carver(BIG, BIGN)
        kT = ybT
        VN = cv.take(18 * 8 * 128).rearrange("p (t h c) -> p t h c", t=18, h=8)
        tab = cv.take(_NU * 512).rearrange("p (u c) -> p u c", u=_NU)
        qTb = [cv.take(4096).rearrange("p (j a t) -> p j a t", j=4, a=2) for _ in range(2)]
        NPT = 4
        PT = [Fb[6][:, :].bitcast(BF16)[:, 0:512], Fb[3][:, :].bitcast(BF16)[:, 0:512], Fb[1][:, :].bitcast(BF16)[:, 0:512], Fb[0][:, :].bitcast(BF16)[:, 0:512]]
        identb = PW[:, 13000:13128]
        bcs, t_bcs, rsb, t_rs = Fb[2], tF[2], Fb[3], tF[3]
        t_k, t_v = P.toks("nk", len(BLOCKS)), P.toks("nv", len(BLOCKS))
        t_q = P.toks("nq", 2)
        t_tab = ptok("tab")
        t_PT = P.toks("nPT", NPT)
        t_ident = ptok("n_ident")
        t_vones = P.tok("nvones")
        wC = PW[:, 0:12288].rearrange("p (k c) -> p k c", k=8)
        load_w(tab, na_tab[l], t_tab)
        for u in range(_NU):
            nb, tnb = next_f(4, 6)
            P.dma("sp", nb[:, 0:512], na_neg[:, u, :], [], [tnb], tnb)
            dve(lambda h, u=u, nb=nb: h.scalar_tensor_tensor(out=tab[:, u, :], in0=tab[:, u, :], scalar=1.0 / NA_SCALE, in1=nb[:, 0:512], op0=ALU.mult, op1=ALU.add), [t_tab, tnb], [t_tab])
        dve(lambda h: h.memset(VN[:, :, :, 64:128], 1.0), [], [t_vones])
        load_w(identb, ident_in, t_ident)
        for i in range(2):
            dve(lambda h, i=i: h.memset(qTb[i][:], 0.0), [], [t_q[i]])

        def proj_fm(which, dst_fn, tdst, bi, t0, n):
            for j in range(4):
                pb, tp = next_ps(0, 4)

                def mm(h, pb=pb, j=j):
                    ins = None
                    for k in range(8):
                        ins = h.matmul(pb[:, 0:n], wC[:, k, which * 512 + j * 128:which * 512 + (j + 1) * 128], xnT[:, k, t0:t0 + n], start=(k == 0), stop=(k == 7))
                    return ins
                pe(mm, [t_PW, t_xn[bi]], [tp])
                if j % 2 == 0:
                    act(lambda h, pb=pb, j=j: h.activation(out=dst_fn(j), in_=pb[:, 0:n], func=AF.Copy), [tp], [tdst])
                else:
                    dve(lambda h, pb=pb, j=j: h.tensor_copy(out=dst_fn(j), in_=pb[:, 0:n]), [tp], [tdst])

        def stageK(bi, t0, n):
            proj_fm(1, lambda j, t0=t0, n=n: kT[:, j, t0:t0 + n], t_k[bi], bi, t0, n)
            for ti in range(n // 128):
                tt = (t0 + ti * 128) // 128
                pb, tp = next_ps(0, 4)

                def mmv(h, pb=pb, ti=ti):
                    ins = None
                    for k in range(8):
                        ins = h.matmul(pb[:, 0:512], xnT[:, k, t0 + ti * 128:t0 + (ti + 1) * 128], wC[:, k, 1024:1536], start=(k == 0), stop=(k == 7))
                    return ins
                pe(mmv, [t_PW, t_xn[bi]], [tp])
                act(lambda h, pb=pb, tt=tt: h.activation(out=VN[:, tt, :, 0:64], in_=pb[:, 0:512].rearrange("p (h c) -> p h c", h=8), func=AF.Copy),
                    [tp, t_vones], [t_v[bi]])
        for bi, (t0, n) in enumerate(BLOCKS):
            stageK(bi, t0, n)
        kvdeps = t_k + t_v

        def stageQ(bi, t0, n):
            qT, tq = qTb[bi % 2], t_q[bi % 2]
            for j in range(4):
                pb, tp = next_ps(0, 4)

                def mmq(h, pb=pb, j=j):
                    ins = None
                    for k in range(8):
                        ins = h.matmul(pb[:, 0:n], wC[:, k, j * 128:(j + 1) * 128], xnT[:, k, t0:t0 + n], start=(k == 0), stop=(k == 7))
                    return ins
                pe(mmq, [t_PW, t_xn[bi]], [tp])
                act(lambda h, pb=pb, j=j: h.activation(out=qT[0:64, j, 0, 0:n], in_=pb[0:64, 0:n], func=AF.Copy), [tp], [tq])
                dve(lambda h, pb=pb, j=j: h.tensor_copy(out=qT[64:128, j, 1, 0:n], in_=pb[64:128, 0:n]), [tp], [tq])
            items = []
            for rr in range(n // 64):
                r = t0 // 64 + rr
                if t0 < SEQ:
                    tiles = list(_PLAN[r]) + [(16, None), (17, None)]
                else:
                    tiles = [(16, None), (17, None)]
                for ki, (tt, u) in enumerate(tiles):
                    items.append((rr, ki, len(tiles), tt, u))
            pos, pts = {}, {}
            LOOK = 3

            def issue_S(idx):
                rr, ki, nt, tt, u = items[idx]
                psS, tps = next_ps(0, 6)

                def mms(h):
                    ins = None
                    for hd in range(8):
                        ins = h.matmul(psS[:, hd * 64:(hd + 1) * 64], kT[:, hd // 2, tt * 128:(tt + 1) * 128], qT[:, hd // 2, hd % 2, rr * 64:(rr + 1) * 64],
                                       start=(hd == 0), stop=(hd == 7 and u is None), skip_group_check=True)
                    if u is not None:
                        ins = h.matmul(psS[:, 0:512], identb, tab[:, u, :], start=False, stop=True, skip_group_check=True)
                    return ins
                pe(mms, kvdeps + [tq, t_tab, t_ident], [tps])
                pt, tpt = PT[idx % NPT], t_PT[idx % NPT]
                act(lambda h: h.activation(out=pt[:, 0:512], in_=psS[:, 0:512], func=AF.Exp, scale=NA_SCALE), [tps], [tpt])
                pts[idx] = (pt, tpt)

            def issue_PV(idx):
                rr, ki, nt, tt, u = items[idx]
                if ki == 0:
                    pos[rr] = next_ps(6, 8)
                po, tpo = pos[rr]
                pt, tpt = pts.pop(idx)

                def mmo(h):
                    ins = None
                    for hd in range(8):
                        ins = h.matmul(po[:, hd * 64:(hd + 1) * 64], VN[:, tt, hd, :], pt[:, hd * 64:(hd + 1) * 64], start=(ki == 0 and hd == 0), stop=(ki == nt - 1 and hd == 7), skip_group_check=True)
                    return ins
                pe(mmo, kvdeps + [tpt, t_vones], [tpo])
                if ki == nt - 1:
                    q0 = t0 + rr * 64
                    dve(lambda h: h.reciprocal(out=bcs[0:64, 0:512], in_=po[64:128, 0:512]), [tpo], [t_bcs])
                    for par in range(2):
                        def fin(h, par=par):
                            pov = po[0:64, 0:512].rearrange("p (j a c) -> p j a c", j=4, a=2)[:, :, par, :]
                            bcv = bcs[0:64, 0:512].rearrange("p (j a c) -> p j a c", j=4, a=2)[:, :, par, :]
                            return h.tensor_tensor(out=ycT[par * 64:(par + 1) * 64, :, q0:q0 + 64], in0=pov, in1=bcv, op=ALU.mult)
                        dve(fin, [tpo, t_bcs], [t_yc])
            for idx in range(len(items) + LOOK):
                if idx < len(items):
                    issue_S(idx)
                if idx >= LOOK:
                    issue_PV(idx - LOOK)

        for bi, (t0, n) in enumerate(BLOCKS):
            stageQ(bi, t0, n)

    def gla_phase(l, e):
        cv = Carver(BIG, BIGN)
        oacc = cv.take(4 * T, F32).rearrange("p (h t) -> p h t", h=4)
        lrT = cv.take(128)
        wg = [cv.take(256) for _ in range(2)]
        Ltok = cv.take(256)
        k_tok = cv.take(256, F32)
        qTt = cv.take(512, F32)
        kTt = cv.take(512, F32)
        EK = cv.take(512, F32)
        v_toks = [cv.take(512) for _ in range(2)]
        kends = [cv.take(256) for _ in range(2)]
        EQs = [cv.take(512, F32) for _ in range(2)]
        qdecs = [cv.take(512).rearrange("p (h t) -> p h t", h=4) for _ in range(2)]
        kinvs = [cv.take(512).rearrange("p (h t) -> p h t", h=4) for _ in range(2)]
        attms = [cv.take(512).rearrange("p (h t) -> p h t", h=4) for _ in range(2)]
        S = [cv.take(512, F32).rearrange("p (h c) -> p h c", h=4) for _ in range(2)]
        Sbf = cv.take(512).rearrange("p (h c) -> p h c", h=4)
        mk = [cv.take(128) for _ in range(4)]
        mkf = [cv.take(128, F32) for _ in range(2)]
        sqg = cv.take(512)
        et, Rg, EE, osum_, sg_ = Fb[0], Fb[1], Fb[2], Fb[3], Fb[4]
        t_et, t_Rg, t_EE, t_osum, t_sg = tF[0], tF[1], tF[2], tF[3], tF[4]
        osum = osum_[:, 0:512].rearrange("p (h t) -> p h t", h=4)
        sg = sg_[:, 0:512].rearrange("p (h t) -> p h t", h=4)
        tk = lambda nme: P.tok("g_" + nme)
        t_lr, t_wg, t_L, t_kt = tk("lr"), ptok("g_wg"), tk("L"), tk("kt")
        t_qT, t_kT, t_EK, t_attm = tk("qT"), tk("kT"), tk("EK"), P.toks("g_attm", 2)
        t_v, t_kend, t_EQ, t_qdec, t_kinv = P.toks("g_v", 2), P.toks("g_kend", 2), P.toks("g_EQ", 2), P.toks("g_qdec", 2), P.toks("g_kinv", 2)
        t_S = [P.toks(f"gS{d}_", 4) for d in range(2)]
        t_Sbf, t_mk, t_sqg, t_oacc = tk("Sbf"), ptok("g_mk"), tk("sqg"), tk("oacc")
        wB = PW[:, 0:12544].rearrange("p (k c) -> p k c", k=8)
        for d in range(2):
            load_w(wg[d][0:17, 0:256], gla_gate[l, d], t_wg)
        for i in range(4):
            load_w(mk[i][:, 0:128], gmask[i], t_mk)
        for i in range(2):
            P.dma("sp", mkf[i][:, 0:128], gmask[i], [], [t_mk], ptok("g_mkf%d" % i))
        dve(lambda h: h.memset(lrT[0:32, 0:128], 1.0), [], [t_lr])
        for d in range(2):
            dve(lambda h, d=d: h.memset(S[d][:], 0.0), [], t_S[d])

        def stage_a(tt, d, sl):
            tok0 = tt * 128
            bi = 4 if tt >= 16 else tt // 4
            xs = lambda k: xnT[:, k, tok0:tok0 + 128]
            v_tok, kend, EQ, qdec, kinv = v_toks[sl], kends[sl], EQs[sl], qdecs[sl], kinvs[sl]
            pb, tp = next_ps(0, 6)

            def mm_lr(h):
                ins = None
                for k in range(8):
                    ins = h.matmul(pb[0:16, 0:128], wB[:, k, 1536 + d * 16:1536 + (d + 1) * 16], xs(k), start=(k == 0), stop=(k == 7))
                return ins
            pe(mm_lr, [t_PW, t_xn[bi]], [tp])
            dve(lambda h: h.tensor_copy(out=lrT[0:16, 0:128], in_=pb[0:16, 0:128]), [tp], [t_lr])
            pz, tpz = next_ps(0, 6)
            pe(lambda h: h.matmul(pz[:, 0:256], lrT[0:17, 0:128], wg[d][0:17, 0:256], start=True, stop=True), [t_lr, t_wg], [tpz])
            act(lambda h: h.activation(out=et[:, 0:256], in_=pz[:, 0:256], func=AF.Exp, scale=-1.0), [tpz], [t_et])
            act(lambda h: h.activation(out=et[:, 0:256], in_=et[:, 0:256], func=AF.Ln, bias=one_col[:, 0:1], scale=1.0), [t_et, t_const], [t_et])
            dve(lambda h: h.tensor_scalar(out=Ltok[:, 0:256], in0=et[:, 0:256], scalar1=1.0 / 16.0, scalar2=None, op0=ALU.mult), [t_et], [t_L])
            for which, dst, td in ((0, qTt, t_qT), (1, kTt, t_kT)):
                pq, tpq = next_ps(0, 6)

                def mmq(h, pq=pq, which=which):
                    ins = None
                    for j in range(2):
                        for k in range(8):
                            ins = h.matmul(pq[:, j * 128:(j + 1) * 128], wB[:, k, which * 256 + j * 128:which * 256 + (j + 1) * 128], xs(k), start=(k == 0), stop=(k == 7))
                    return ins
                pe(mmq, [t_PW, t_xn[bi]], [tpq])
                dst4 = dst[0:64, 0:512].rearrange("p (j a t) -> p j a t", j=2, a=2)
                act(lambda h, pq=pq, dst4=dst4: h.activation(out=dst4[:, :, 0, :], in_=pq[0:64, 0:256].rearrange("p (j t) -> p j t", j=2), func=AF.Copy), [tpq], [td])
                dve(lambda h, pq=pq, dst4=dst4: h.tensor_copy(out=dst4[:, :, 1, :], in_=pq[64:128, 0:256].rearrange("p (j t) -> p j t", j=2)), [tpq], [td])
            pk, tpk = next_ps(0, 6)

            def mmk(h):
                ins = None
                for k in range(8):
                    ins = h.matmul(pk[:, 0:256], xs(k), wB[:, k, 256:512], start=(k == 0), stop=(k == 7))
                return ins
            pe(mmk, [t_PW, t_xn[bi]], [tpk])
            dve(lambda h: h.tensor_copy(out=k_tok[:, 0:256], in_=pk[:, 0:256]), [tpk], [t_kt])
            pv, tpv = next_ps(0, 6)

            def mmv(h):
                ins = None
                for k in range(8):
                    ins = h.matmul(pv[:, 0:512], xs(k), wB[:, k, 512:1024], start=(k == 0), stop=(k == 7))
                return ins
            pe(mmv, [t_PW, t_xn[bi]], [tpv])
            act(lambda h: h.activation(out=v_tok[:, 0:512], in_=pv[:, 0:512], func=AF.Copy), [tpv], [t_v[sl]])
            pc, tpc = next_ps(0, 6)

            def mmc(h):
                ins = None
                for hd in range(4):
                    ins = h.matmul(pc[0:64, hd * 128:(hd + 1) * 128], Ltok[:, hd * 64:(hd + 1) * 64], mk[d][:, 0:128], start=True, stop=True)
                return ins
            pe(mmc, [t_L, t_mk], [tpc])
            act(lambda h: h.activation(out=EQ[0:64, 0:512], in_=pc[0:64, 0:512], func=AF.Exp, scale=-1.0), [tpc], [t_EQ[sl]])
            act(lambda h: h.activation(out=EK[0:64, 0:512], in_=pc[0:64, 0:512], func=AF.Exp, scale=1.0), [tpc], [t_EK])
            pr, tpr = next_ps(0, 6)
            pe(lambda h: h.matmul(pr[:, 0:256], mk[2 + d][:, 0:128], Ltok[:, 0:256], start=True, stop=True), [t_L, t_mk], [tpr])
            act(lambda h: h.activation(out=EE[:, 0:256], in_=pr[:, 0:256], func=AF.Exp, scale=-1.0), [tpr], [t_EE])
            dve(lambda h: h.scalar_tensor_tensor(out=qdec[0:64].rearrange("p h t -> p (h t)"), in0=qTt[0:64, 0:512], scalar=0.125, in1=EQ[0:64, 0:512], op0=ALU.mult, op1=ALU.mult),
                [t_qT, t_EQ[sl]], [t_qdec[sl]])
            dve(lambda h: h.tensor_tensor(out=kinv[0:64].rearrange("p h t -> p (h t)"), in0=kTt[0:64, 0:512], in1=EK[0:64, 0:512], op=ALU.mult), [t_kT, t_EK], [t_kinv[sl]])
            dve(lambda h: h.tensor_tensor(out=kend[:, 0:256], in0=k_tok[:, 0:256], in1=EE[:, 0:256], op=ALU.mult), [t_kt, t_EE], [t_kend[sl]])
            attm = attms[sl]
            pa, tpa = next_ps(0, 6)

            def mma(h):
                ins = None
                for hd in range(4):
                    ins = h.matmul(pa[:, hd * 128:(hd + 1) * 128], kinv[0:64, hd, :], qdec[0:64, hd, :], start=True, stop=True)
                return ins
            pe(mma, [t_kinv[sl], t_qdec[sl]], [tpa])
            for hd in range(4):
                dve(lambda h, hd=hd: h.tensor_tensor(out=attm[:, hd, :], in0=pa[:, hd * 128:(hd + 1) * 128], in1=mkf[d][:, 0:128], op=ALU.mult), [tpa, t_mk], [t_attm[sl]])

        def stage_b(tt, d, sl):
            tok0 = tt * 128
            bi = 4 if tt >= 16 else tt // 4
            v_tok, kend, EQ, qdec, kinv = v_toks[sl], kends[sl], EQs[sl], qdecs[sl], kinvs[sl]
            EQ3 = EQ.rearrange("p (h t) -> p h t", h=4)
            col = 127 if d == 0 else 0
            attm = attms[sl]
            po, tpo = next_ps(6, 8)
            dve(lambda h: h.tensor_copy(out=Sbf[0:64], in_=S[d][0:64]), t_S[d], [t_Sbf])

            def mmo(h):
                ins = None
                for hd in range(4):
                    reg = po[:, hd * 128:(hd + 1) * 128]
                    h.matmul(reg, v_tok[:, hd * 128:(hd + 1) * 128], attm[:, hd, :], start=True, stop=False)
                    ins = h.matmul(reg, Sbf[0:64, hd, :], qdec[0:64, hd, :], start=False, stop=True)
                return ins
            pe(mmo, [t_v[sl], t_attm[sl], t_Sbf, t_qdec[sl]], [tpo])
            pkv, tpkv = next_ps(0, 6)

            def mmkv(h):
                ins = None
                for hd in range(4):
                    ins = h.matmul(pkv[0:64, hd * 128:(hd + 1) * 128], kend[:, hd * 64:(hd + 1) * 64], v_tok[:, hd * 128:(hd + 1) * 128], start=True, stop=True)
                return ins
            pe(mmkv, [t_kend[sl], t_v[sl]], [tpkv])
            for hd in range(4):
                dve(lambda h, hd=hd: h.scalar_tensor_tensor(
                    out=S[d][0:64, hd, :], in0=S[d][0:64, hd, :], scalar=EQ3[0:64, hd, col:col + 1], in1=pkv[0:64, hd * 128:(hd + 1) * 128],
                    op0=ALU.mult, op1=ALU.add), [t_S[d][hd], t_EQ[sl], tpkv, t_Sbf], [t_S[d][hd]])
            if d == 0:
                act(lambda h: h.activation(out=oacc[:, :, tok0:tok0 + 128], in_=po[:, 0:512].rearrange("p (h t) -> p h t", h=4), func=AF.Copy), [tpo], [t_oacc])
                return
            dve(lambda h: h.tensor_tensor(out=osum[:], in0=po[:, 0:512].rearrange("p (h t) -> p h t", h=4), in1=oacc[:, :, tok0:tok0 + 128], op=ALU.add), [tpo, t_oacc], [t_osum])
            act(lambda h: h.activation(out=sqg[:, 0:512], in_=osum_[:, 0:512], func=AF.Square), [t_osum], [t_sqg])
            pss, tss = next_ps(0, 6)
            pe(lambda h: h.matmul(pss[:, 0:512], ones_bf[:], sqg[:, 0:512], start=True, stop=True), [t_sqg, t_const], [tss])
            rstd_from_ss(pss, tss, 512, 128, Rg, t_Rg)
            pg, tpg = next_ps(0, 6)

            def mmg(h):
                ins = None
                for hd in range(4):
                    for k in range(8):
                        ins = h.matmul(pg[:, hd * 128:(hd + 1) * 128], wB[:, k, 1024 + hd * 128:1024 + (hd + 1) * 128], xnT[:, k, tok0:tok0 + 128], start=(k == 0), stop=(k == 7))
                return ins
            pe(mmg, [t_PW, t_xn[bi]], [tpg])
            act(lambda h: h.activation(out=sg_[:, 0:512], in_=pg[:, 0:512], func=AF.Silu), [tpg], [t_sg])
            dve(lambda h: h.scalar_tensor_tensor(out=osum_[:, 0:512], in0=osum_[:, 0:512], scalar=gns[:, l:l + 1], in1=Rg[:, 0:512], op0=ALU.mult, op1=ALU.mult),
                [t_osum, t_Rg, t_const], [t_osum])
            dve(lambda h: h.tensor_tensor(out=ybT[:, :, tok0:tok0 + 128], in0=osum[:], in1=sg[:], op=ALU.mult), [t_osum, t_sg], [t_yb])

        order_f = [16, 17] + list(range(16))
        order_b = [17, 16] + list(range(15, -1, -1))
        seq = [(tt, 0) for tt in order_f] + [(tt, 1) for tt in order_b]
        stage_a(seq[0][0], seq[0][1], 0)
        for i, (tt, d) in enumerate(seq):
            if i + 1 < len(seq):
                stage_a(seq[i + 1][0], seq[i + 1][1], (i + 1) % 2)
            stage_b(tt, d, i % 2)

    def resid_update(e, bi, t0, n, oc, pb, tp, j):
        tf, ttf = next_f(3, 6)
        P.dma("sp", tf[:, 0:n], hcur[e][:, oc, t0:t0 + n], [t_h[e][bi]], [ttf], ttf)
        dve(lambda h: h.scalar_tensor_tensor(out=tf[:, 0:n], in0=pb[:, 0:n], scalar=Gc[:, oc, j:j + 1], in1=tf[:, 0:n], op0=ALU.mult, op1=ALU.add),
            [tp, ttf, t_coef], [ttf])
        P.dma("sp", h_scr[e][:, oc, t0:t0 + n], tf[:, 0:n], [ttf], [t_hs[e][bi]], ttf)

    def merge_phase(l, e):
        mT = BIG[:, 0:8 * T].rearrange("p (k t) -> p k t", k=8)
        macc, t_macc = Fb[0], tF[0]
        sgt, t_sgt = [Fb[1], Fb[2]], [tF[1], tF[2]]
        t_m = P.tok("mT")
        ys = ((yaT, w_a_o), (ybT, w_b_o), (ycT, w_c_o))
        def stage1(oc):
            buf, tb = next_ring()
            wg_ = buf[:, 0:3072].rearrange("p (x k c) -> p x k c", x=3, k=8)
            wo_ = buf[:, 3072:4608].rearrange("p (x k c) -> p x k c", x=3, k=4)
            for x in range(3):
                load_w(wg_[:, x], w_in_cols(l, G0 + x * 1024 + oc * 128, G0 + x * 1024 + (oc + 1) * 128), tb)
                load_w(wo_[:, x], ys[x][1][l].rearrange("(k p) c -> p k c", p=128)[:, :, oc * 128:(oc + 1) * 128], tb)
            for bi, (t0, n) in enumerate(BLOCKS):
                stage1b(oc, bi, t0, n, tb, wg_, wo_)

        def stage1b(oc, bi, t0, n, tb, wg_, wo_):
            if True:
                for x in range(3):
                    pg, tpg = next_ps(0, 4)
                    pp, tpp = next_ps(4, 8)

                    def mm(h, pg=pg, pp=pp, x=x):
                        ins = None
                        for k in range(8):
                            ins = h.matmul(pg[:, 0:n], wg_[:, x, k, :], xnT[:, k, t0:t0 + n], start=(k == 0), stop=(k == 7))
                        for k in range(4):
                            ins = h.matmul(pp[:, 0:n], wo_[:, x, k, :], ys[x][0][:, k, t0:t0 + n], start=(k == 0), stop=(k == 3))
                        return ins
                    pe(mm, [tb, t_xn[bi], t_ya[bi], t_yb, t_yc], [tpg, tpp])
                    sg_, tsg = sgt[x % 2], t_sgt[x % 2]
                    act(lambda h, pg=pg, sg_=sg_: h.activation(out=sg_[:, 0:n], in_=pg[:, 0:n], func=AF.Sigmoid), [tpg], [tsg])
                    if x == 0:
                        dve(lambda h, pp=pp, sg_=sg_: h.tensor_tensor(out=macc[:, 0:n], in0=pp[:, 0:n], in1=sg_[:, 0:n], op=ALU.mult), [tpp, tsg], [t_macc])
                    else:
                        dve(lambda h, pp=pp, sg_=sg_: h.tensor_tensor(out=sg_[:, 0:n], in0=pp[:, 0:n], in1=sg_[:, 0:n], op=ALU.mult), [tpp, tsg], [tsg])
                        if x == 1:
                            dve(lambda h, sg_=sg_: h.tensor_tensor(out=macc[:, 0:n], in0=macc[:, 0:n], in1=sg_[:, 0:n], op=ALU.add), [t_macc, tsg], [t_macc])
                        else:
                            dve(lambda h, sg_=sg_: h.tensor_tensor(out=mT[:, oc, t0:t0 + n], in0=macc[:, 0:n], in1=sg_[:, 0:n], op=ALU.add), [t_macc, tsg], [t_m])
        for oc in range(8):
            stage1(oc)

        def stage2(oc):
            buf, tb = next_ring()
            wo = buf[:, 0:1024].rearrange("p (k c) -> p k c", k=8)
            load_w(wo, w_out[l].rearrange("(k p) c -> p k c", p=128)[:, :, oc * 128:(oc + 1) * 128], tb)
            for bi, (t0, n) in enumerate(BLOCKS):
                stage2b(oc, bi, t0, n, tb, wo)

        def stage2b(oc, bi, t0, n, tb, wo):
            if True:
                pb, tp = next_ps()

                def mm(h, pb=pb, wo=wo):
                    ins = None
                    for k in range(8):
                        ins = h.matmul(pb[:, 0:n], wo[:, k, :], mT[:, k, t0:t0 + n], start=(k == 0), stop=(k == 7))
                    return ins
                pe(mm, [tb, t_m], [tp])
                resid_update(e, bi, t0, n, oc, pb, tp, 1 if t0 >= SEQ else 0)

        for oc in range(8):
            stage2(oc)

    def ffn_phase(l, e):
        NB_ = 15

        def ff(j):
            if j < NB_:
                return BIG[:, j * T:(j + 1) * T]
            return Y[:, (j - NB_) * T:(j - NB_ + 1) * T]
        sgt, t_sgt = [Fb[0], Fb[1]], [tF[0], tF[1]]
        t_ff = P.tok("ff")
        def stage1(j2):
            buf, tb = next_ring()
            wv = buf[:, 0:4096].rearrange("p (x k c) -> p x k c", x=2, k=8)
            load_w(wv[:, 0], w_ffn_in[l].rearrange("(k p) c -> p k c", p=128)[:, :, j2 * 256:(j2 + 1) * 256], tb)
            load_w(wv[:, 1], w_ffn_in[l].rearrange("(k p) c -> p k c", p=128)[:, :, FFH + j2 * 256:FFH + (j2 + 1) * 256], tb)
            for jj in range(2):
                j = j2 * 2 + jj
                for bi, (t0, n) in enumerate(BLOCKS):
                    stage1b(j, jj, bi, t0, n, tb, wv)

        def stage1b(j, jj, bi, t0, n, tb, wv):
            if True:
                if True:
                    pg, tpg = next_ps(0, 4)
                    pu, tpu = next_ps(4, 8)

                    def mm(h, pg=pg, pu=pu, jj=jj, wv=wv):
                        ins = None
                        for k in range(8):
                            ins = h.matmul(pg[:, 0:n], wv[:, 0, k, jj * 128:(jj + 1) * 128], xnT[:, k, t0:t0 + n], start=(k == 0), stop=(k == 7))
                        for k in range(8):
                            ins = h.matmul(pu[:, 0:n], wv[:, 1, k, jj * 128:(jj + 1) * 128], xnT[:, k, t0:t0 + n], start=(k == 0), stop=(k == 7))
                        return ins
                    pe(mm, [tb, t_xn[bi]], [tpg, tpu])
                    sg_, tsg = sgt[j % 2], t_sgt[j % 2]
                    act(lambda h, pg=pg, sg_=sg_: h.activation(out=sg_[:, 0:n], in_=pg[:, 0:n], func=AF.Silu), [tpg], [tsg])
                    dve(lambda h, pu=pu, sg_=sg_, j=j: h.tensor_tensor(out=ff(j)[:, t0:t0 + n], in0=pu[:, 0:n], in1=sg_[:, 0:n], op=ALU.mult), [tpu, tsg], [t_ff])
        for j2 in range(11):
            stage1(j2)

        def stage2(oc):
            buf, tb = next_ring()
            wo = buf[:, 0:22 * 128].rearrange("p (j c) -> p j c", j=22)
            load_w(wo, w_ffn_out[l].rearrange("(j p) c -> p j c", p=128)[:, :, oc * 128:(oc + 1) * 128], tb)
            for bi, (t0, n) in enumerate(BLOCKS):
                stage2b(oc, bi, t0, n, tb, wo)

        def stage2b(oc, bi, t0, n, tb, wo):
            if True:
                pb, tp = next_ps()

                def mm(h, pb=pb, wo=wo):
                    ins = None
                    for j in range(22):
                        ins = h.matmul(pb[:, 0:n], wo[:, j, :], ff(j)[:, t0:t0 + n], start=(j == 0), stop=(j == 21))
                    return ins
                pe(mm, [tb, t_ff], [tp])
                resid_update(e, bi, t0, n, oc, pb, tp, 1 if t0 >= SEQ else 0)

        for oc in range(8):
            stage2(oc)

    def tap(name, src_ap, toks):
        if name in tap_out:
            P.dma("sp", tap_out[name], src_ap, list(toks), [t_out], P.tok("tap"))

    t_out = P.tok("out")
    t_h0 = [P.toks(f"h0_{e}_", len(BLOCKS)) for e in range(2)]
    t_hs = [P.toks(f"hs_{e}_", len(BLOCKS)) for e in range(2)]
    hcur = [h0[0], h0[1]]
    t_h = [t_h0[0], t_h0[1]]
    P.phase = 'prologue'
    prologue()
    P.barrier()
    tap("modT", modT[:].rearrange("p l c e -> p (l c e)"), [t_modT])
    done = False
    for e in range(n_elems):
        if done:
            break
        for l in range(n_layers):
            P.phase = f'norm1_{e}_{l}'
            mla_weights(l)
            coef_cols(l, e, 0)
            norm_phase(e)
            mla_weights_prep(l)
            P.barrier()
            if stop_after == "norm1":
                done = True
                break
            P.phase = f'mla_{e}_{l}'
            mla_phase(l, e)
            na_weights(l)
            P.barrier()
            if stop_after == "mla":
                done = True
                break
            P.phase = f'na_{e}_{l}'
            na_phase(l, e)
            gla_weights(l)
            P.barrier()
            if stop_after == "na":
                done = True
                break
            P.phase = f'gla_{e}_{l}'
            gla_phase(l, e)
            P.barrier()
            if stop_after == "gla":
                done = True
                break
            P.phase = f'merge_{e}_{l}'
            merge_phase(l, e)
            hcur[e] = h_scr[e]
            t_h[e] = t_hs[e]
            P.barrier()
            if stop_after == "merge":
                done = True
                break
            P.phase = f'norm2_{e}_{l}'
            coef_cols(l, e, 1)
            norm_phase(e)
            P.barrier()
            P.phase = f'ffn_{e}_{l}'
            ffn_phase(l, e)
            P.barrier()
        if not done:
            P.phase = f'final_{e}'
            norm_phase(e, final=True)
            P.barrier()
    tap("Y", Y[:], t_ya + [t_yb, t_yc])
    tap("xnT", xnT[:], t_xn)
    tap("hs", h_scr[0], t_hs[0])
    P.add("sp", None, [], [t_out] + t_hs[0] + t_hs[1])
    P.emit()
    nc._prog = P
    return nc, stack


def _col(v, n):
    return np.ascontiguousarray(v.reshape(n, 128).T)


def prepare_shared(inp):
    L = DEPTH
    sh = {}
    for k in ("w_mod", "w_in", "w_a_o", "w_b_o", "w_c_o", "w_out", "w_ffn_in", "w_ffn_out"):
        sh[k] = np.ascontiguousarray(inp[k], dtype=np.float32)
    sh["w_q_up"] = np.ascontiguousarray(inp["mla_w_q_up"], dtype=np.float32)
    sh["w_kv_up"] = np.ascontiguousarray(inp["mla_w_kv_up"], dtype=np.float32)
    sh["b_modT"] = np.ascontiguousarray(np.stack([_col(inp["b_mod"][l], 48) for l in range(L)], 1))
    sh["n1T"] = np.ascontiguousarray(np.stack([_col(inp["norm1_w"][l], 8) for l in range(L)], 1))
    sh["n2T"] = np.ascontiguousarray(np.stack([_col(inp["norm2_w"][l], 8) for l in range(L)], 1))
    sh["fnT"] = _col(inp["final_norm_w"], 8)
    qn = np.stack([_col(inp["mla_q_norm_w"][l], 3) for l in range(L)], 1)
    sh["qnT"] = np.ascontiguousarray(qn)
    kvn = np.stack([_col(inp["mla_kv_norm_w"][l], 2) for l in range(L)], 1)
    sh["kvnT"] = np.ascontiguousarray(kvn)
    sh["gnT"] = np.ascontiguousarray(inp["gla_norm_w"].T)
    gg = np.zeros((L, 2, 17, 256), np.float32)
    gg[:, 0, :16] = inp["gla_w_gate_f"]
    gg[:, 0, 16] = inp["gla_b_gate_f"]
    gg[:, 1, :16] = inp["gla_w_gate_b"]
    gg[:, 1, 16] = inp["gla_b_gate_b"]
    sh["gla_gate"] = gg
    ia, ib, neg = build_na_gather_index(_PATS)
    hidx = np.arange(8)[None, None, :, None]
    tabs = []
    for l in range(L):
        g = inp["na_rpb"][l][hidx, ia, ib]
        tabs.append(np.ascontiguousarray(g.reshape(_NU, 128, 512).transpose(1, 0, 2)))
    sh["na_tab"] = np.ascontiguousarray(np.stack(tabs, 0), dtype=np.float32)
    sh["na_neg"] = np.ascontiguousarray(neg.reshape(_NU, 128, 512).transpose(1, 0, 2)) * np.float32(1.0 / NA_SCALE)
    C, S = rope_tables()
    sh["ropeC"], sh["ropeS"] = C, S
    sh["gmask"] = gla_masks()
    sh["ident"] = np.eye(128, dtype=np.float32)
    return sh


def prepare_core(inp, core):
    b0 = 2 * core
    h0 = np.empty((2, 128, 8, T), np.float32)
    cT = np.empty((128, 8, 3), np.float32)
    for e in range(2):
        full = np.concatenate([inp["x"][b0 + e], inp["ctx"][b0 + e]], 0)
        h0[e] = full.T.reshape(8, 128, T).transpose(1, 0, 2)
        cT[:, :, e] = _col(inp["c"][b0 + e], 8)
    cT[:, :, 2] = _col(inp["c_ctx"], 8)
    return {"h0": h0, "cT": cT}


_CACHE = {}


def kernel(**inputs):
    inp = {k: np.asarray(v) for k, v in inputs.items()}
    if "prog" not in _CACHE:
        _CACHE["prog"] = build_program()
    nc, _ = _CACHE["prog"]
    sh = prepare_shared(inp)
    in_maps = []
    for core in range(8):
        m = dict(sh)
        m.update(prepare_core(inp, core))
        in_maps.append(m)
    res = run_bass_kernel_spmd(nc, in_maps, core_ids=list(range(8)))
    out = np.empty((16, SEQ, D), np.float32)
    for core in range(8):
        o = res.results[core]["outT"]
        for e in range(2):
            out[2 * core + e] = o[e].transpose(2, 1, 0).reshape(SEQ, D)
    return out
```

```python
import numpy as np
from contextlib import ExitStack
import concourse.bass as bass
import concourse.mybir as mybir
from concourse.bass_utils import run_bass_kernel_spmd

F32 = mybir.dt.float32
BF16 = mybir.dt.bfloat16
AF = mybir.ActivationFunctionType
ALU = mybir.AluOpType

D = 1024
DEPTH = 4
SEQ = 2048
CTX = 256
T = SEQ + CTX
GRID_W = 64
EPS = 1e-6
IN_W = 6848
A0, B0, C0, G0 = 0, 672, 2240, 3776
FFH = 2816
MLA_SCALE = 96 ** -0.5
NA_SCALE = 0.125
BLOCKS = [(0, 512), (512, 512), (1024, 512), (1536, 512), (2048, 256)]
NEG = -30000.0
SEM_CAP = 30000
SAME_ENG_SYNC = True


class Tok:
    __slots__ = ("name", "w", "r", "dsem", "dcnt")

    def __init__(self, name):
        self.name = name
        self.w = None
        self.r = {}
        self.dsem = None
        self.dcnt = 0


class _CountProxy:
    def __init__(self, h, prog):
        self._h, self._p = h, prog

    def matmul(self, *a, **k):
        c = self._p.mm_counts
        c[self._p._cur_phase] = c.get(self._p._cur_phase, 0) + 1
        return self._h.matmul(*a, **k)

    def __getattr__(self, name):
        return getattr(self._h, name)


INSTRUMENT = False


class Op:
    __slots__ = ("eng", "fn", "deps", "ev", "signal", "signo", "waits", "dma", "phase", "raw")

    def __init__(self, eng, fn):
        self.eng = eng
        self.fn = fn
        self.deps = set()
        self.raw = set()
        self.ev = None
        self.signal = False
        self.signo = 0
        self.waits = []
        self.dma = None


class Prog:
    ENGS = ["pe", "act", "dve", "pool", "sp"]

    def __init__(self, nc, stack):
        self.nc = nc
        self.stack = stack
        self.ops = {e: [] for e in self.ENGS}
        self.anchors = []
        self.active_anchors = set()
        self.phase = "init"
        self.mm_counts = {}

    def tok(self, name):
        return Tok(name)

    def toks(self, name, n):
        return [Tok(f"{name}{i}") for i in range(n)]

    def _deps(self, op, reads, writes, anchor=None):
        for t in reads:
            if t.w is not None:
                op.deps.add(t.w)
                op.raw.add(t.w)
        for t in writes:
            if t.w is not None and not (anchor is not None and t.w[0] == "d" and t.w[1] is anchor):
                op.deps.add(t.w)
            for k, v in t.r.items():
                op.deps.add((k[0], k[1], v))

    def _commit(self, ev, reads, writes):
        for t in reads:
            key = (ev[0], ev[1])
            if t.r.get(key, 0) < ev[2]:
                t.r[key] = ev[2]
        for t in writes:
            t.w = ev
            t.r = {}

    def add(self, eng, fn, reads=(), writes=()):
        op = Op(eng, fn)
        op.phase = self.phase
        self._deps(op, reads, writes)
        self.ops[eng].append(op)
        ev = ("c", eng, len(self.ops[eng]))
        op.ev = ev
        self._commit(ev, reads, writes)
        return op

    def dma(self, q, out, in_, reads, writes, anchor, exempt=False):
        op = Op(q, None)
        op.dma = (out, in_, anchor)
        self._deps(op, reads, writes, anchor)
        if anchor.dsem is None:
            anchor.dsem = self.stack.enter_context(self.nc.semaphore(f"d{len(self.anchors)}"))
            self.anchors.append(anchor)
        anchor.dcnt += 16
        if not exempt:
            self.active_anchors.add(anchor)
        self.ops[q].append(op)
        ev = ("d", anchor, anchor.dcnt)
        op.ev = ev
        self._commit(ev, reads, writes)
        return op

    def barrier(self):
        evs = []
        for e in self.ENGS:
            if self.ops[e]:
                last = None
                for o in reversed(self.ops[e]):
                    if o.ev[0] == "c" and o.fn is not None:
                        last = o
                        break
                if last is not None:
                    evs.append(last.ev)
        devs = [("d", a, a.dcnt) for a in self.active_anchors]
        self.active_anchors = set()
        for e in self.ENGS:
            op = Op(e, None)
            for ev in evs:
                if ev[1] != e:
                    op.deps.add(ev)
            for ev in devs:
                op.deps.add(ev)
            self.ops[e].append(op)
            op.ev = ("c", e, len(self.ops[e]))

    def finalize(self):
        for e in self.ENGS:
            waited = {}
            for op in self.ops[e]:
                best = {}
                for ev in op.deps:
                    if ev[0] == "c":
                        if ev[1] == e and (e == "pe" or not SAME_ENG_SYNC):
                            continue
                        if ev[1] == e and ev not in op.raw:
                            continue
                        if ev[1] == e and ev[2] >= op.ev[2]:
                            continue
                    key = (ev[0], ev[1])
                    if ev[2] > best.get(key, 0):
                        best[key] = ev[2]
                for key, v in best.items():
                    if waited.get(key, 0) >= v:
                        continue
                    waited[key] = v
                    if key[0] == "c":
                        src = self.ops[key[1]][v - 1]
                        idx = v - 1
                        while src.fn is None and src.dma is None:
                            idx -= 1
                            if idx < 0:
                                src = None
                                break
                            src = self.ops[key[1]][idx]
                        if src is None:
                            continue
                        if src.dma is not None:
                            op.waits.append(("d", src.dma[2], src.ev[2]))
                            continue
                        src.signal = True
                        op.waits.append(("c", key[1], src))
                    else:
                        op.waits.append(("d", key[1], v))
        self.csems = {}
        for e in self.ENGS:
            n = 0
            for op in self.ops[e]:
                if op.signal:
                    n += 1
                    op.signo = n
            nsem = (n + SEM_CAP - 1) // SEM_CAP
            self.csems[e] = [self.stack.enter_context(self.nc.semaphore(f"c_{e}{i}")) for i in range(max(nsem, 1))]

    def _semval(self, e, signo):
        return self.csems[e][(signo - 1) // SEM_CAP], (signo - 1) % SEM_CAP + 1

    def emit_engine(self, e, h):
        if INSTRUMENT and e == "pe":
            h = _CountProxy(h, self)
        for op in self.ops[e]:
            self._cur_phase = getattr(op, "phase", None)
            for w in op.waits:
                if w[0] == "c":
                    sem, val = self._semval(w[1], w[2].signo)
                    h.wait_ge(sem, val)
                else:
                    h.wait_ge(w[1].dsem, w[2])
            if op.dma is not None:
                out, in_, anchor = op.dma
                h.dma_start(out=out, in_=in_).then_inc(anchor.dsem, 16)
            elif op.fn is not None:
                ins = op.fn(h)
                if op.signal:
                    sem, _ = self._semval(e, op.signo)
                    ins.then_inc(sem, 1)

    def emit(self):
        self.finalize()
        nc = self.nc
        with nc.Block() as block:
            @block.tensor
            def _(h):
                self.emit_engine("pe", h)

            @block.scalar
            def _(h):
                self.emit_engine("act", h)

            @block.vector
            def _(h):
                self.emit_engine("dve", h)

            @block.gpsimd
            def _(h):
                self.emit_engine("pool", h)

            @block.sync
            def _(h):
                self.emit_engine("sp", h)


def na_tables():
    rows = SEQ // GRID_W
    pats = {}
    plan = {}
    for r in range(rows):
        r0 = min(max(r - 4, 0), rows - 8)
        tiles = list(range(r0 // 2, (r0 + 7) // 2 + 1))
        lst = []
        for tt in tiles:
            j0 = 2 * tt
            key = (j0 - r, r0 <= j0 < r0 + 8, r0 <= j0 + 1 < r0 + 8)
            if key not in pats:
                pats[key] = len(pats)
            lst.append((tt, pats[key]))
        plan[r] = lst
    return pats, plan


def build_na_gather_index(pats):
    NU = len(pats)
    ia = np.zeros((NU, 128, 8, 64), np.int64)
    ib = np.zeros((NU, 128, 8, 64), np.int64)
    neg = np.zeros((NU, 128, 8, 64), np.float32)
    col = np.arange(64)
    cs = np.clip(col - 8, 0, 48)
    for (dr0, v0, v1), u in pats.items():
        for jj in range(2):
            valid = (v0, v1)[jj]
            dr = dr0 + jj
            for kc in range(64):
                p = jj * 64 + kc
                for qc in range(64):
                    ok = valid and (kc >= cs[qc]) and (kc < cs[qc] + 16) and (-7 <= dr <= 7)
                    a = min(max(dr + 7, 0), 14)
                    b = min(max(kc - qc + 15, 0), 30)
                    ia[u, p, :, qc] = a
                    ib[u, p, :, qc] = b
                    if not ok:
                        neg[u, p, :, qc] = NEG
    return ia, ib, neg


def rope_tables():
    t = np.arange(SEQ)
    rows = (t // GRID_W).astype(np.float32)
    cols = (t % GRID_W).astype(np.float32)
    inv = (10000.0 ** (-np.arange(8, dtype=np.float32) / 8)).astype(np.float32)
    ang = np.concatenate([rows[:, None] * inv, cols[:, None] * inv], -1)
    cos, sin = np.cos(ang).astype(np.float32), np.sin(ang).astype(np.float32)
    C = np.ones((96, T), np.float32)
    S = np.zeros((96, T), np.float32)
    C[64:80, :SEQ] = cos.T
    C[80:96, :SEQ] = cos.T
    S[64:80, :SEQ] = sin.T
    S[80:96, :SEQ] = sin.T
    return C, S


def gla_masks():
    s = np.arange(128)[:, None]
    t = np.arange(128)[None, :]
    m = np.zeros((4, 128, 128), np.float32)
    m[0] = (s <= t)
    m[1] = (s >= t)
    m[2] = (s > t)
    m[3] = (s < t)
    return m


_PATS, _PLAN = na_tables()
_NU = len(_PATS)


PHASES = ("norm1", "mla", "na", "gla", "merge", "norm2", "ffn")


def build_program(n_layers=DEPTH, n_elems=2, taps=(), stop_after=None):
    nc = bass.Bass("TRN2", target_bir_lowering=False)
    stack = ExitStack()
    P = Prog(nc, stack)
    L = DEPTH

    def din(name, shape, dt=F32):
        return nc.dram_tensor(name, list(shape), dt, kind="ExternalInput").ap()

    h0 = din("h0", [2, 128, 8, T])
    cT = din("cT", [128, 8, 3])
    w_mod = din("w_mod", [L, D, 6 * D])
    b_modT = din("b_modT", [128, L, 48])
    n1T = din("n1T", [128, L, 8])
    n2T = din("n2T", [128, L, 8])
    fnT = din("fnT", [128, 8])
    w_in = din("w_in", [L, D, IN_W])
    qnT = din("qnT", [128, L, 3])
    kvnT = din("kvnT", [128, L, 2])
    w_q_up = din("w_q_up", [L, 384, 768])
    w_kv_up = din("w_kv_up", [L, 256, 1024])
    gla_gate = din("gla_gate", [L, 2, 17, 256])
    gnT = din("gnT", [128, L])
    na_tab = din("na_tab", [L, 128, _NU, 512])
    na_neg = din("na_neg", [128, _NU, 512])
    w_a_o = din("w_a_o", [L, 512, D])
    w_b_o = din("w_b_o", [L, 512, D])
    w_c_o = din("w_c_o", [L, 512, D])
    w_out = din("w_out", [L, D, D])
    w_ffn_in = din("w_ffn_in", [L, D, 2 * FFH])
    w_ffn_out = din("w_ffn_out", [L, FFH, D])
    ropeC = din("ropeC", [96, T])
    ropeS = din("ropeS", [96, T])
    gmask = din("gmask", [4, 128, 128])
    ident_in = din("ident", [128, 128])
    outT = nc.dram_tensor("outT", [2, 128, 8, SEQ], F32, kind="ExternalOutput").ap()
    h_scr = nc.dram_tensor("h_scr", [2, 128, 8, T], F32, kind="Internal").ap()
    tap_out = {}
    for name, shape, dt_ in taps:
        tap_out[name] = nc.dram_tensor("tap_" + name, list(shape), dt_, kind="ExternalOutput").ap()

    def sb(name, shape, dt=F32):
        return stack.enter_context(nc.sbuf_tensor(name, list(shape), dt))

    ps = [stack.enter_context(nc.psum_tensor(f"ps{i}", [128, 512], F32)) for i in range(8)]
    pst = P.toks("ps", 8)

    ones_bf = sb("ones_bf", [128, 128], BF16)
    ones_f = sb("ones_f", [128, 64], F32)
    eps_col = sb("eps_col", [128, 1])
    one_col = sb("one_col", [128, 1])
    modT = sb("modT", [128, L, 48, 3])
    t_modT = P.tok("modT")
    cact = sb("cact", [128, 8, 3], BF16)
    cin = sb("cin", [128, 8, 3])
    bmod = sb("bmod", [128, L, 48])
    n1s = sb("n1s", [128, L, 8])
    n2s = sb("n2s", [128, L, 8])
    fns = sb("fns", [128, 8])
    qns = sb("qns", [128, L, 3])
    kvns = sb("kvns", [128, L, 2])
    gns = sb("gns", [128, L])
    t_const = P.tok("const")
    Wc = sb("Wc", [128, 8, 2])
    Sc = sb("Sc", [128, 8, 2])
    Gc = sb("Gc", [128, 8, 2])
    t_coef = P.tok("coef")
    xnT = sb("xnT", [128, 8, T], BF16)
    t_xn = P.toks("xn", len(BLOCKS))
    Y = sb("Y", [128, 3 * 4 * T], BF16)
    yaT = Y[:, 0:4 * T].rearrange("p (j t) -> p j t", j=4)
    ybT = Y[:, 4 * T:8 * T].rearrange("p (j t) -> p j t", j=4)
    ycT = Y[:, 8 * T:12 * T].rearrange("p (j t) -> p j t", j=4)
    t_ya = P.toks("ya", len(BLOCKS))
    t_yb = P.tok("yb")
    t_yc = P.tok("yc")
    Fb = [sb(f"F{i}", [128, 512]) for i in range(7)]
    tF = P.toks("F", 7)
    Rb, t_Rb = Fb[6], tF[6]
    PWN = 13824
    PW = sb("PW", [128, PWN], BF16)
    t_PW = P.tok("PW")
    ring = [PW[:, i * 4608:(i + 1) * 4608] for i in range(3)]
    t_ring = P.toks("ring", 3)
    BIGN = 34816
    BIG = sb("BIG", [128, BIGN], BF16)
    t_big = P.tok("BIG")

    cnt = {"ring": 0, "ps": 0, "f": 0}
    _ptoks = {}

    def ptok(name):
        if name not in _ptoks:
            _ptoks[name] = P.tok(name)
        return _ptoks[name]

    def next_ring():
        i = cnt["ring"] % 3
        cnt["ring"] += 1
        return ring[i], t_ring[i]

    def ring_load(dst, src, tb):
        load_w(dst, src, tb, extra_writes=[t_PW, ptok("wA"), ptok("wq"), ptok("wkv"), ptok("wAr"), ptok("wqr")], exempt=True)

    def next_ps(lo=0, hi=8):
        n = hi - lo
        i = lo + cnt["ps"] % n
        cnt["ps"] += 1
        return ps[i], pst[i]

    def next_f(lo, hi):
        n = hi - lo
        i = lo + cnt["f"] % n
        cnt["f"] += 1
        return Fb[i], tF[i]

    def act(fn, reads, writes):
        return P.add("act", fn, reads, writes)

    def dve(fn, reads, writes):
        return P.add("dve", fn, reads, writes)

    def pe(fn, reads, writes):
        return P.add("pe", fn, reads, writes)

    class Carver:
        def __init__(self, base, limit):
            self.base, self.o, self.limit = base, 0, limit

        def take(self, nel, dt=BF16):
            n = nel if dt == BF16 else 2 * nel
            ap = self.base[:, self.o:self.o + n]
            self.o += n
            assert self.o <= self.limit, (self.o, self.limit)
            return ap if dt == BF16 else ap.bitcast(F32)

    def rstd_from_ss(ps_ss, t_ss, n, dim, Rout, t_R):
        act(lambda h: h.activation(out=Rout[:, 0:n], in_=ps_ss[:, 0:n], func=AF.Ln, scale=1.0 / dim, bias=eps_col[:, 0:1]),
            [t_ss, t_const], [t_R])
        act(lambda h: h.activation(out=Rout[:, 0:n], in_=Rout[:, 0:n], func=AF.Exp, scale=-0.5), [t_R], [t_R])

    def load_w(dst_ap, src_ap, tok, reads=(), extra_writes=(), exempt=False):
        P.dma("pool", dst_ap, src_ap, list(reads), [tok] + list(extra_writes), tok, exempt=exempt)

    def pw_users():
        return [t_PW, ptok("wA"), ptok("wq"), ptok("wkv"), ptok("wAr"), ptok("wqr")] + t_ring

    def w_in_cols(l, c0, c1):
        return w_in[l].rearrange("(k p) c -> p k c", p=128)[:, :, c0:c1]

    def prologue():
        dve(lambda h: h.memset(ones_bf[:], 1.0), [], [t_const])
        dve(lambda h: h.memset(ones_f[:], 1.0), [], [t_const])
        dve(lambda h: h.memset(eps_col[:], EPS), [], [t_const])
        dve(lambda h: h.memset(one_col[:], 1.0), [], [t_const])
        tl = P.tok("ld_small")
        for dst, src in ((cin, cT), (bmod, b_modT), (n1s, n1T), (n2s, n2T), (fns, fnT), (qns, qnT), (kvns, kvnT), (gns, gnT)):
            P.dma("sp", dst[:], src, [], [tl, t_const], P.tok("a"))
        act(lambda h: h.activation(out=cin[:], in_=cin[:], func=AF.Silu), [tl, t_const], [tl])
        dve(lambda h: h.tensor_copy(out=cact[:], in_=cin[:]), [tl], [t_const])
        for l in range(n_layers):
            for cch in range(12):
                buf, tb = next_ring()
                bv = buf[:, 0:4096].rearrange("p (k c) -> p k c", k=8)
                load_w(bv, w_mod[l].rearrange("(k p) c -> p k c", p=128)[:, :, cch * 512:(cch + 1) * 512], tb)
                pb, tp = next_ps()

                def mm(h, bv=bv, pb=pb):
                    ins = None
                    for j in range(4):
                        for k in range(8):
                            ins = h.matmul(pb[:, j * 4:j * 4 + 3], bv[:, k, j * 128:(j + 1) * 128], cact[:, k, :], start=(k == 0), stop=(k == 7))
                    return ins
                pe(mm, [tb, t_const], [tp])

                def ev(h, pb=pb, l=l, cch=cch):
                    ins = None
                    for j in range(4):
                        ins = h.tensor_scalar(out=modT[:, l, cch * 4 + j, :], in0=pb[:, j * 4:j * 4 + 3], scalar1=bmod[:, l, cch * 4 + j:cch * 4 + j + 1],
                                              scalar2=None, op0=ALU.add)
                    return ins
                dve(ev, [tp, t_const], [t_modT])

    def coef_cols(l, e, which):
        base = 3 * which
        nw = n1s if which == 0 else n2s
        for j, colm in ((0, e), (1, 2)):
            dve(lambda h, j=j, colm=colm: h.scalar_tensor_tensor(out=Wc[:, :, j], in0=modT[:, l, (base + 1) * 8:(base + 2) * 8, colm], scalar=1.0,
                                                               in1=nw[:, l, :], op0=ALU.add, op1=ALU.mult), [t_modT, t_const], [t_coef])
            dve(lambda h, j=j, colm=colm: h.tensor_copy(out=Sc[:, :, j], in_=modT[:, l, base * 8:(base + 1) * 8, colm]), [t_modT], [t_coef])
            dve(lambda h, j=j, colm=colm: h.tensor_copy(out=Gc[:, :, j], in_=modT[:, l, (base + 2) * 8:(base + 3) * 8, colm]), [t_modT], [t_coef])

    def norm_phase(e, final=False):
        cv = Carver(BIG, BIGN)
        hb = [cv.take(4096, F32).rearrange("p (k t) -> p k t", k=8) for _ in range(2)]
        sqbs = [cv.take(4096).rearrange("p (k t) -> p k t", k=8) for _ in range(2)]
        Rbs = [cv.take(512, F32) for _ in range(2)]
        ob = cv.take(4096, F32).rearrange("p (k t) -> p k t", k=8) if final else None
        t_hb, t_sqbs, t_ob, t_Rbs = [ptok("hb0"), ptok("hb1")], P.toks("sqb", 2), ptok("ob"), P.toks("nRb", 2)
        for bi, (t0, n) in enumerate(BLOCKS):
            if final and t0 >= SEQ:
                continue
            j = 1 if t0 >= SEQ else 0
            hbuf, thb = hb[bi % 2], t_hb[bi % 2]
            sqb, t_sqb = sqbs[bi % 2], t_sqbs[bi % 2]
            Rb, t_Rb = Rbs[bi % 2], t_Rbs[bi % 2]
            P.dma("sp", hbuf[:, :, 0:n], hcur[e][:, :, t0:t0 + n], [t_h[e][bi]], [thb], thb)
            act(lambda h, hbuf=hbuf, n=n, sqb=sqb: h.activation(out=sqb[:, :, 0:n], in_=hbuf[:, :, 0:n], func=AF.Square), [thb], [t_sqb])
            pb, tp = next_ps()

            def mm(h, pb=pb, n=n, sqb=sqb):
                ins = None
                for k in range(8):
                    ins = h.matmul(pb[:, 0:n], ones_bf[:], sqb[:, k, 0:n], start=(k == 0), stop=(k == 7))
                return ins
            pe(mm, [t_sqb, t_const], [tp])
            rstd_from_ss(pb, tp, n, D, Rb, t_Rb)
            if not final:
                for k in range(8):
                    tf, ttf = next_f(0, 3)
                    dve(lambda h, k=k, tf=tf, hbuf=hbuf, n=n, j=j, Rb=Rb: h.scalar_tensor_tensor(
                        out=tf[:, 0:n], in0=hbuf[:, k, 0:n], scalar=Wc[:, k, j:j + 1], in1=Rb[:, 0:n], op0=ALU.mult, op1=ALU.mult),
                        [thb, t_Rb, t_coef], [ttf])
                    act(lambda h, k=k, tf=tf, n=n, t0=t0, j=j: h.activation(
                        out=xnT[:, k, t0:t0 + n], in_=tf[:, 0:n], func=AF.Identity, bias=Sc[:, k, j:j + 1], scale=1.0),
                        [ttf, t_coef], [t_xn[bi]])
            else:
                for k in range(8):
                    dve(lambda h, k=k, hbuf=hbuf, n=n, Rb=Rb: h.scalar_tensor_tensor(
                        out=ob[:, k, 0:n], in0=hbuf[:, k, 0:n], scalar=fns[:, k:k + 1], in1=Rb[:, 0:n], op0=ALU.mult, op1=ALU.mult),
                        [thb, t_Rb, t_const], [t_ob])
                P.dma("sp", outT[e][:, :, t0:t0 + n], ob[:, :, 0:n], [t_ob], [t_out], t_ob)

    mla_w = {}

    def mla_weights(l):
        wA = PW[:, 0:5376].rearrange("p (k c) -> p k c", k=8)
        wAr = PW[:, 5376:6144].rearrange("p (k c) -> p k c", k=8)
        wq = PW[:, 6144:8448].rearrange("p (k c) -> p k c", k=3)
        wqr = PW[:, 8448:10752].rearrange("p (k c) -> p k c", k=3)
        wkv = PW[:, 10752:12800].rearrange("p (k c) -> p k c", k=2)
        t_wA, t_wq, t_wkv, t_wAr, t_wqr = ptok("wA"), ptok("wq"), ptok("wkv"), ptok("wAr"), ptok("wqr")
        load_w(wA, w_in_cols(l, A0, A0 + 672), t_wA, extra_writes=pw_users(), exempt=True)
        load_w(wq, w_q_up[l].rearrange("(k p) c -> p k c", p=128), t_wq, exempt=True)
        load_w(wkv, w_kv_up[l].rearrange("(k p) c -> p k c", p=128), t_wkv, exempt=True)
        mla_w.update(wA=wA, wAr=wAr, wq=wq, wqr=wqr, wkv=wkv)

    def mla_weights_prep(l):
        wA, wAr, wq, wqr, wkv = mla_w["wA"], mla_w["wAr"], mla_w["wq"], mla_w["wqr"], mla_w["wkv"]
        t_wA, t_wq, t_wkv, t_wAr, t_wqr = ptok("wA"), ptok("wq"), ptok("wkv"), ptok("wAr"), ptok("wqr")

        def scale_q(h):
            ins = None
            for k in range(3):
                ins = h.tensor_scalar(out=wq[:, k, :], in0=wq[:, k, :], scalar1=qns[:, l, k:k + 1], scalar2=None, op0=ALU.mult)
            return ins
        dve(scale_q, [t_wq, t_const], [t_wq])

        def scale_kv(h):
            ins = None
            for k in range(2):
                ins = h.tensor_scalar(out=wkv[:, k, :], in0=wkv[:, k, :], scalar1=kvns[:, l, k:k + 1], scalar2=None, op0=ALU.mult)
            return ins
        dve(scale_kv, [t_wkv, t_const], [t_wkv])
        dve(lambda h: h.memset(wAr[:], 0.0), [t_wA], [t_wAr])

        def rotA(h):
            h.tensor_scalar(out=wAr[:, :, 64:80], in0=wA[:, :, 656:672], scalar1=-1.0, scalar2=None, op0=ALU.mult)
            return h.tensor_copy(out=wAr[:, :, 80:96], in_=wA[:, :, 640:656])
        dve(rotA, [t_wA, t_wAr], [t_wAr])
        dve(lambda h: h.memset(wqr[:], 0.0), [t_wA], [t_wqr])

        def rotQ(h):
            wq4 = wq.rearrange("p k (h c) -> p k h c", h=8)
            wqr4 = wqr.rearrange("p k (h c) -> p k h c", h=8)
            ins = None
            for k in range(3):
                h.tensor_scalar(out=wqr4[:, k, :, 64:80], in0=wq4[:, k, :, 80:96], scalar1=-1.0, scalar2=None, op0=ALU.mult)
                ins = h.tensor_copy(out=wqr4[:, k, :, 80:96], in_=wq4[:, k, :, 64:80])
            return ins
        dve(rotQ, [t_wq, t_wqr], [t_wqr])

    def na_weights(l):
        wC = PW[:, 0:12288].rearrange("p (k c) -> p k c", k=8)
        load_w(wC, w_in_cols(l, C0, C0 + 1536), t_PW, extra_writes=pw_users(), exempt=True)

    def gla_weights(l):
        wB = PW[:, 0:12544].rearrange("p (k c) -> p k c", k=8)
        load_w(wB, w_in_cols(l, B0, B0 + 1568), t_PW, extra_writes=pw_users(), exempt=True)

    def mla_phase(l, e):
        KT = Y[:, 4 * T:12 * T].rearrange("p (h t) -> p h t", h=8)
        cv = Carver(BIG, BIGN)
        VA = cv.take(18 * 8 * 128).rearrange("p (t h c) -> p t h c", t=18, h=8)
        QTb = [cv.take(4096).rearrange("p (h t) -> p h t", h=8) for _ in range(2)]
        qdn = cv.take(1536).rearrange("p (c t) -> p c t", c=3)
        kvdn = cv.take(1024).rearrange("p (c t) -> p c t", c=2)
        sqb = cv.take(1536).rearrange("p (c t) -> p c t", c=3)
        NPT = 5
        PT = [cv.take(512) for _ in range(NPT)]
        t1b, t2b, bcs, rsb, rC, rS = Fb[0], Fb[1], Fb[2], Fb[3], Fb[4], Fb[5]
        t_t1, t_t2, t_bcs, t_rs, t_rC, t_rS = tF[0], tF[1], tF[2], tF[3], tF[4], tF[5]
        t_KT = P.toks("KT", len(BLOCKS))
        t_VA = P.toks("VA", len(BLOCKS))
        t_QT = P.toks("QT", 2)
        t_qdn, t_kvdn, t_sqb = P.tok("qdn"), P.tok("kvdn"), P.tok("msq")
        t_PT = P.toks("PT", NPT)
        t_vones = P.tok("vones")
        wA, wAr, wq, wqr, wkv = mla_w["wA"], mla_w["wAr"], mla_w["wq"], mla_w["wqr"], mla_w["wkv"]
        t_wA, t_wq, t_wkv, t_wAr, t_wqr = ptok("wA"), ptok("wq"), ptok("wkv"), ptok("wAr"), ptok("wqr")
        dve(lambda h: h.memset(VA[:, :, :, 64:128], 1.0), [], [t_vones])

        def down_norm(c0, nchunk, dim, dst, t_dst, bi, t0, n):
            banks = []
            for c in range(nchunk):
                pb, tp = next_ps(0, 4)
                banks.append((pb, tp))

                def mm(h, pb=pb, c=c):
                    ins = None
                    for k in range(8):
                        ins = h.matmul(pb[:, 0:n], wA[:, k, c0 + c * 128:c0 + (c + 1) * 128], xnT[:, k, t0:t0 + n], start=(k == 0), stop=(k == 7))
                    return ins
                pe(mm, [t_wA, t_xn[bi]], [tp])
                act(lambda h, pb=pb, c=c: h.activation(out=sqb[:, c, 0:n], in_=pb[:, 0:n], func=AF.Square), [tp], [t_sqb])
            pss, tss = next_ps(4, 6)

            def mm2(h):
                ins = None
                for c in range(nchunk):
                    ins = h.matmul(pss[:, 0:n], ones_bf[:], sqb[:, c, 0:n], start=(c == 0), stop=(c == nchunk - 1))
                return ins
            pe(mm2, [t_sqb, t_const], [tss])
            rstd_from_ss(pss, tss, n, dim, Rb, t_Rb)
            for c, (pb, tp) in enumerate(banks):
                dve(lambda h, pb=pb, c=c: h.tensor_tensor(out=dst[:, c, 0:n], in0=pb[:, 0:n], in1=Rb[:, 0:n], op=ALU.mult), [tp, t_Rb], [t_dst])

        def load_rope(t0, n):
            P.dma("sp", rC[0:96, 0:n], ropeC[:, t0:t0 + n], [], [t_rC], t_rC)
            P.dma("sp", rS[0:96, 0:n], ropeS[:, t0:t0 + n], [], [t_rS], t_rS)

        def stageK(bi, t0, n):
            down_norm(384, 2, 256, kvdn, t_kvdn, bi, t0, n)
            load_rope(t0, n)
            for hd in range(8):
                pb, tp = next_ps(0, 4)

                def mm(h, pb=pb, hd=hd):
                    ins = None
                    for c in range(2):
                        ins = h.matmul(pb[0:64, 0:n], wkv[:, c, hd * 128:hd * 128 + 64], kvdn[:, c, 0:n], start=(c == 0), stop=(c == 1))
                    return ins
                pe(mm, [t_wkv, t_kvdn], [tp])
                if hd % 2 == 0:
                    act(lambda h, pb=pb, hd=hd: h.activation(out=KT[0:64, hd, t0:t0 + n], in_=pb[0:64, 0:n], func=AF.Copy), [tp], [t_KT[bi]])
                else:
                    dve(lambda h, pb=pb, hd=hd: h.tensor_copy(out=KT[0:64, hd, t0:t0 + n], in_=pb[0:64, 0:n]), [tp], [t_KT[bi]])
            pa, tpa = next_ps(0, 4)
            pb2, tpb2 = next_ps(0, 4)

            def mmr(h, pa=pa, pb2=pb2):
                ins = None
                for k in range(8):
                    ins = h.matmul(pa[0:96, 0:n], wA[:, k, 576:672], xnT[:, k, t0:t0 + n], start=(k == 0), stop=(k == 7))
                for k in range(8):
                    ins = h.matmul(pb2[0:96, 0:n], wAr[:, k, 0:96], xnT[:, k, t0:t0 + n], start=(k == 0), stop=(k == 7))
                return ins
            pe(mmr, [t_wA, t_wAr, t_xn[bi]], [tpa, tpb2])
            dve(lambda h, pa=pa: h.tensor_tensor(out=t1b[64:96, 0:n], in0=pa[64:96, 0:n], in1=rC[64:96, 0:n], op=ALU.mult), [tpa, t_rC], [t_t1])
            dve(lambda h, pb2=pb2: h.tensor_tensor(out=t2b[64:96, 0:n], in0=pb2[64:96, 0:n], in1=rS[64:96, 0:n], op=ALU.mult), [tpb2, t_rS], [t_t2])
            dve(lambda h: h.tensor_tensor(out=t1b[64:96, 0:n], in0=t1b[64:96, 0:n], in1=t2b[64:96, 0:n], op=ALU.add), [t_t1, t_t2], [t_t1])
            for hd in range(8):
                if hd % 2 == 0:
                    act(lambda h, hd=hd: h.activation(out=KT[64:96, hd, t0:t0 + n], in_=t1b[64:96, 0:n], func=AF.Copy), [t_t1], [t_KT[bi]])
                else:
                    dve(lambda h, hd=hd: h.tensor_copy(out=KT[64:96, hd, t0:t0 + n], in_=t1b[64:96, 0:n]), [t_t1], [t_KT[bi]])
            for ti in range(n // 128):
                tt = (t0 + ti * 128) // 128
                pb, tp = next_ps(0, 4)

                def mmv(h, pb=pb, ti=ti):
                    ins = None
                    for hd in range(8):
                        for c in range(2):
                            ins = h.matmul(pb[:, hd * 64:(hd + 1) * 64], kvdn[:, c, ti * 128:(ti + 1) * 128], wkv[:, c, hd * 128 + 64:(hd + 1) * 128], start=(c == 0), stop=(c == 1))
                    return ins
                pe(mmv, [t_wkv, t_kvdn], [tp])
                act(lambda h, pb=pb, tt=tt: h.activation(out=VA[:, tt, :, 0:64], in_=pb[:, 0:512].rearrange("p (h c) -> p h c", h=8), func=AF.Copy),
                    [tp, t_vones], [t_VA[bi]])

        for bi, (t0, n) in enumerate(BLOCKS):
            stageK(bi, t0, n)

        def stageQproj(bi, t0, n):
            down_norm(0, 3, 384, qdn, t_qdn, bi, t0, n)
            load_rope(t0, n)
            QT, tQ = QTb[bi % 2], t_QT[bi % 2]
            for hd in range(8):
                pa, tpa = next_ps(0, 4)
                pb2, tpb2 = next_ps(0, 4)

                def mmq(h, pa=pa, pb2=pb2, hd=hd):
                    ins = None
                    for c in range(3):
                        ins = h.matmul(pa[0:96, 0:n], wq[:, c, hd * 96:(hd + 1) * 96], qdn[:, c, 0:n], start=(c == 0), stop=(c == 2))
                    for c in range(3):
                        ins = h.matmul(pb2[0:96, 0:n], wqr[:, c, hd * 96:(hd + 1) * 96], qdn[:, c, 0:n], start=(c == 0), stop=(c == 2))
                    return ins
                pe(mmq, [t_wq, t_wqr, t_qdn], [tpa, tpb2])
                act(lambda h, pa=pa, hd=hd, QT=QT: h.activation(out=QT[0:64, hd, 0:n], in_=pa[0:64, 0:n], func=AF.Copy), [tpa], [tQ])
                dve(lambda h, pa=pa: h.tensor_tensor(out=t1b[64:96, 0:n], in0=pa[64:96, 0:n], in1=rC[64:96, 0:n], op=ALU.mult), [tpa, t_rC], [t_t1])
                dve(lambda h, pb2=pb2: h.tensor_tensor(out=t2b[64:96, 0:n], in0=pb2[64:96, 0:n], in1=rS[64:96, 0:n], op=ALU.mult), [tpb2, t_rS], [t_t2])
                dve(lambda h, hd=hd, QT=QT: h.tensor_tensor(out=QT[64:96, hd, 0:n], in0=t1b[64:96, 0:n], in1=t2b[64:96, 0:n], op=ALU.add), [t_t1, t_t2], [tQ])

        def stageAttn(bi, t0, n):
            ctxq = t0 >= SEQ
            ktiles = [16, 17] if ctxq else list(range(18))
            kdeps = ([t_KT[4], t_VA[4]] if ctxq else t_KT + t_VA)
            QT, tQ = QTb[bi % 2], t_QT[bi % 2]
            items = [(hd, ki, kt) for hd in range(8) for ki, kt in enumerate(ktiles)]
            nk = len(ktiles)
            pos, pts = {}, {}
            LOOK = 4

            def issue_S(idx):
                hd, ki, kt = items[idx]
                psS, tps = next_ps(0, 6)
                pe(lambda h: h.matmul(psS[:, 0:n], KT[0:96, hd, kt * 128:(kt + 1) * 128], QT[0:96, hd, 0:n], start=True, stop=True), kdeps + [tQ], [tps])
                pt, tpt = PT[idx % NPT], t_PT[idx % NPT]
                act(lambda h: h.activation(out=pt[:, 0:n], in_=psS[:, 0:n], func=AF.Exp, scale=MLA_SCALE), [tps], [tpt])
                pts[idx] = (pt, tpt)

            def issue_PV(idx):
                hd, ki, kt = items[idx]
                if ki == 0:
                    pos[hd] = next_ps(6, 8)
                po, tpo = pos[hd]
                pt, tpt = pts.pop(idx)
                pe(lambda h: h.matmul(po[:, 0:n], VA[:, kt, hd, :], pt[:, 0:n], start=(ki == 0), stop=(ki == nk - 1)), kdeps + [tpt, t_vones], [tpo])
                if ki == nk - 1:
                    dve(lambda h: h.reciprocal(out=bcs[0:64, 0:n], in_=po[64:128, 0:n]), [tpo], [t_bcs])
                    p0 = (hd % 2) * 64
                    dve(lambda h: h.tensor_tensor(out=yaT[p0:p0 + 64, hd // 2, t0:t0 + n], in0=po[0:64, 0:n], in1=bcs[0:64, 0:n], op=ALU.mult),
                        [tpo, t_bcs], [t_ya[bi]])
            for idx in range(len(items) + LOOK):
                if idx < len(items):
                    issue_S(idx)
                if idx >= LOOK:
                    issue_PV(idx - LOOK)

        stageQproj(0, *BLOCKS[0])
        for bi, (t0, n) in enumerate(BLOCKS):
            if bi + 1 < len(BLOCKS):
                stageQproj(bi + 1, *BLOCKS[bi + 1])
            stageAttn(bi, t0, n)

    def na_phase(l, e):
        cv = Carver(BIG, BIGN)
        kT = ybT
        VN = cv.take(18 * 8 * 128).rearrange("p (t h c) -> p t h c", t=18, h=8)
        tab = cv.take(_NU * 512).rearrange("p (u c) -> p u c", u=_NU)
        qTb = [cv.take(4096).rearrange("p (j a t) -> p j a t", j=4, a=2) for _ in range(2)]
        NPT = 4
        PT = [Fb[6][:, :].bitcast(BF16)[:, 0:512], Fb[3][:, :].bitcast(BF16)[:, 0:512], Fb[1][:, :].bitcast(BF16)[:, 0:512], Fb[0][:, :].bitcast(BF16)[:, 0:512]]
        identb = PW[:, 13000:13128]
        bcs, t_bcs, rsb, t_rs = Fb[2], tF[2], Fb[3], tF[3]
        t_k, t_v = P.toks("nk", len(BLOCKS)), P.toks("nv", len(BLOCKS))
        t_q = P.toks("nq", 2)
        t_tab = ptok("tab")
        t_PT = P.toks("nPT", NPT)
        t_ident = ptok("n_ident")
        t_vones = P.tok("nvones")
        wC = PW[:, 0:12288].rearrange("p (k c) -> p k c", k=8)
        load_w(tab, na_tab[l], t_tab)
        for u in range(_NU):
            nb, tnb = next_f(4, 6)
            P.dma("sp", nb[:, 0:512], na_neg[:, u, :], [], [tnb], tnb)
            dve(lambda h, u=u, nb=nb: h.scalar_tensor_tensor(out=tab[:, u, :], in0=tab[:, u, :], scalar=1.0 / NA_SCALE, in1=nb[:, 0:512], op0=ALU.mult, op1=ALU.add), [t_tab, tnb], [t_tab])
        dve(lambda h: h.memset(VN[:, :, :, 64:128], 1.0), [], [t_vones])
        load_w(identb, ident_in, t_ident)
        for i in range(2):
            dve(lambda h, i=i: h.memset(qTb[i][:], 0.0), [], [t_q[i]])

        def proj_fm(which, dst_fn, tdst, bi, t0, n):
            for j in range(4):
                pb, tp = next_ps(0, 4)

                def mm(h, pb=pb, j=j):
                    ins = None
                    for k in range(8):
                        ins = h.matmul(pb[:, 0:n], wC[:, k, which * 512 + j * 128:which * 512 + (j + 1) * 128], xnT[:, k, t0:t0 + n], start=(k == 0), stop=(k == 7))
                    return ins
                pe(mm, [t_PW, t_xn[bi]], [tp])
                if j % 2 == 0:
                    act(lambda h, pb=pb, j=j: h.activation(out=dst_fn(j), in_=pb[:, 0:n], func=AF.Copy), [tp], [tdst])
                else:
                    dve(lambda h, pb=pb, j=j: h.tensor_copy(out=dst_fn(j), in_=pb[:, 0:n]), [tp], [tdst])

        def stageK(bi, t0, n):
            proj_fm(1, lambda j, t0=t0, n=n: kT[:, j, t0:t0 + n], t_k[bi], bi, t0, n)
            for ti in range(n // 128):
                tt = (t0 + ti * 128) // 128
                pb, tp = next_ps(0, 4)

                def mmv(h, pb=pb, ti=ti):
                    ins = None
                    for k in range(8):
                        ins = h.matmul(pb[:, 0:512], xnT[:, k, t0 + ti * 128:t0 + (ti + 1) * 128], wC[:, k, 1024:1536], start=(k == 0), stop=(k == 7))
                    return ins
                pe(mmv, [t_PW, t_xn[bi]], [tp])
                act(lambda h, pb=pb, tt=tt: h.activation(out=VN[:, tt, :, 0:64], in_=pb[:, 0:512].rearrange("p (h c) -> p h c", h=8), func=AF.Copy),
                    [tp, t_vones], [t_v[bi]])
        for bi, (t0, n) in enumerate(BLOCKS):
            stageK(bi, t0, n)
        kvdeps = t_k + t_v

        def stageQ(bi, t0, n):
            qT, tq = qTb[bi % 2], t_q[bi % 2]
            for j in range(4):
                pb, tp = next_ps(0, 4)

                def mmq(h, pb=pb, j=j):
                    ins = None
                    for k in range(8):
                        ins = h.matmul(pb[:, 0:n], wC[:, k, j * 128:(j + 1) * 128], xnT[:, k, t0:t0 + n], start=(k == 0), stop=(k == 7))
                    return ins
                pe(mmq, [t_PW, t_xn[bi]], [tp])
                act(lambda h, pb=pb, j=j: h.activation(out=qT[0:64, j, 0, 0:n], in_=pb[0:64, 0:n], func=AF.Copy), [tp], [tq])
                dve(lambda h, pb=pb, j=j: h.tensor_copy(out=qT[64:128, j, 1, 0:n], in_=pb[64:128, 0:n]), [tp], [tq])
            items = []
            for rr in range(n // 64):
                r = t0 // 64 + rr
                if t0 < SEQ:
                    tiles = list(_PLAN[r]) + [(16, None), (17, None)]
                else:
                    tiles = [(16, None), (17, None)]
                for ki, (tt, u) in enumerate(tiles):
                    items.append((rr, ki, len(tiles), tt, u))
            pos, pts = {}, {}
            LOOK = 3

            def issue_S(idx):
                rr, ki, nt, tt, u = items[idx]
                psS, tps = next_ps(0, 6)

                def mms(h):
                    ins = None
                    for hd in range(8):
                        ins = h.matmul(psS[:, hd * 64:(hd + 1) * 64], kT[:, hd // 2, tt * 128:(tt + 1) * 128], qT[:, hd // 2, hd % 2, rr * 64:(rr + 1) * 64],
                                       start=(hd == 0), stop=(hd == 7 and u is None), skip_group_check=True)
                    if u is not None:
                        ins = h.matmul(psS[:, 0:512], identb, tab[:, u, :], start=False, stop=True, skip_group_check=True)
                    return ins
                pe(mms, kvdeps + [tq, t_tab, t_ident], [tps])
                pt, tpt = PT[idx % NPT], t_PT[idx % NPT]
                act(lambda h: h.activation(out=pt[:, 0:512], in_=psS[:, 0:512], func=AF.Exp, scale=NA_SCALE), [tps], [tpt])
                pts[idx] = (pt, tpt)

            def issue_PV(idx):
                rr, ki, nt, tt, u = items[idx]
                if ki == 0:
                    pos[rr] = next_ps(6, 8)
                po, tpo = pos[rr]
                pt, tpt = pts.pop(idx)

                def mmo(h):
                    ins = None
                    for hd in range(8):
                        ins = h.matmul(po[:, hd * 64:(hd + 1) * 64], VN[:, tt, hd, :], pt[:, hd * 64:(hd + 1) * 64], start=(ki == 0 and hd == 0), stop=(ki == nt - 1 and hd == 7), skip_group_check=True)
                    return ins
                pe(mmo, kvdeps + [tpt, t_vones], [tpo])
                if ki == nt - 1:
                    q0 = t0 + rr * 64
                    dve(lambda h: h.reciprocal(out=bcs[0:64, 0:512], in_=po[64:128, 0:512]), [tpo], [t_bcs])
                    for par in range(2):
                        def fin(h, par=par):
                            pov = po[0:64, 0:512].rearrange("p (j a c) -> p j a c", j=4, a=2)[:, :, par, :]
                            bcv = bcs[0:64, 0:512].rearrange("p (j a c) -> p j a c", j=4, a=2)[:, :, par, :]
                            return h.tensor_tensor(out=ycT[par * 64:(par + 1) * 64, :, q0:q0 + 64], in0=pov, in1=bcv, op=ALU.mult)
                        dve(fin, [tpo, t_bcs], [t_yc])
            for idx in range(len(items) + LOOK):
                if idx < len(items):
                    issue_S(idx)
                if idx >= LOOK:
                    issue_PV(idx - LOOK)

        for bi, (t0, n) in enumerate(BLOCKS):
            stageQ(bi, t0, n)

    def gla_phase(l, e):
        cv = Carver(BIG, BIGN)
        oacc = cv.take(4 * T, F32).rearrange("p (h t) -> p h t", h=4)
        lrT = cv.take(128)
        wg = [cv.take(256) for _ in range(2)]
        Ltok = cv.take(256)
        k_tok = cv.take(256, F32)
        qTt = cv.take(512, F32)
        kTt = cv.take(512, F32)
        EK = cv.take(512, F32)
        v_toks = [cv.take(512) for _ in range(2)]
        kends = [cv.take(256) for _ in range(2)]
        EQs = [cv.take(512, F32) for _ in range(2)]
        qdecs = [cv.take(512).rearrange("p (h t) -> p h t", h=4) for _ in range(2)]
        kinvs = [cv.take(512).rearrange("p (h t) -> p h t", h=4) for _ in range(2)]
        attms = [cv.take(512).rearrange("p (h t) -> p h t", h=4) for _ in range(2)]
        S = [cv.take(512, F32).rearrange("p (h c) -> p h c", h=4) for _ in range(2)]
        Sbf = cv.take(512).rearrange("p (h c) -> p h c", h=4)
        mk = [cv.take(128) for _ in range(4)]
        mkf = [cv.take(128, F32) for _ in range(2)]
        sqg = cv.take(512)
        et, Rg, EE, osum_, sg_ = Fb[0], Fb[1], Fb[2], Fb[3], Fb[4]
        t_et, t_Rg, t_EE, t_osum, t_sg = tF[0], tF[1], tF[2], tF[3], tF[4]
        osum = osum_[:, 0:512].rearrange("p (h t) -> p h t", h=4)
        sg = sg_[:, 0:512].rearrange("p (h t) -> p h t", h=4)
        tk = lambda nme: P.tok("g_" + nme)
        t_lr, t_wg, t_L, t_kt = tk("lr"), ptok("g_wg"), tk("L"), tk("kt")
        t_qT, t_kT, t_EK, t_attm = tk("qT"), tk("kT"), tk("EK"), P.toks("g_attm", 2)
        t_v, t_kend, t_EQ, t_qdec, t_kinv = P.toks("g_v", 2), P.toks("g_kend", 2), P.toks("g_EQ", 2), P.toks("g_qdec", 2), P.toks("g_kinv", 2)
        t_S = [P.toks(f"gS{d}_", 4) for d in range(2)]
        t_Sbf, t_mk, t_sqg, t_oacc = tk("Sbf"), ptok("g_mk"), tk("sqg"), tk("oacc")
        wB = PW[:, 0:12544].rearrange("p (k c) -> p k c", k=8)
        for d in range(2):
            load_w(wg[d][0:17, 0:256], gla_gate[l, d], t_wg)
        for i in range(4):
            load_w(mk[i][:, 0:128], gmask[i], t_mk)
        for i in range(2):
            P.dma("sp", mkf[i][:, 0:128], gmask[i], [], [t_mk], ptok("g_mkf%d" % i))
        dve(lambda h: h.memset(lrT[0:32, 0:128], 1.0), [], [t_lr])
        for d in range(2):
            dve(lambda h, d=d: h.memset(S[d][:], 0.0), [], t_S[d])

        def stage_a(tt, d, sl):
            tok0 = tt * 128
            bi = 4 if tt >= 16 else tt // 4
            xs = lambda k: xnT[:, k, tok0:tok0 + 128]
            v_tok, kend, EQ, qdec, kinv = v_toks[sl], kends[sl], EQs[sl], qdecs[sl], kinvs[sl]
            pb, tp = next_ps(0, 6)

            def mm_lr(h):
                ins = None
                for k in range(8):
                    ins = h.matmul(pb[0:16, 0:128], wB[:, k, 1536 + d * 16:1536 + (d + 1) * 16], xs(k), start=(k == 0), stop=(k == 7))
                return ins
            pe(mm_lr, [t_PW, t_xn[bi]], [tp])
            dve(lambda h: h.tensor_copy(out=lrT[0:16, 0:128], in_=pb[0:16, 0:128]), [tp], [t_lr])
            for which, dst, td in ((0, qTt, t_qT), (1, kTt, t_kT)):
                pq, tpq = next_ps(0, 6)

                def mmq(h, pq=pq, which=which):
                    ins = None
                    for j in range(2):
                        for k in range(8):
                            ins = h.matmul(pq[:, j * 128:(j + 1) * 128], wB[:, k, which * 256 + j * 128:which * 256 + (j + 1) * 128], xs(k), start=(k == 0), stop=(k == 7))
                    return ins
                pe(mmq, [t_PW, t_xn[bi]], [tpq])
                dst4 = dst[0:64, 0:512].rearrange("p (j a t) -> p j a t", j=2, a=2)
                act(lambda h, pq=pq, dst4=dst4: h.activation(out=dst4[:, :, 0, :], in_=pq[0:64, 0:256].rearrange("p (j t) -> p j t", j=2), func=AF.Copy), [tpq], [td])
                dve(lambda h, pq=pq, dst4=dst4: h.tensor_copy(out=dst4[:, :, 1, :], in_=pq[64:128, 0:256].rearrange("p (j t) -> p j t", j=2)), [tpq], [td])
            pz, tpz = next_ps(0, 6)
            pe(lambda h: h.matmul(pz[:, 0:256], lrT[0:17, 0:128], wg[d][0:17, 0:256], start=True, stop=True), [t_lr, t_wg], [tpz])
            act(lambda h: h.activation(out=et[:, 0:256], in_=pz[:, 0:256], func=AF.Exp, scale=-1.0), [tpz], [t_et])
            act(lambda h: h.activation(out=et[:, 0:256], in_=et[:, 0:256], func=AF.Ln, bias=one_col[:, 0:1], scale=1.0), [t_et, t_const], [t_et])
            dve(lambda h: h.tensor_scalar(out=Ltok[:, 0:256], in0=et[:, 0:256], scalar1=1.0 / 16.0, scalar2=None, op0=ALU.mult), [t_et], [t_L])
            pk, tpk = next_ps(0, 6)

            def mmk(h):
                ins = None
                for k in range(8):
                    ins = h.matmul(pk[:, 0:256], xs(k), wB[:, k, 256:512], start=(k == 0), stop=(k == 7))
                return ins
            pe(mmk, [t_PW, t_xn[bi]], [tpk])
            dve(lambda h: h.tensor_copy(out=k_tok[:, 0:256], in_=pk[:, 0:256]), [tpk], [t_kt])
            pv, tpv = next_ps(0, 6)

            def mmv(h):
                ins = None
                for k in range(8):
                    ins = h.matmul(pv[:, 0:512], xs(k), wB[:, k, 512:1024], start=(k == 0), stop=(k == 7))
                return ins
            pe(mmv, [t_PW, t_xn[bi]], [tpv])
            act(lambda h: h.activation(out=v_tok[:, 0:512], in_=pv[:, 0:512], func=AF.Copy), [tpv], [t_v[sl]])
            pc, tpc = next_ps(0, 6)

            def mmc(h):
                ins = None
                for hd in range(4):
                    ins = h.matmul(pc[0:64, hd * 128:(hd + 1) * 128], Ltok[:, hd * 64:(hd + 1) * 64], mk[d][:, 0:128], start=True, stop=True)
                return ins
            pe(mmc, [t_L, t_mk], [tpc])
            act(lambda h: h.activation(out=EQ[0:64, 0:512], in_=pc[0:64, 0:512], func=AF.Exp, scale=-1.0), [tpc], [t_EQ[sl]])
            act(lambda h: h.activation(out=EK[0:64, 0:512], in_=pc[0:64, 0:512], func=AF.Exp, scale=1.0), [tpc], [t_EK])
            pr, tpr = next_ps(0, 6)
            pe(lambda h: h.matmul(pr[:, 0:256], mk[2 + d][:, 0:128], Ltok[:, 0:256], start=True, stop=True), [t_L, t_mk], [tpr])
            act(lambda h: h.activation(out=EE[:, 0:256], in_=pr[:, 0:256], func=AF.Exp, scale=-1.0), [tpr], [t_EE])
            dve(lambda h: h.scalar_tensor_tensor(out=qdec[0:64].rearrange("p h t -> p (h t)"), in0=qTt[0:64, 0:512], scalar=0.125, in1=EQ[0:64, 0:512], op0=ALU.mult, op1=ALU.mult),
                [t_qT, t_EQ[sl]], [t_qdec[sl]])
            dve(lambda h: h.tensor_tensor(out=kinv[0:64].rearrange("p h t -> p (h t)"), in0=kTt[0:64, 0:512], in1=EK[0:64, 0:512], op=ALU.mult), [t_kT, t_EK], [t_kinv[sl]])
            dve(lambda h: h.tensor_tensor(out=kend[:, 0:256], in0=k_tok[:, 0:256], in1=EE[:, 0:256], op=ALU.mult), [t_kt, t_EE], [t_kend[sl]])

        def stage_att(d, sl):
            qdec, kinv = qdecs[sl], kinvs[sl]
            attm = attms[sl]
            pa, tpa = next_ps(0, 6)

            def mma(h):
                ins = None
                for hd in range(4):
                    ins = h.matmul(pa[:, hd * 128:(hd + 1) * 128], kinv[0:64, hd, :], qdec[0:64, hd, :], start=True, stop=True)
                return ins
            pe(mma, [t_kinv[sl], t_qdec[sl]], [tpa])
            for hd in range(4):
                dve(lambda h, hd=hd: h.tensor_tensor(out=attm[:, hd, :], in0=pa[:, hd * 128:(hd + 1) * 128], in1=mkf[d][:, 0:128], op=ALU.mult), [tpa, t_mk], [t_attm[sl]])

        def stage_b(tt, d, sl):
            tok0 = tt * 128
            bi = 4 if tt >= 16 else tt // 4
            v_tok, kend, EQ, qdec, kinv = v_toks[sl], kends[sl], EQs[sl], qdecs[sl], kinvs[sl]
            EQ3 = EQ.rearrange("p (h t) -> p h t", h=4)
            col = 127 if d == 0 else 0
            attm = attms[sl]
            po, tpo = next_ps(6, 8)
            dve(lambda h: h.tensor_copy(out=Sbf[0:64], in_=S[d][0:64]), t_S[d], [t_Sbf])

            def mmo(h):
                ins = None
                for hd in range(4):
                    reg = po[:, hd * 128:(hd + 1) * 128]
                    h.matmul(reg, v_tok[:, hd * 128:(hd + 1) * 128], attm[:, hd, :], start=True, stop=False)
                    ins = h.matmul(reg, Sbf[0:64, hd, :], qdec[0:64, hd, :], start=False, stop=True)
                return ins
            pe(mmo, [t_v[sl], t_attm[sl], t_Sbf, t_qdec[sl]], [tpo])
            pkv, tpkv = next_ps(0, 6)

            def mmkv(h):
                ins = None
                for hd in range(4):
                    ins = h.matmul(pkv[0:64, hd * 128:(hd + 1) * 128], kend[:, hd * 64:(hd + 1) * 64], v_tok[:, hd * 128:(hd + 1) * 128], start=True, stop=True)
                return ins
            pe(mmkv, [t_kend[sl], t_v[sl]], [tpkv])
            for hd in range(4):
                dve(lambda h, hd=hd: h.scalar_tensor_tensor(
                    out=S[d][0:64, hd, :], in0=S[d][0:64, hd, :], scalar=EQ3[0:64, hd, col:col + 1], in1=pkv[0:64, hd * 128:(hd + 1) * 128],
                    op0=ALU.mult, op1=ALU.add), [t_S[d][hd], t_EQ[sl], tpkv, t_Sbf], [t_S[d][hd]])
            if d == 0:
                act(lambda h: h.activation(out=oacc[:, :, tok0:tok0 + 128], in_=po[:, 0:512].rearrange("p (h t) -> p h t", h=4), func=AF.Copy), [tpo], [t_oacc])
                return
            dve(lambda h: h.tensor_tensor(out=osum[:], in0=po[:, 0:512].rearrange("p (h t) -> p h t", h=4), in1=oacc[:, :, tok0:tok0 + 128], op=ALU.add), [tpo, t_oacc], [t_osum])
            act(lambda h: h.activation(out=sqg[:, 0:512], in_=osum_[:, 0:512], func=AF.Square), [t_osum], [t_sqg])
            pss, tss = next_ps(0, 6)
            pe(lambda h: h.matmul(pss[:, 0:512], ones_bf[:], sqg[:, 0:512], start=True, stop=True), [t_sqg, t_const], [tss])
            rstd_from_ss(pss, tss, 512, 128, Rg, t_Rg)
            pg, tpg = next_ps(0, 6)

            def mmg(h):
                ins = None
                for hd in range(4):
                    for k in range(8):
                        ins = h.matmul(pg[:, hd * 128:(hd + 1) * 128], wB[:, k, 1024 + hd * 128:1024 + (hd + 1) * 128], xnT[:, k, tok0:tok0 + 128], start=(k == 0), stop=(k == 7))
                return ins
            pe(mmg, [t_PW, t_xn[bi]], [tpg])
            act(lambda h: h.activation(out=sg_[:, 0:512], in_=pg[:, 0:512], func=AF.Silu), [tpg], [t_sg])
            dve(lambda h: h.scalar_tensor_tensor(out=osum_[:, 0:512], in0=osum_[:, 0:512], scalar=gns[:, l:l + 1], in1=Rg[:, 0:512], op0=ALU.mult, op1=ALU.mult),
                [t_osum, t_Rg, t_const], [t_osum])
            dve(lambda h: h.tensor_tensor(out=ybT[:, :, tok0:tok0 + 128], in0=osum[:], in1=sg[:], op=ALU.mult), [t_osum, t_sg], [t_yb])

        order_f = [16, 17] + list(range(16))
        order_b = [17, 16] + list(range(15, -1, -1))
        seq = [(tt, 0) for tt in order_f] + [(tt, 1) for tt in order_b]
        stage_a(seq[0][0], seq[0][1], 0)
        for i, (tt, d) in enumerate(seq):
            stage_att(d, i % 2)
            if i + 1 < len(seq):
                stage_a(seq[i + 1][0], seq[i + 1][1], (i + 1) % 2)
            stage_b(tt, d, i % 2)

    def resid_update(e, bi, t0, n, oc, pb, tp, j):
        tf, ttf = next_f(3, 6)
        P.dma("sp", tf[:, 0:n], hcur[e][:, oc, t0:t0 + n], [t_h[e][bi]], [ttf], ttf)
        dve(lambda h: h.scalar_tensor_tensor(out=tf[:, 0:n], in0=pb[:, 0:n], scalar=Gc[:, oc, j:j + 1], in1=tf[:, 0:n], op0=ALU.mult, op1=ALU.add),
            [tp, ttf, t_coef], [ttf])
        P.dma("sp", h_scr[e][:, oc, t0:t0 + n], tf[:, 0:n], [ttf], [t_hs[e][bi]], ttf)

    def merge_phase(l, e):
        mT = BIG[:, 0:8 * T].rearrange("p (k t) -> p k t", k=8)
        macc, t_macc = Fb[0], tF[0]
        sgt, t_sgt = [Fb[1], Fb[2]], [tF[1], tF[2]]
        t_m = P.tok("mT")
        ys = ((yaT, w_a_o), (ybT, w_b_o), (ycT, w_c_o))
        def stage1(oc):
            buf, tb = next_ring()
            wg_ = buf[:, 0:3072].rearrange("p (x k c) -> p x k c", x=3, k=8)
            wo_ = buf[:, 3072:4608].rearrange("p (x k c) -> p x k c", x=3, k=4)
            for x in range(3):
                load_w(wg_[:, x], w_in_cols(l, G0 + x * 1024 + oc * 128, G0 + x * 1024 + (oc + 1) * 128), tb)
                load_w(wo_[:, x], ys[x][1][l].rearrange("(k p) c -> p k c", p=128)[:, :, oc * 128:(oc + 1) * 128], tb)
            for bi, (t0, n) in enumerate(BLOCKS):
                stage1b(oc, bi, t0, n, tb, wg_, wo_)

        def stage1b(oc, bi, t0, n, tb, wg_, wo_):
            if True:
                for x in range(3):
                    pg, tpg = next_ps(0, 4)
                    pp, tpp = next_ps(4, 8)

                    def mm(h, pg=pg, pp=pp, x=x):
                        ins = None
                        for k in range(8):
                            ins = h.matmul(pg[:, 0:n], wg_[:, x, k, :], xnT[:, k, t0:t0 + n], start=(k == 0), stop=(k == 7))
                        for k in range(4):
                            ins = h.matmul(pp[:, 0:n], wo_[:, x, k, :], ys[x][0][:, k, t0:t0 + n], start=(k == 0), stop=(k == 3))
                        return ins
                    pe(mm, [tb, t_xn[bi], t_ya[bi], t_yb, t_yc], [tpg, tpp])
                    sg_, tsg = sgt[x % 2], t_sgt[x % 2]
                    act(lambda h, pg=pg, sg_=sg_: h.activation(out=sg_[:, 0:n], in_=pg[:, 0:n], func=AF.Sigmoid), [tpg], [tsg])
                    if x == 0:
                        dve(lambda h, pp=pp, sg_=sg_: h.tensor_tensor(out=macc[:, 0:n], in0=pp[:, 0:n], in1=sg_[:, 0:n], op=ALU.mult), [tpp, tsg], [t_macc])
                    else:
                        dve(lambda h, pp=pp, sg_=sg_: h.tensor_tensor(out=sg_[:, 0:n], in0=pp[:, 0:n], in1=sg_[:, 0:n], op=ALU.mult), [tpp, tsg], [tsg])
                        if x == 1:
                            dve(lambda h, sg_=sg_: h.tensor_tensor(out=macc[:, 0:n], in0=macc[:, 0:n], in1=sg_[:, 0:n], op=ALU.add), [t_macc, tsg], [t_macc])
                        else:
                            dve(lambda h, sg_=sg_: h.tensor_tensor(out=mT[:, oc, t0:t0 + n], in0=macc[:, 0:n], in1=sg_[:, 0:n], op=ALU.add), [t_macc, tsg], [t_m])
        for oc in range(8):
            stage1(oc)

        def stage2(oc):
            buf, tb = next_ring()
            wo = buf[:, 0:1024].rearrange("p (k c) -> p k c", k=8)
            load_w(wo, w_out[l].rearrange("(k p) c -> p k c", p=128)[:, :, oc * 128:(oc + 1) * 128], tb)
            for bi, (t0, n) in enumerate(BLOCKS):
                stage2b(oc, bi, t0, n, tb, wo)

        def stage2b(oc, bi, t0, n, tb, wo):
            if True:
                pb, tp = next_ps()

                def mm(h, pb=pb, wo=wo):
                    ins = None
                    for k in range(8):
                        ins = h.matmul(pb[:, 0:n], wo[:, k, :], mT[:, k, t0:t0 + n], start=(k == 0), stop=(k == 7))
                    return ins
                pe(mm, [tb, t_m], [tp])
                resid_update(e, bi, t0, n, oc, pb, tp, 1 if t0 >= SEQ else 0)

        for oc in range(8):
            stage2(oc)

    def ffn_phase(l, e):
        NB_ = 15

        def ff(j):
            if j < NB_:
                return BIG[:, j * T:(j + 1) * T]
            return Y[:, (j - NB_) * T:(j - NB_ + 1) * T]
        sgt, t_sgt = [Fb[0], Fb[1]], [tF[0], tF[1]]
        t_ff = P.tok("ff")
        def stage1(j2):
            buf, tb = next_ring()
            wv = buf[:, 0:4096].rearrange("p (x k c) -> p x k c", x=2, k=8)
            load_w(wv[:, 0], w_ffn_in[l].rearrange("(k p) c -> p k c", p=128)[:, :, j2 * 256:(j2 + 1) * 256], tb)
            load_w(wv[:, 1], w_ffn_in[l].rearrange("(k p) c -> p k c", p=128)[:, :, FFH + j2 * 256:FFH + (j2 + 1) * 256], tb)
            for jj in range(2):
                j = j2 * 2 + jj
                for bi, (t0, n) in enumerate(BLOCKS):
                    stage1b(j, jj, bi, t0, n, tb, wv)

        def stage1b(j, jj, bi, t0, n, tb, wv):
            if True:
                if True:
                    pg, tpg = next_ps(0, 4)
                    pu, tpu = next_ps(4, 8)

                    def mm(h, pg=pg, pu=pu, jj=jj, wv=wv):
                        ins = None
                        for k in range(8):
                            ins = h.matmul(pg[:, 0:n], wv[:, 0, k, jj * 128:(jj + 1) * 128], xnT[:, k, t0:t0 + n], start=(k == 0), stop=(k == 7))
                        for k in range(8):
                            ins = h.matmul(pu[:, 0:n], wv[:, 1, k, jj * 128:(jj + 1) * 128], xnT[:, k, t0:t0 + n], start=(k == 0), stop=(k == 7))
                        return ins
                    pe(mm, [tb, t_xn[bi]], [tpg, tpu])
                    sg_, tsg = sgt[j % 2], t_sgt[j % 2]
                    act(lambda h, pg=pg, sg_=sg_: h.activation(out=sg_[:, 0:n], in_=pg[:, 0:n], func=AF.Silu), [tpg], [tsg])
                    dve(lambda h, pu=pu, sg_=sg_, j=j: h.tensor_tensor(out=ff(j)[:, t0:t0 + n], in0=pu[:, 0:n], in1=sg_[:, 0:n], op=ALU.mult), [tpu, tsg], [t_ff])
        for j2 in range(11):
            stage1(j2)

        def stage2(oc):
            buf, tb = next_ring()
            wo = buf[:, 0:22 * 128].rearrange("p (j c) -> p j c", j=22)
            load_w(wo, w_ffn_out[l].rearrange("(j p) c -> p j c", p=128)[:, :, oc * 128:(oc + 1) * 128], tb)
            for bi, (t0, n) in enumerate(BLOCKS):
                stage2b(oc, bi, t0, n, tb, wo)

        def stage2b(oc, bi, t0, n, tb, wo):
            if True:
                pb, tp = next_ps()

                def mm(h, pb=pb, wo=wo):
                    ins = None
                    for j in range(22):
                        ins = h.matmul(pb[:, 0:n], wo[:, j, :], ff(j)[:, t0:t0 + n], start=(j == 0), stop=(j == 21))
                    return ins
                pe(mm, [tb, t_ff], [tp])
                resid_update(e, bi, t0, n, oc, pb, tp, 1 if t0 >= SEQ else 0)

        for oc in range(8):
            stage2(oc)

    def tap(name, src_ap, toks):
        if name in tap_out:
            P.dma("sp", tap_out[name], src_ap, list(toks), [t_out], P.tok("tap"))

    t_out = P.tok("out")
    t_h0 = [P.toks(f"h0_{e}_", len(BLOCKS)) for e in range(2)]
    t_hs = [P.toks(f"hs_{e}_", len(BLOCKS)) for e in range(2)]
    hcur = [h0[0], h0[1]]
    t_h = [t_h0[0], t_h0[1]]
    P.phase = 'prologue'
    prologue()
    P.barrier()
    tap("modT", modT[:].rearrange("p l c e -> p (l c e)"), [t_modT])
    done = False
    for e in range(n_elems):
        if done:
            break
        for l in range(n_layers):
            P.phase = f'norm1_{e}_{l}'
            mla_weights(l)
            coef_cols(l, e, 0)
            norm_phase(e)
            mla_weights_prep(l)
            P.barrier()
            if stop_after == "norm1":
                done = True
                break
            P.phase = f'mla_{e}_{l}'
            mla_phase(l, e)
            na_weights(l)
            P.barrier()
            if stop_after == "mla":
                done = True
                break
            P.phase = f'na_{e}_{l}'
            na_phase(l, e)
            gla_weights(l)
            P.barrier()
            if stop_after == "na":
                done = True
                break
            P.phase = f'gla_{e}_{l}'
            gla_phase(l, e)
            P.barrier()
            if stop_after == "gla":
                done = True
                break
            P.phase = f'merge_{e}_{l}'
            merge_phase(l, e)
            hcur[e] = h_scr[e]
            t_h[e] = t_hs[e]
            P.barrier()
            if stop_after == "merge":
                done = True
                break
            P.phase = f'norm2_{e}_{l}'
            coef_cols(l, e, 1)
            norm_phase(e)
            P.barrier()
            P.phase = f'ffn_{e}_{l}'
            ffn_phase(l, e)
            P.barrier()
        if not done:
            P.phase = f'final_{e}'
            norm_phase(e, final=True)
            P.barrier()
    tap("Y", Y[:], t_ya + [t_yb, t_yc])
    tap("xnT", xnT[:], t_xn)
    tap("hs", h_scr[0], t_hs[0])
    P.add("sp", None, [], [t_out] + t_hs[0] + t_hs[1])
    P.emit()
    nc._prog = P
    return nc, stack


def _col(v, n):
    return np.ascontiguousarray(v.reshape(n, 128).T)


def prepare_shared(inp):
    L = DEPTH
    sh = {}
    for k in ("w_mod", "w_in", "w_a_o", "w_b_o", "w_c_o", "w_out", "w_ffn_in", "w_ffn_out"):
        sh[k] = np.ascontiguousarray(inp[k], dtype=np.float32)
    sh["w_q_up"] = np.ascontiguousarray(inp["mla_w_q_up"], dtype=np.float32)
    sh["w_kv_up"] = np.ascontiguousarray(inp["mla_w_kv_up"], dtype=np.float32)
    sh["b_modT"] = np.ascontiguousarray(np.stack([_col(inp["b_mod"][l], 48) for l in range(L)], 1))
    sh["n1T"] = np.ascontiguousarray(np.stack([_col(inp["norm1_w"][l], 8) for l in range(L)], 1))
    sh["n2T"] = np.ascontiguousarray(np.stack([_col(inp["norm2_w"][l], 8) for l in range(L)], 1))
    sh["fnT"] = _col(inp["final_norm_w"], 8)
    qn = np.stack([_col(inp["mla_q_norm_w"][l], 3) for l in range(L)], 1)
    sh["qnT"] = np.ascontiguousarray(qn)
    kvn = np.stack([_col(inp["mla_kv_norm_w"][l], 2) for l in range(L)], 1)
    sh["kvnT"] = np.ascontiguousarray(kvn)
    sh["gnT"] = np.ascontiguousarray(inp["gla_norm_w"].T)
    gg = np.zeros((L, 2, 17, 256), np.float32)
    gg[:, 0, :16] = inp["gla_w_gate_f"]
    gg[:, 0, 16] = inp["gla_b_gate_f"]
    gg[:, 1, :16] = inp["gla_w_gate_b"]
    gg[:, 1, 16] = inp["gla_b_gate_b"]
    sh["gla_gate"] = gg
    ia, ib, neg = build_na_gather_index(_PATS)
    hidx = np.arange(8)[None, None, :, None]
    tabs = []
    for l in range(L):
        g = inp["na_rpb"][l][hidx, ia, ib]
        tabs.append(np.ascontiguousarray(g.reshape(_NU, 128, 512).transpose(1, 0, 2)))
    sh["na_tab"] = np.ascontiguousarray(np.stack(tabs, 0), dtype=np.float32)
    sh["na_neg"] = np.ascontiguousarray(neg.reshape(_NU, 128, 512).transpose(1, 0, 2)) * np.float32(1.0 / NA_SCALE)
    C, S = rope_tables()
    sh["ropeC"], sh["ropeS"] = C, S
    sh["gmask"] = gla_masks()
    sh["ident"] = np.eye(128, dtype=np.float32)
    return sh


def prepare_core(inp, core):
    b0 = 2 * core
    h0 = np.empty((2, 128, 8, T), np.float32)
    cT = np.empty((128, 8, 3), np.float32)
    for e in range(2):
        full = np.concatenate([inp["x"][b0 + e], inp["ctx"][b0 + e]], 0)
        h0[e] = full.T.reshape(8, 128, T).transpose(1, 0, 2)
        cT[:, :, e] = _col(inp["c"][b0 + e], 8)
    cT[:, :, 2] = _col(inp["c_ctx"], 8)
    return {"h0": h0, "cT": cT}


_CACHE = {}


def kernel(**inputs):
    inp = {k: np.asarray(v) for k, v in inputs.items()}
    if "prog" not in _CACHE:
        _CACHE["prog"] = build_program()
    nc, _ = _CACHE["prog"]
    sh = prepare_shared(inp)
    in_maps = []
    for core in range(8):
        m = dict(sh)
        m.update(prepare_core(inp, core))
        in_maps.append(m)
    res = run_bass_kernel_spmd(nc, in_maps, core_ids=list(range(8)))
    out = np.empty((16, SEQ, D), np.float32)
    for core in range(8):
        o = res.results[core]["outT"]
        for e in range(2):
            out[2 * core + e] = o[e].transpose(2, 1, 0).reshape(SEQ, D)
    return out
```

```python
import numpy as np
from contextlib import ExitStack
import concourse.bass as bass
import concourse.mybir as mybir
from concourse.bass_utils import run_bass_kernel_spmd

F32 = mybir.dt.float32
BF16 = mybir.dt.bfloat16
AF = mybir.ActivationFunctionType
ALU = mybir.AluOpType

D = 1024
DEPTH = 4
SEQ = 2048
CTX = 256
T = SEQ + CTX
GRID_W = 64
EPS = 1e-6
IN_W = 6848
A0, B0, C0, G0 = 0, 672, 2240, 3776
FFH = 2816
MLA_SCALE = 96 ** -0.5
NA_SCALE = 0.125
BLOCKS = [(0, 512), (512, 512), (1024, 512), (1536, 512), (2048, 256)]
NEG = -30000.0
SEM_CAP = 30000
SAME_ENG_SYNC = True


class Tok:
    __slots__ = ("name", "w", "r", "dsem", "dcnt")

    def __init__(self, name):
        self.name = name
        self.w = None
        self.r = {}
        self.dsem = None
        self.dcnt = 0


class _CountProxy:
    def __init__(self, h, prog):
        self._h, self._p = h, prog

    def matmul(self, *a, **k):
        c = self._p.mm_counts
        c[self._p._cur_phase] = c.get(self._p._cur_phase, 0) + 1
        return self._h.matmul(*a, **k)

    def __getattr__(self, name):
        return getattr(self._h, name)


INSTRUMENT = False


class Op:
    __slots__ = ("eng", "fn", "deps", "ev", "signal", "signo", "waits", "dma", "phase", "raw")

    def __init__(self, eng, fn):
        self.eng = eng
        self.fn = fn
        self.deps = set()
        self.raw = set()
        self.ev = None
        self.signal = False
        self.signo = 0
        self.waits = []
        self.dma = None


class Prog:
    ENGS = ["pe", "act", "dve", "pool", "sp"]

    def __init__(self, nc, stack):
        self.nc = nc
        self.stack = stack
        self.ops = {e: [] for e in self.ENGS}
        self.anchors = []
        self.active_anchors = set()
        self.phase = "init"
        self.mm_counts = {}

    def tok(self, name):
        return Tok(name)

    def toks(self, name, n):
        return [Tok(f"{name}{i}") for i in range(n)]

    def _deps(self, op, reads, writes, anchor=None):
        for t in reads:
            if t.w is not None:
                op.deps.add(t.w)
                op.raw.add(t.w)
        for t in writes:
            if t.w is not None and not (anchor is not None and t.w[0] == "d" and t.w[1] is anchor):
                op.deps.add(t.w)
            for k, v in t.r.items():
                op.deps.add((k[0], k[1], v))

    def _commit(self, ev, reads, writes):
        for t in reads:
            key = (ev[0], ev[1])
            if t.r.get(key, 0) < ev[2]:
                t.r[key] = ev[2]
        for t in writes:
            t.w = ev
            t.r = {}

    def add(self, eng, fn, reads=(), writes=()):
        op = Op(eng, fn)
        op.phase = self.phase
        self._deps(op, reads, writes)
        self.ops[eng].append(op)
        ev = ("c", eng, len(self.ops[eng]))
        op.ev = ev
        self._commit(ev, reads, writes)
        return op

    def dma(self, q, out, in_, reads, writes, anchor, exempt=False):
        op = Op(q, None)
        op.dma = (out, in_, anchor)
        self._deps(op, reads, writes, anchor)
        if anchor.dsem is None:
            anchor.dsem = self.stack.enter_context(self.nc.semaphore(f"d{len(self.anchors)}"))
            self.anchors.append(anchor)
        anchor.dcnt += 16
        if not exempt:
            self.active_anchors.add(anchor)
        self.ops[q].append(op)
        ev = ("d", anchor, anchor.dcnt)
        op.ev = ev
        self._commit(ev, reads, writes)
        return op

    def barrier(self):
        evs = []
        for e in self.ENGS:
            if self.ops[e]:
                last = None
                for o in reversed(self.ops[e]):
                    if o.ev[0] == "c" and o.fn is not None:
                        last = o
                        break
                if last is not None:
                    evs.append(last.ev)
        devs = [("d", a, a.dcnt) for a in self.active_anchors]
        self.active_anchors = set()
        for e in self.ENGS:
            op = Op(e, None)
            for ev in evs:
                if ev[1] != e:
                    op.deps.add(ev)
            for ev in devs:
                op.deps.add(ev)
            self.ops[e].append(op)
            op.ev = ("c", e, len(self.ops[e]))

    def finalize(self):
        for e in self.ENGS:
            waited = {}
            for op in self.ops[e]:
                best = {}
                for ev in op.deps:
                    if ev[0] == "c":
                        if ev[1] == e and (e == "pe" or not SAME_ENG_SYNC):
                            continue
                        if ev[1] == e and ev not in op.raw:
                            continue
                        if ev[1] == e and ev[2] >= op.ev[2]:
                            continue
                    key = (ev[0], ev[1])
                    if ev[2] > best.get(key, 0):
                        best[key] = ev[2]
                for key, v in best.items():
                    if waited.get(key, 0) >= v:
                        continue
                    waited[key] = v
                    if key[0] == "c":
                        src = self.ops[key[1]][v - 1]
                        idx = v - 1
                        while src.fn is None and src.dma is None:
                            idx -= 1
                            if idx < 0:
                                src = None
                                break
                            src = self.ops[key[1]][idx]
                        if src is None:
                            continue
                        if src.dma is not None:
                            op.waits.append(("d", src.dma[2], src.ev[2]))
                            continue
                        src.signal = True
                        op.waits.append(("c", key[1], src))
                    else:
                        op.waits.append(("d", key[1], v))
        self.csems = {}
        for e in self.ENGS:
            n = 0
            for op in self.ops[e]:
                if op.signal:
                    n += 1
                    op.signo = n
            nsem = (n + SEM_CAP - 1) // SEM_CAP
            self.csems[e] = [self.stack.enter_context(self.nc.semaphore(f"c_{e}{i}")) for i in range(max(nsem, 1))]

    def _semval(self, e, signo):
        return self.csems[e][(signo - 1) // SEM_CAP], (signo - 1) % SEM_CAP + 1

    def emit_engine(self, e, h):
        if INSTRUMENT and e == "pe":
            h = _CountProxy(h, self)
        for op in self.ops[e]:
            self._cur_phase = getattr(op, "phase", None)
            for w in op.waits:
                if w[0] == "c":
                    sem, val = self._semval(w[1], w[2].signo)
                    h.wait_ge(sem, val)
                else:
                    h.wait_ge(w[1].dsem, w[2])
            if op.dma is not None:
                out, in_, anchor = op.dma
                h.dma_start(out=out, in_=in_).then_inc(anchor.dsem, 16)
            elif op.fn is not None:
                ins = op.fn(h)
                if op.signal:
                    sem, _ = self._semval(e, op.signo)
                    ins.then_inc(sem, 1)

    def emit(self):
        self.finalize()
        nc = self.nc
        with nc.Block() as block:
            @block.tensor
            def _(h):
                self.emit_engine("pe", h)

            @block.scalar
            def _(h):
                self.emit_engine("act", h)

            @block.vector
            def _(h):
                self.emit_engine("dve", h)

            @block.gpsimd
            def _(h):
                self.emit_engine("pool", h)

            @block.sync
            def _(h):
                self.emit_engine("sp", h)


def na_tables():
    rows = SEQ // GRID_W
    pats = {}
    plan = {}
    for r in range(rows):
        r0 = min(max(r - 4, 0), rows - 8)
        tiles = list(range(r0 // 2, (r0 + 7) // 2 + 1))
        lst = []
        for tt in tiles:
            j0 = 2 * tt
            key = (j0 - r, r0 <= j0 < r0 + 8, r0 <= j0 + 1 < r0 + 8)
            if key not in pats:
                pats[key] = len(pats)
            lst.append((tt, pats[key]))
        plan[r] = lst
    return pats, plan


def build_na_gather_index(pats):
    NU = len(pats)
    ia = np.zeros((NU, 128, 8, 64), np.int64)
    ib = np.zeros((NU, 128, 8, 64), np.int64)
    neg = np.zeros((NU, 128, 8, 64), np.float32)
    col = np.arange(64)
    cs = np.clip(col - 8, 0, 48)
    for (dr0, v0, v1), u in pats.items():
        for jj in range(2):
            valid = (v0, v1)[jj]
            dr = dr0 + jj
            for kc in range(64):
                p = jj * 64 + kc
                for qc in range(64):
                    ok = valid and (kc >= cs[qc]) and (kc < cs[qc] + 16) and (-7 <= dr <= 7)
                    a = min(max(dr + 7, 0), 14)
                    b = min(max(kc - qc + 15, 0), 30)
                    ia[u, p, :, qc] = a
                    ib[u, p, :, qc] = b
                    if not ok:
                        neg[u, p, :, qc] = NEG
    return ia, ib, neg


def rope_tables():
    t = np.arange(SEQ)
    rows = (t // GRID_W).astype(np.float32)
    cols = (t % GRID_W).astype(np.float32)
    inv = (10000.0 ** (-np.arange(8, dtype=np.float32) / 8)).astype(np.float32)
    ang = np.concatenate([rows[:, None] * inv, cols[:, None] * inv], -1)
    cos, sin = np.cos(ang).astype(np.float32), np.sin(ang).astype(np.float32)
    C = np.ones((96, T), np.float32)
    S = np.zeros((96, T), np.float32)
    C[64:80, :SEQ] = cos.T
    C[80:96, :SEQ] = cos.T
    S[64:80, :SEQ] = sin.T
    S[80:96, :SEQ] = sin.T
    return C, S


def gla_masks():
    s = np.arange(128)[:, None]
    t = np.arange(128)[None, :]
    m = np.zeros((4, 128, 128), np.float32)
    m[0] = (s <= t)
    m[1] = (s >= t)
    m[2] = (s > t)
    m[3] = (s < t)
    return m


_PATS, _PLAN = na_tables()
_NU = len(_PATS)


PHASES = ("norm1", "mla", "na", "gla", "merge", "norm2", "ffn")


def build_program(n_layers=DEPTH, n_elems=2, taps=(), stop_after=None):
    nc = bass.Bass("TRN2", target_bir_lowering=False)
    stack = ExitStack()
    P = Prog(nc, stack)
    L = DEPTH

    def din(name, shape, dt=F32):
        return nc.dram_tensor(name, list(shape), dt, kind="ExternalInput").ap()

    h0 = din("h0", [2, 128, 8, T])
    cT = din("cT", [128, 8, 3])
    w_mod = din("w_mod", [L, D, 6 * D])
    b_modT = din("b_modT", [128, L, 48])
    n1T = din("n1T", [128, L, 8])
    n2T = din("n2T", [128, L, 8])
    fnT = din("fnT", [128, 8])
    w_in = din("w_in", [L, D, IN_W])
    qnT = din("qnT", [128, L, 3])
    kvnT = din("kvnT", [128, L, 2])
    w_q_up = din("w_q_up", [L, 384, 768])
    w_kv_up = din("w_kv_up", [L, 256, 1024])
    gla_gate = din("gla_gate", [L, 2, 17, 256])
    gnT = din("gnT", [128, L])
    na_tab = din("na_tab", [L, 128, _NU, 512])
    na_neg = din("na_neg", [128, _NU, 512])
    w_a_o = din("w_a_o", [L, 512, D])
    w_b_o = din("w_b_o", [L, 512, D])
    w_c_o = din("w_c_o", [L, 512, D])
    w_out = din("w_out", [L, D, D])
    w_ffn_in = din("w_ffn_in", [L, D, 2 * FFH])
    w_ffn_out = din("w_ffn_out", [L, FFH, D])
    ropeC = din("ropeC", [96, T])
    ropeS = din("ropeS", [96, T])
    gmask = din("gmask", [4, 128, 128])
    ident_in = din("ident", [128, 128])
    outT = nc.dram_tensor("outT", [2, 128, 8, SEQ], F32, kind="ExternalOutput").ap()
    h_scr = nc.dram_tensor("h_scr", [2, 128, 8, T], F32, kind="Internal").ap()
    tap_out = {}
    for name, shape, dt_ in taps:
        tap_out[name] = nc.dram_tensor("tap_" + name, list(shape), dt_, kind="ExternalOutput").ap()

    def sb(name, shape, dt=F32):
        return stack.enter_context(nc.sbuf_tensor(name, list(shape), dt))

    ps = [stack.enter_context(nc.psum_tensor(f"ps{i}", [128, 512], F32)) for i in range(8)]
    pst = P.toks("ps", 8)

    ones_bf = sb("ones_bf", [128, 128], BF16)
    ones_f = sb("ones_f", [128, 64], F32)
    eps_col = sb("eps_col", [128, 1])
    one_col = sb("one_col", [128, 1])
    modT = sb("modT", [128, L, 48, 3])
    t_modT = P.tok("modT")
    cact = sb("cact", [128, 8, 3], BF16)
    cin = sb("cin", [128, 8, 3])
    bmod = sb("bmod", [128, L, 48])
    n1s = sb("n1s", [128, L, 8])
    n2s = sb("n2s", [128, L, 8])
    fns = sb("fns", [128, 8])
    qns = sb("qns", [128, L, 3])
    kvns = sb("kvns", [128, L, 2])
    gns = sb("gns", [128, L])
    t_const = P.tok("const")
    Wc = sb("Wc", [128, 8, 2])
    Sc = sb("Sc", [128, 8, 2])
    Gc = sb("Gc", [128, 8, 2])
    t_coef = P.tok("coef")
    xnT = sb("xnT", [128, 8, T], BF16)
    t_xn = P.toks("xn", len(BLOCKS))
    Y = sb("Y", [128, 3 * 4 * T], BF16)
    yaT = Y[:, 0:4 * T].rearrange("p (j t) -> p j t", j=4)
    ybT = Y[:, 4 * T:8 * T].rearrange("p (j t) -> p j t", j=4)
    ycT = Y[:, 8 * T:12 * T].rearrange("p (j t) -> p j t", j=4)
    t_ya = P.toks("ya", len(BLOCKS))
    t_yb = P.tok("yb")
    t_yc = P.tok("yc")
    Fb = [sb(f"F{i}", [128, 512]) for i in range(7)]
    tF = P.toks("F", 7)
    Rb, t_Rb = Fb[6], tF[6]
    PWN = 13824
    PW = sb("PW", [128, PWN], BF16)
    t_PW = P.tok("PW")
    ring = [PW[:, i * 4608:(i + 1) * 4608] for i in range(3)]
    t_ring = P.toks("ring", 3)
    BIGN = 34816
    BIG = sb("BIG", [128, BIGN], BF16)
    t_big = P.tok("BIG")

    cnt = {"ring": 0, "ps": 0, "f": 0}
    _ptoks = {}

    def ptok(name):
        if name not in _ptoks:
            _ptoks[name] = P.tok(name)
        return _ptoks[name]

    def next_ring():
        i = cnt["ring"] % 3
        cnt["ring"] += 1
        return ring[i], t_ring[i]

    def ring_load(dst, src, tb):
        load_w(dst, src, tb, extra_writes=[t_PW, ptok("wA"), ptok("wq"), ptok("wkv"), ptok("wAr"), ptok("wqr")], exempt=True)

    def next_ps(lo=0, hi=8):
        n = hi - lo
        i = lo + cnt["ps"] % n
        cnt["ps"] += 1
        return ps[i], pst[i]

    def next_f(lo, hi):
        n = hi - lo
        i = lo + cnt["f"] % n
        cnt["f"] += 1
        return Fb[i], tF[i]

    def act(fn, reads, writes):
        return P.add("act", fn, reads, writes)

    def dve(fn, reads, writes):
        return P.add("dve", fn, reads, writes)

    def pe(fn, reads, writes):
        return P.add("pe", fn, reads, writes)

    class Carver:
        def __init__(self, base, limit):
            self.base, self.o, self.limit = base, 0, limit

        def take(self, nel, dt=BF16):
            n = nel if dt == BF16 else 2 * nel
            ap = self.base[:, self.o:self.o + n]
            self.o += n
            assert self.o <= self.limit, (self.o, self.limit)
            return ap if dt == BF16 else ap.bitcast(F32)

    def rstd_from_ss(ps_ss, t_ss, n, dim, Rout, t_R):
        act(lambda h: h.activation(out=Rout[:, 0:n], in_=ps_ss[:, 0:n], func=AF.Ln, scale=1.0 / dim, bias=eps_col[:, 0:1]),
            [t_ss, t_const], [t_R])
        act(lambda h: h.activation(out=Rout[:, 0:n], in_=Rout[:, 0:n], func=AF.Exp, scale=-0.5), [t_R], [t_R])

    def load_w(dst_ap, src_ap, tok, reads=(), extra_writes=(), exempt=False):
        P.dma("pool", dst_ap, src_ap, list(reads), [tok] + list(extra_writes), tok, exempt=exempt)

    def pw_users():
        return [t_PW, ptok("wA"), ptok("wq"), ptok("wkv"), ptok("wAr"), ptok("wqr")] + t_ring

    def w_in_cols(l, c0, c1):
        return w_in[l].rearrange("(k p) c -> p k c", p=128)[:, :, c0:c1]

    def prologue():
        dve(lambda h: h.memset(ones_bf[:], 1.0), [], [t_const])
        dve(lambda h: h.memset(ones_f[:], 1.0), [], [t_const])
        dve(lambda h: h.memset(eps_col[:], EPS), [], [t_const])
        dve(lambda h: h.memset(one_col[:], 1.0), [], [t_const])
        tl = P.tok("ld_small")
        for dst, src in ((cin, cT), (bmod, b_modT), (n1s, n1T), (n2s, n2T), (fns, fnT), (qns, qnT), (kvns, kvnT), (gns, gnT)):
            P.dma("sp", dst[:], src, [], [tl, t_const], P.tok("a"))
        act(lambda h: h.activation(out=cin[:], in_=cin[:], func=AF.Silu), [tl, t_const], [tl])
        dve(lambda h: h.tensor_copy(out=cact[:], in_=cin[:]), [tl], [t_const])
        for l in range(n_layers):
            for cch in range(12):
                buf, tb = next_ring()
                bv = buf[:, 0:4096].rearrange("p (k c) -> p k c", k=8)
                load_w(bv, w_mod[l].rearrange("(k p) c -> p k c", p=128)[:, :, cch * 512:(cch + 1) * 512], tb)
                pb, tp = next_ps()

                def mm(h, bv=bv, pb=pb):
                    ins = None
                    for j in range(4):
                        for k in range(8):
                            ins = h.matmul(pb[:, j * 4:j * 4 + 3], bv[:, k, j * 128:(j + 1) * 128], cact[:, k, :], start=(k == 0), stop=(k == 7))
                    return ins
                pe(mm, [tb, t_const], [tp])

                def ev(h, pb=pb, l=l, cch=cch):
                    ins = None
                    for j in range(4):
                        ins = h.tensor_scalar(out=modT[:, l, cch * 4 + j, :], in0=pb[:, j * 4:j * 4 + 3], scalar1=bmod[:, l, cch * 4 + j:cch * 4 + j + 1],
                                              scalar2=None, op0=ALU.add)
                    return ins
                dve(ev, [tp, t_const], [t_modT])

    def coef_cols(l, e, which):
        base = 3 * which
        nw = n1s if which == 0 else n2s
        for j, colm in ((0, e), (1, 2)):
            dve(lambda h, j=j, colm=colm: h.scalar_tensor_tensor(out=Wc[:, :, j], in0=modT[:, l, (base + 1) * 8:(base + 2) * 8, colm], scalar=1.0,
                                                               in1=nw[:, l, :], op0=ALU.add, op1=ALU.mult), [t_modT, t_const], [t_coef])
            dve(lambda h, j=j, colm=colm: h.tensor_copy(out=Sc[:, :, j], in_=modT[:, l, base * 8:(base + 1) * 8, colm]), [t_modT], [t_coef])
            dve(lambda h, j=j, colm=colm: h.tensor_copy(out=Gc[:, :, j], in_=modT[:, l, (base + 2) * 8:(base + 3) * 8, colm]), [t_modT], [t_coef])

    def norm_phase(e, final=False, skip_ctx=False):
        cv = Carver(BIG, BIGN)
        hb = [cv.take(4096, F32).rearrange("p (k t) -> p k t", k=8) for _ in range(2)]
        sqbs = [cv.take(4096).rearrange("p (k t) -> p k t", k=8) for _ in range(2)]
        Rbs = [cv.take(512, F32) for _ in range(2)]
        ob = cv.take(4096, F32).rearrange("p (k t) -> p k t", k=8) if final else None
        t_hb, t_sqbs, t_ob, t_Rbs = [ptok("hb0"), ptok("hb1")], P.toks("sqb", 2), ptok("ob"), P.toks("nRb", 2)
        for bi, (t0, n) in enumerate(BLOCKS):
            if (final or skip_ctx) and t0 >= SEQ:
                continue
            j = 1 if t0 >= SEQ else 0
            hbuf, thb = hb[bi % 2], t_hb[bi % 2]
            sqb, t_sqb = sqbs[bi % 2], t_sqbs[bi % 2]
            Rb, t_Rb = Rbs[bi % 2], t_Rbs[bi % 2]
            P.dma("sp", hbuf[:, :, 0:n], hcur[e][:, :, t0:t0 + n], [t_h[e][bi]], [thb], thb)
            act(lambda h, hbuf=hbuf, n=n, sqb=sqb: h.activation(out=sqb[:, :, 0:n], in_=hbuf[:, :, 0:n], func=AF.Square), [thb], [t_sqb])
            pb, tp = next_ps()

            def mm(h, pb=pb, n=n, sqb=sqb):
                ins = None
                for k in range(8):
                    ins = h.matmul(pb[:, 0:n], ones_bf[:], sqb[:, k, 0:n], start=(k == 0), stop=(k == 7))
                return ins
            pe(mm, [t_sqb, t_const], [tp])
            rstd_from_ss(pb, tp, n, D, Rb, t_Rb)
            if not final:
                for k in range(8):
                    tf, ttf = next_f(0, 3)
                    dve(lambda h, k=k, tf=tf, hbuf=hbuf, n=n, j=j, Rb=Rb: h.scalar_tensor_tensor(
                        out=tf[:, 0:n], in0=hbuf[:, k, 0:n], scalar=Wc[:, k, j:j + 1], in1=Rb[:, 0:n], op0=ALU.mult, op1=ALU.mult),
                        [thb, t_Rb, t_coef], [ttf])
                    act(lambda h, k=k, tf=tf, n=n, t0=t0, j=j: h.activation(
                        out=xnT[:, k, t0:t0 + n], in_=tf[:, 0:n], func=AF.Identity, bias=Sc[:, k, j:j + 1], scale=1.0),
                        [ttf, t_coef], [t_xn[bi]])
            else:
                for k in range(8):
                    dve(lambda h, k=k, hbuf=hbuf, n=n, Rb=Rb: h.scalar_tensor_tensor(
                        out=ob[:, k, 0:n], in0=hbuf[:, k, 0:n], scalar=fns[:, k:k + 1], in1=Rb[:, 0:n], op0=ALU.mult, op1=ALU.mult),
                        [thb, t_Rb, t_const], [t_ob])
                P.dma("sp", outT[e][:, :, t0:t0 + n], ob[:, :, 0:n], [t_ob], [t_out], t_ob)

    mla_w = {}

    def mla_weights(l):
        wA = PW[:, 0:5376].rearrange("p (k c) -> p k c", k=8)
        wAr = PW[:, 5376:6144].rearrange("p (k c) -> p k c", k=8)
        wq = PW[:, 6144:8448].rearrange("p (k c) -> p k c", k=3)
        wqr = PW[:, 8448:10752].rearrange("p (k c) -> p k c", k=3)
        wkv = PW[:, 10752:12800].rearrange("p (k c) -> p k c", k=2)
        t_wA, t_wq, t_wkv, t_wAr, t_wqr = ptok("wA"), ptok("wq"), ptok("wkv"), ptok("wAr"), ptok("wqr")
        load_w(wA, w_in_cols(l, A0, A0 + 672), t_wA, extra_writes=pw_users(), exempt=True)
        load_w(wq, w_q_up[l].rearrange("(k p) c -> p k c", p=128), t_wq, exempt=True)
        load_w(wkv, w_kv_up[l].rearrange("(k p) c -> p k c", p=128), t_wkv, exempt=True)
        mla_w.update(wA=wA, wAr=wAr, wq=wq, wqr=wqr, wkv=wkv)

    def mla_weights_prep(l):
        wA, wAr, wq, wqr, wkv = mla_w["wA"], mla_w["wAr"], mla_w["wq"], mla_w["wqr"], mla_w["wkv"]
        t_wA, t_wq, t_wkv, t_wAr, t_wqr = ptok("wA"), ptok("wq"), ptok("wkv"), ptok("wAr"), ptok("wqr")

        def scale_q(h):
            ins = None
            for k in range(3):
                ins = h.tensor_scalar(out=wq[:, k, :], in0=wq[:, k, :], scalar1=qns[:, l, k:k + 1], scalar2=None, op0=ALU.mult)
            return ins
        dve(scale_q, [t_wq, t_const], [t_wq])

        def scale_kv(h):
            ins = None
            for k in range(2):
                ins = h.tensor_scalar(out=wkv[:, k, :], in0=wkv[:, k, :], scalar1=kvns[:, l, k:k + 1], scalar2=None, op0=ALU.mult)
            return ins
        dve(scale_kv, [t_wkv, t_const], [t_wkv])
        dve(lambda h: h.memset(wAr[:], 0.0), [t_wA], [t_wAr])

        def rotA(h):
            h.tensor_scalar(out=wAr[:, :, 64:80], in0=wA[:, :, 656:672], scalar1=-1.0, scalar2=None, op0=ALU.mult)
            return h.tensor_copy(out=wAr[:, :, 80:96], in_=wA[:, :, 640:656])
        dve(rotA, [t_wA, t_wAr], [t_wAr])
        dve(lambda h: h.memset(wqr[:], 0.0), [t_wA], [t_wqr])

        def rotQ(h):
            wq4 = wq.rearrange("p k (h c) -> p k h c", h=8)
            wqr4 = wqr.rearrange("p k (h c) -> p k h c", h=8)
            ins = None
            for k in range(3):
                h.tensor_scalar(out=wqr4[:, k, :, 64:80], in0=wq4[:, k, :, 80:96], scalar1=-1.0, scalar2=None, op0=ALU.mult)
                ins = h.tensor_copy(out=wqr4[:, k, :, 80:96], in_=wq4[:, k, :, 64:80])
            return ins
        dve(rotQ, [t_wq, t_wqr], [t_wqr])

    def na_weights(l):
        wC = PW[:, 0:12288].rearrange("p (k c) -> p k c", k=8)
        load_w(wC, w_in_cols(l, C0, C0 + 1536), t_PW, extra_writes=pw_users(), exempt=True)

    def gla_weights(l):
        wB = PW[:, 0:12544].rearrange("p (k c) -> p k c", k=8)
        load_w(wB, w_in_cols(l, B0, B0 + 1568), t_PW, extra_writes=pw_users(), exempt=True)

    def mla_phase(l, e):
        KT = Y[:, 4 * T:12 * T].rearrange("p (h t) -> p h t", h=8)
        cv = Carver(BIG, BIGN)
        VA = cv.take(18 * 8 * 128).rearrange("p (t h c) -> p t h c", t=18, h=8)
        QTb = [cv.take(4096).rearrange("p (h t) -> p h t", h=8) for _ in range(2)]
        qdn = cv.take(1536).rearrange("p (c t) -> p c t", c=3)
        kvdns = [cv.take(1024).rearrange("p (c t) -> p c t", c=2) for _ in range(2)]
        sqb = cv.take(1536).rearrange("p (c t) -> p c t", c=3)
        NPT = 5
        PT = [cv.take(512) for _ in range(NPT)]
        t1b, t2b, bcs, rsb, rC, rS = Fb[0], Fb[1], Fb[2], Fb[3], Fb[4], Fb[5]
        t_t1, t_t2, t_bcs, t_rs, t_rC, t_rS = tF[0], tF[1], tF[2], tF[3], tF[4], tF[5]
        t_KT = P.toks("KT", len(BLOCKS))
        t_VA = P.toks("VA", len(BLOCKS))
        t_QT = P.toks("QT", 2)
        t_qdn, t_kvdns, t_sqb = P.tok("qdn"), P.toks("kvdn", 2), P.tok("msq")
        t_PT = P.toks("PT", NPT)
        t_vones = P.tok("vones")
        wA, wAr, wq, wqr, wkv = mla_w["wA"], mla_w["wAr"], mla_w["wq"], mla_w["wqr"], mla_w["wkv"]
        t_wA, t_wq, t_wkv, t_wAr, t_wqr = ptok("wA"), ptok("wq"), ptok("wkv"), ptok("wAr"), ptok("wqr")
        dve(lambda h: h.memset(VA[:, :, :, 64:128], 1.0), [], [t_vones])

        def down_norm(c0, nchunk, dim, dst, t_dst, bi, t0, n):
            banks = []
            for c in range(nchunk):
                pb, tp = next_ps(0, 4)
                banks.append((pb, tp))

                def mm(h, pb=pb, c=c):
                    ins = None
                    for k in range(8):
                        ins = h.matmul(pb[:, 0:n], wA[:, k, c0 + c * 128:c0 + (c + 1) * 128], xnT[:, k, t0:t0 + n], start=(k == 0), stop=(k == 7))
                    return ins
                pe(mm, [t_wA, t_xn[bi]], [tp])
                act(lambda h, pb=pb, c=c: h.activation(out=sqb[:, c, 0:n], in_=pb[:, 0:n], func=AF.Square), [tp], [t_sqb])
            pss, tss = next_ps(4, 6)

            def mm2(h):
                ins = None
                for c in range(nchunk):
                    ins = h.matmul(pss[:, 0:n], ones_bf[:], sqb[:, c, 0:n], start=(c == 0), stop=(c == nchunk - 1))
                return ins
            pe(mm2, [t_sqb, t_const], [tss])
            rstd_from_ss(pss, tss, n, dim, Rb, t_Rb)
            for c, (pb, tp) in enumerate(banks):
                dve(lambda h, pb=pb, c=c: h.tensor_tensor(out=dst[:, c, 0:n], in0=pb[:, 0:n], in1=Rb[:, 0:n], op=ALU.mult), [tp, t_Rb], [t_dst])

        def load_rope(t0, n):
            P.dma("sp", rC[0:96, 0:n], ropeC[:, t0:t0 + n], [], [t_rC], t_rC)
            P.dma("sp", rS[0:96, 0:n], ropeS[:, t0:t0 + n], [], [t_rS], t_rS)

        def stageK(bi, t0, n):
            kvdn, t_kvdn = kvdns[bi % 2], t_kvdns[bi % 2]
            down_norm(384, 2, 256, kvdn, t_kvdn, bi, t0, n)
            load_rope(t0, n)
            for hd in range(8):
                pb, tp = next_ps(0, 4)

                def mm(h, pb=pb, hd=hd):
                    ins = None
                    for c in range(2):
                        ins = h.matmul(pb[0:64, 0:n], wkv[:, c, hd * 128:hd * 128 + 64], kvdn[:, c, 0:n], start=(c == 0), stop=(c == 1))
                    return ins
                pe(mm, [t_wkv, t_kvdn], [tp])
                if hd % 2 == 0:
                    act(lambda h, pb=pb, hd=hd: h.activation(out=KT[0:64, hd, t0:t0 + n], in_=pb[0:64, 0:n], func=AF.Copy), [tp], [t_KT[bi]])
                else:
                    dve(lambda h, pb=pb, hd=hd: h.tensor_copy(out=KT[0:64, hd, t0:t0 + n], in_=pb[0:64, 0:n]), [tp], [t_KT[bi]])
            pa, tpa = next_ps(0, 4)
            pb2, tpb2 = next_ps(0, 4)

            def mmr(h, pa=pa, pb2=pb2):
                ins = None
                for k in range(8):
                    ins = h.matmul(pa[0:96, 0:n], wA[:, k, 576:672], xnT[:, k, t0:t0 + n], start=(k == 0), stop=(k == 7))
                for k in range(8):
                    ins = h.matmul(pb2[0:96, 0:n], wAr[:, k, 0:96], xnT[:, k, t0:t0 + n], start=(k == 0), stop=(k == 7))
                return ins
            pe(mmr, [t_wA, t_wAr, t_xn[bi]], [tpa, tpb2])
            dve(lambda h, pa=pa: h.tensor_tensor(out=t1b[64:96, 0:n], in0=pa[64:96, 0:n], in1=rC[64:96, 0:n], op=ALU.mult), [tpa, t_rC], [t_t1])
            dve(lambda h, pb2=pb2: h.tensor_tensor(out=t2b[64:96, 0:n], in0=pb2[64:96, 0:n], in1=rS[64:96, 0:n], op=ALU.mult), [tpb2, t_rS], [t_t2])
            dve(lambda h: h.tensor_tensor(out=t1b[64:96, 0:n], in0=t1b[64:96, 0:n], in1=t2b[64:96, 0:n], op=ALU.add), [t_t1, t_t2], [t_t1])
            for hd in range(8):
                if hd % 2 == 0:
                    act(lambda h, hd=hd: h.activation(out=KT[64:96, hd, t0:t0 + n], in_=t1b[64:96, 0:n], func=AF.Copy), [t_t1], [t_KT[bi]])
                else:
                    dve(lambda h, hd=hd: h.tensor_copy(out=KT[64:96, hd, t0:t0 + n], in_=t1b[64:96, 0:n]), [t_t1], [t_KT[bi]])
            for ti in range(n // 128):
                tt = (t0 + ti * 128) // 128
                pb, tp = next_ps(0, 4)

                def mmv(h, pb=pb, ti=ti):
                    ins = None
                    for hd in range(8):
                        for c in range(2):
                            ins = h.matmul(pb[:, hd * 64:(hd + 1) * 64], kvdn[:, c, ti * 128:(ti + 1) * 128], wkv[:, c, hd * 128 + 64:(hd + 1) * 128], start=(c == 0), stop=(c == 1))
                    return ins
                pe(mmv, [t_wkv, t_kvdn], [tp])
                act(lambda h, pb=pb, tt=tt: h.activation(out=VA[:, tt, :, 0:64], in_=pb[:, 0:512].rearrange("p (h c) -> p h c", h=8), func=AF.Copy),
                    [tp, t_vones], [t_VA[bi]])

        for bi, (t0, n) in enumerate(BLOCKS):
            stageK(bi, t0, n)

        def stageQproj(bi, t0, n):
            down_norm(0, 3, 384, qdn, t_qdn, bi, t0, n)
            load_rope(t0, n)
            QT, tQ = QTb[bi % 2], t_QT[bi % 2]
            for hd in range(8):
                pa, tpa = next_ps(0, 4)
                pb2, tpb2 = next_ps(0, 4)

                def mmq(h, pa=pa, pb2=pb2, hd=hd):
                    ins = None
                    for c in range(3):
                        ins = h.matmul(pa[0:96, 0:n], wq[:, c, hd * 96:(hd + 1) * 96], qdn[:, c, 0:n], start=(c == 0), stop=(c == 2))
                    for c in range(3):
                        ins = h.matmul(pb2[0:96, 0:n], wqr[:, c, hd * 96:(hd + 1) * 96], qdn[:, c, 0:n], start=(c == 0), stop=(c == 2))
                    return ins
                pe(mmq, [t_wq, t_wqr, t_qdn], [tpa, tpb2])
                act(lambda h, pa=pa, hd=hd, QT=QT: h.activation(out=QT[0:64, hd, 0:n], in_=pa[0:64, 0:n], func=AF.Copy), [tpa], [tQ])
                dve(lambda h, pa=pa: h.tensor_tensor(out=t1b[64:96, 0:n], in0=pa[64:96, 0:n], in1=rC[64:96, 0:n], op=ALU.mult), [tpa, t_rC], [t_t1])
                dve(lambda h, pb2=pb2: h.tensor_tensor(out=t2b[64:96, 0:n], in0=pb2[64:96, 0:n], in1=rS[64:96, 0:n], op=ALU.mult), [tpb2, t_rS], [t_t2])
                dve(lambda h, hd=hd, QT=QT: h.tensor_tensor(out=QT[64:96, hd, 0:n], in0=t1b[64:96, 0:n], in1=t2b[64:96, 0:n], op=ALU.add), [t_t1, t_t2], [tQ])

        def stageAttn(bi, t0, n):
            ctxq = t0 >= SEQ
            ktiles = [16, 17] if ctxq else list(range(18))
            kdeps = ([t_KT[4], t_VA[4]] if ctxq else t_KT + t_VA)
            QT, tQ = QTb[bi % 2], t_QT[bi % 2]
            items = [(hd, ki, kt) for hd in range(8) for ki, kt in enumerate(ktiles)]
            nk = len(ktiles)
            pos, pts = {}, {}
            LOOK = 4

            def issue_S(idx):
                hd, ki, kt = items[idx]
                psS, tps = next_ps(0, 6)
                pe(lambda h: h.matmul(psS[:, 0:n], KT[0:96, hd, kt * 128:(kt + 1) * 128], QT[0:96, hd, 0:n], start=True, stop=True), kdeps + [tQ], [tps])
                pt, tpt = PT[idx % NPT], t_PT[idx % NPT]
                act(lambda h: h.activation(out=pt[:, 0:n], in_=psS[:, 0:n], func=AF.Exp, scale=MLA_SCALE), [tps], [tpt])
                pts[idx] = (pt, tpt)

            def issue_PV(idx):
                hd, ki, kt = items[idx]
                if ki == 0:
                    pos[hd] = next_ps(6, 8)
                po, tpo = pos[hd]
                pt, tpt = pts.pop(idx)
                pe(lambda h: h.matmul(po[:, 0:n], VA[:, kt, hd, :], pt[:, 0:n], start=(ki == 0), stop=(ki == nk - 1)), kdeps + [tpt, t_vones], [tpo])
                if ki == nk - 1:
                    dve(lambda h: h.reciprocal(out=bcs[0:64, 0:n], in_=po[64:128, 0:n]), [tpo], [t_bcs])
                    p0 = (hd % 2) * 64
                    dve(lambda h: h.tensor_tensor(out=yaT[p0:p0 + 64, hd // 2, t0:t0 + n], in0=po[0:64, 0:n], in1=bcs[0:64, 0:n], op=ALU.mult),
                        [tpo, t_bcs], [t_ya[bi]])
            for idx in range(len(items) + LOOK):
                if idx < len(items):
                    issue_S(idx)
                if idx >= LOOK:
                    issue_PV(idx - LOOK)

        qblocks = [b for b in enumerate(BLOCKS) if not (l == n_layers - 1 and b[1][0] >= SEQ)]
        stageQproj(0, *BLOCKS[0])
        for i, (bi, (t0, n)) in enumerate(qblocks):
            if i + 1 < len(qblocks):
                stageQproj(qblocks[i + 1][0], *qblocks[i + 1][1])
            stageAttn(bi, t0, n)

    def na_phase(l, e):
        cv = Carver(BIG, BIGN)
        kT = ybT
        VN = cv.take(18 * 8 * 128).rearrange("p (t h c) -> p t h c", t=18, h=8)
        tab = cv.take(_NU * 512).rearrange("p (u c) -> p u c", u=_NU)
        qTb = [cv.take(4096).rearrange("p (j a t) -> p j a t", j=4, a=2) for _ in range(2)]
        NPT = 4
        PT = [Fb[6][:, :].bitcast(BF16)[:, 0:512], Fb[3][:, :].bitcast(BF16)[:, 0:512], Fb[1][:, :].bitcast(BF16)[:, 0:512], Fb[0][:, :].bitcast(BF16)[:, 0:512]]
        identb = PW[:, 13000:13128]
        bcs, t_bcs, rsb, t_rs = Fb[2], tF[2], Fb[3], tF[3]
        t_k, t_v = P.toks("nk", len(BLOCKS)), P.toks("nv", len(BLOCKS))
        t_q = P.toks("nq", 2)
        t_tab = ptok("tab")
        t_PT = P.toks("nPT", NPT)
        t_ident = ptok("n_ident")
        t_vones = P.tok("nvones")
        wC = PW[:, 0:12288].rearrange("p (k c) -> p k c", k=8)
        load_w(tab, na_tab[l], t_tab)
        for u in range(_NU):
            nb, tnb = next_f(4, 6)
            P.dma("sp", nb[:, 0:512], na_neg[:, u, :], [], [tnb], tnb)
            dve(lambda h, u=u, nb=nb: h.scalar_tensor_tensor(out=tab[:, u, :], in0=tab[:, u, :], scalar=1.0 / NA_SCALE, in1=nb[:, 0:512], op0=ALU.mult, op1=ALU.add), [t_tab, tnb], [t_tab])
        dve(lambda h: h.memset(VN[:, :, :, 64:128], 1.0), [], [t_vones])
        load_w(identb, ident_in, t_ident)
        for i in range(2):
            dve(lambda h, i=i: h.memset(qTb[i][:], 0.0), [], [t_q[i]])

        def proj_fm(which, dst_fn, tdst, bi, t0, n):
            for j in range(4):
                pb, tp = next_ps(0, 4)

                def mm(h, pb=pb, j=j):
                    ins = None
                    for k in range(8):
                        ins = h.matmul(pb[:, 0:n], wC[:, k, which * 512 + j * 128:which * 512 + (j + 1) * 128], xnT[:, k, t0:t0 + n], start=(k == 0), stop=(k == 7))
                    return ins
                pe(mm, [t_PW, t_xn[bi]], [tp])
                if j % 2 == 0:
                    act(lambda h, pb=pb, j=j: h.activation(out=dst_fn(j), in_=pb[:, 0:n], func=AF.Copy), [tp], [tdst])
                else:
                    dve(lambda h, pb=pb, j=j: h.tensor_copy(out=dst_fn(j), in_=pb[:, 0:n]), [tp], [tdst])

        def stageK(bi, t0, n):
            proj_fm(1, lambda j, t0=t0, n=n: kT[:, j, t0:t0 + n], t_k[bi], bi, t0, n)
            for ti in range(n // 128):
                tt = (t0 + ti * 128) // 128
                pb, tp = next_ps(0, 4)

                def mmv(h, pb=pb, ti=ti):
                    ins = None
                    for k in range(8):
                        ins = h.matmul(pb[:, 0:512], xnT[:, k, t0 + ti * 128:t0 + (ti + 1) * 128], wC[:, k, 1024:1536], start=(k == 0), stop=(k == 7))
                    return ins
                pe(mmv, [t_PW, t_xn[bi]], [tp])
                act(lambda h, pb=pb, tt=tt: h.activation(out=VN[:, tt, :, 0:64], in_=pb[:, 0:512].rearrange("p (h c) -> p h c", h=8), func=AF.Copy),
                    [tp, t_vones], [t_v[bi]])
        for bi, (t0, n) in enumerate(BLOCKS):
            stageK(bi, t0, n)
        kvdeps = t_k + t_v

        def stageQ(bi, t0, n):
            qT, tq = qTb[bi % 2], t_q[bi % 2]
            for j in range(4):
                pb, tp = next_ps(0, 4)

                def mmq(h, pb=pb, j=j):
                    ins = None
                    for k in range(8):
                        ins = h.matmul(pb[:, 0:n], wC[:, k, j * 128:(j + 1) * 128], xnT[:, k, t0:t0 + n], start=(k == 0), stop=(k == 7))
                    return ins
                pe(mmq, [t_PW, t_xn[bi]], [tp])
                act(lambda h, pb=pb, j=j: h.activation(out=qT[0:64, j, 0, 0:n], in_=pb[0:64, 0:n], func=AF.Copy), [tp], [tq])
                dve(lambda h, pb=pb, j=j: h.tensor_copy(out=qT[64:128, j, 1, 0:n], in_=pb[64:128, 0:n]), [tp], [tq])
            items = []
            for rr in range(n // 64):
                r = t0 // 64 + rr
                if t0 < SEQ:
                    tiles = list(_PLAN[r]) + [(16, None), (17, None)]
                else:
                    tiles = [(16, None), (17, None)]
                for ki, (tt, u) in enumerate(tiles):
                    items.append((rr, ki, len(tiles), tt, u))
            pos, pts = {}, {}
            LOOK = 3

            def issue_S(idx):
                rr, ki, nt, tt, u = items[idx]
                psS, tps = next_ps(0, 6)

                def mms(h):
                    ins = None
                    for hd in range(8):
                        ins = h.matmul(psS[:, hd * 64:(hd + 1) * 64], kT[:, hd // 2, tt * 128:(tt + 1) * 128], qT[:, hd // 2, hd % 2, rr * 64:(rr + 1) * 64],
                                       start=(hd == 0), stop=(hd == 7 and u is None), skip_group_check=True)
                    if u is not None:
                        ins = h.matmul(psS[:, 0:512], identb, tab[:, u, :], start=False, stop=True, skip_group_check=True)
                    return ins
                pe(mms, kvdeps + [tq, t_tab, t_ident], [tps])
                pt, tpt = PT[idx % NPT], t_PT[idx % NPT]
                act(lambda h: h.activation(out=pt[:, 0:512], in_=psS[:, 0:512], func=AF.Exp, scale=NA_SCALE), [tps], [tpt])
                pts[idx] = (pt, tpt)

            def issue_PV(idx):
                rr, ki, nt, tt, u = items[idx]
                if ki == 0:
                    pos[rr] = next_ps(6, 8)
                po, tpo = pos[rr]
                pt, tpt = pts.pop(idx)

                def mmo(h):
                    ins = None
                    for hd in range(8):
                        ins = h.matmul(po[:, hd * 64:(hd + 1) * 64], VN[:, tt, hd, :], pt[:, hd * 64:(hd + 1) * 64], start=(ki == 0 and hd == 0), stop=(ki == nt - 1 and hd == 7), skip_group_check=True)
                    return ins
                pe(mmo, kvdeps + [tpt, t_vones], [tpo])
                if ki == nt - 1:
                    q0 = t0 + rr * 64
                    dve(lambda h: h.reciprocal(out=bcs[0:64, 0:512], in_=po[64:128, 0:512]), [tpo], [t_bcs])
                    for par in range(2):
                        def fin(h, par=par):
                            pov = po[0:64, 0:512].rearrange("p (j a c) -> p j a c", j=4, a=2)[:, :, par, :]
                            bcv = bcs[0:64, 0:512].rearrange("p (j a c) -> p j a c", j=4, a=2)[:, :, par, :]
                            return h.tensor_tensor(out=ycT[par * 64:(par + 1) * 64, :, q0:q0 + 64], in0=pov, in1=bcv, op=ALU.mult)
                        dve(fin, [tpo, t_bcs], [t_yc])
            for idx in range(len(items) + LOOK):
                if idx < len(items):
                    issue_S(idx)
                if idx >= LOOK:
                    issue_PV(idx - LOOK)

        for bi, (t0, n) in enumerate(BLOCKS):
            if l == n_layers - 1 and t0 >= SEQ:
                continue
            stageQ(bi, t0, n)

    def gla_phase(l, e):
        cv = Carver(BIG, BIGN)
        oacc = cv.take(4 * T, F32).rearrange("p (h t) -> p h t", h=4)
        lrT = cv.take(128)
        wg = [cv.take(256) for _ in range(2)]
        Ltok = cv.take(256)
        k_tok = cv.take(256, F32)
        qTt = cv.take(512, F32)
        kTt = cv.take(512, F32)
        EK = cv.take(512, F32)
        v_toks = [cv.take(512) for _ in range(2)]
        kends = [cv.take(256) for _ in range(2)]
        EQs = [cv.take(512, F32) for _ in range(2)]
        qdecs = [cv.take(512).rearrange("p (h t) -> p h t", h=4) for _ in range(2)]
        kinvs = [cv.take(512).rearrange("p (h t) -> p h t", h=4) for _ in range(2)]
        attms = [cv.take(512).rearrange("p (h t) -> p h t", h=4) for _ in range(2)]
        S = [cv.take(512, F32).rearrange("p (h c) -> p h c", h=4) for _ in range(2)]
        Sbf = cv.take(512).rearrange("p (h c) -> p h c", h=4)
        mk = [cv.take(128) for _ in range(4)]
        mkf = [cv.take(128, F32) for _ in range(2)]
        sqg = cv.take(512)
        et, Rg, EE, osum_, sg_ = Fb[0], Fb[1], Fb[2], Fb[3], Fb[4]
        t_et, t_Rg, t_EE, t_osum, t_sg = tF[0], tF[1], tF[2], tF[3], tF[4]
        osum = osum_[:, 0:512].rearrange("p (h t) -> p h t", h=4)
        sg = sg_[:, 0:512].rearrange("p (h t) -> p h t", h=4)
        tk = lambda nme: P.tok("g_" + nme)
        t_lr, t_wg, t_L, t_kt = tk("lr"), ptok("g_wg"), tk("L"), tk("kt")
        t_qT, t_kT, t_EK, t_attm = tk("qT"), tk("kT"), tk("EK"), P.toks("g_attm", 2)
        t_v, t_kend, t_EQ, t_qdec, t_kinv = P.toks("g_v", 2), P.toks("g_kend", 2), P.toks("g_EQ", 2), P.toks("g_qdec", 2), P.toks("g_kinv", 2)
        t_S = [P.toks(f"gS{d}_", 4) for d in range(2)]
        t_Sbf, t_mk, t_sqg, t_oacc = tk("Sbf"), ptok("g_mk"), tk("sqg"), tk("oacc")
        wB = PW[:, 0:12544].rearrange("p (k c) -> p k c", k=8)
        for d in range(2):
            load_w(wg[d][0:17, 0:256], gla_gate[l, d], t_wg)
        for i in range(4):
            load_w(mk[i][:, 0:128], gmask[i], t_mk)
        for i in range(2):
            P.dma("sp", mkf[i][:, 0:128], gmask[i], [], [t_mk], ptok("g_mkf%d" % i))
        dve(lambda h: h.memset(lrT[0:32, 0:128], 1.0), [], [t_lr])
        for d in range(2):
            dve(lambda h, d=d: h.memset(S[d][:], 0.0), [], t_S[d])

        def stage_a(tt, d, sl):
            tok0 = tt * 128
            bi = 4 if tt >= 16 else tt // 4
            xs = lambda k: xnT[:, k, tok0:tok0 + 128]
            v_tok, kend, EQ, qdec, kinv = v_toks[sl], kends[sl], EQs[sl], qdecs[sl], kinvs[sl]
            pb, tp = next_ps(0, 6)

            def mm_lr(h):
                ins = None
                for k in range(8):
                    ins = h.matmul(pb[0:16, 0:128], wB[:, k, 1536 + d * 16:1536 + (d + 1) * 16], xs(k), start=(k == 0), stop=(k == 7))
                return ins
            pe(mm_lr, [t_PW, t_xn[bi]], [tp])
            dve(lambda h: h.tensor_copy(out=lrT[0:16, 0:128], in_=pb[0:16, 0:128]), [tp], [t_lr])
            for which, dst, td in ((0, qTt, t_qT), (1, kTt, t_kT)):
                pq, tpq = next_ps(0, 6)

                def mmq(h, pq=pq, which=which):
                    ins = None
                    for j in range(2):
                        for k in range(8):
                            ins = h.matmul(pq[:, j * 128:(j + 1) * 128], wB[:, k, which * 256 + j * 128:which * 256 + (j + 1) * 128], xs(k), start=(k == 0), stop=(k == 7))
                    return ins
                pe(mmq, [t_PW, t_xn[bi]], [tpq])
                dst4 = dst[0:64, 0:512].rearrange("p (j a t) -> p j a t", j=2, a=2)
                act(lambda h, pq=pq, dst4=dst4: h.activation(out=dst4[:, :, 0, :], in_=pq[0:64, 0:256].rearrange("p (j t) -> p j t", j=2), func=AF.Copy), [tpq], [td])
                dve(lambda h, pq=pq, dst4=dst4: h.tensor_copy(out=dst4[:, :, 1, :], in_=pq[64:128, 0:256].rearrange("p (j t) -> p j t", j=2)), [tpq], [td])
            pz, tpz = next_ps(0, 6)
            pe(lambda h: h.matmul(pz[:, 0:256], lrT[0:17, 0:128], wg[d][0:17, 0:256], start=True, stop=True), [t_lr, t_wg], [tpz])
            act(lambda h: h.activation(out=et[:, 0:256], in_=pz[:, 0:256], func=AF.Exp, scale=-1.0), [tpz], [t_et])
            act(lambda h: h.activation(out=et[:, 0:256], in_=et[:, 0:256], func=AF.Ln, bias=one_col[:, 0:1], scale=1.0), [t_et, t_const], [t_et])
            dve(lambda h: h.tensor_scalar(out=Ltok[:, 0:256], in0=et[:, 0:256], scalar1=1.0 / 16.0, scalar2=None, op0=ALU.mult), [t_et], [t_L])
            pk, tpk = next_ps(0, 6)

            def mmk(h):
                ins = None
                for k in range(8):
                    ins = h.matmul(pk[:, 0:256], xs(k), wB[:, k, 256:512], start=(k == 0), stop=(k == 7))
                return ins
            pe(mmk, [t_PW, t_xn[bi]], [tpk])
            dve(lambda h: h.tensor_copy(out=k_tok[:, 0:256], in_=pk[:, 0:256]), [tpk], [t_kt])
            pv, tpv = next_ps(0, 6)

            def mmv(h):
                ins = None
                for k in range(8):
                    ins = h.matmul(pv[:, 0:512], xs(k), wB[:, k, 512:1024], start=(k == 0), stop=(k == 7))
                return ins
            pe(mmv, [t_PW, t_xn[bi]], [tpv])
            act(lambda h: h.activation(out=v_tok[:, 0:512], in_=pv[:, 0:512], func=AF.Copy), [tpv], [t_v[sl]])
            pc, tpc = next_ps(0, 6)

            def mmc(h):
                ins = None
                for hd in range(4):
                    ins = h.matmul(pc[0:64, hd * 128:(hd + 1) * 128], Ltok[:, hd * 64:(hd + 1) * 64], mk[d][:, 0:128], start=True, stop=True)
                return ins
            pe(mmc, [t_L, t_mk], [tpc])
            act(lambda h: h.activation(out=EQ[0:64, 0:512], in_=pc[0:64, 0:512], func=AF.Exp, scale=-1.0), [tpc], [t_EQ[sl]])
            act(lambda h: h.activation(out=EK[0:64, 0:512], in_=pc[0:64, 0:512], func=AF.Exp, scale=1.0), [tpc], [t_EK])
            pr, tpr = next_ps(0, 6)
            pe(lambda h: h.matmul(pr[:, 0:256], mk[2 + d][:, 0:128], Ltok[:, 0:256], start=True, stop=True), [t_L, t_mk], [tpr])
            act(lambda h: h.activation(out=EE[:, 0:256], in_=pr[:, 0:256], func=AF.Exp, scale=-1.0), [tpr], [t_EE])
            dve(lambda h: h.scalar_tensor_tensor(out=qdec[0:64].rearrange("p h t -> p (h t)"), in0=qTt[0:64, 0:512], scalar=0.125, in1=EQ[0:64, 0:512], op0=ALU.mult, op1=ALU.mult),
                [t_qT, t_EQ[sl]], [t_qdec[sl]])
            dve(lambda h: h.tensor_tensor(out=kinv[0:64].rearrange("p h t -> p (h t)"), in0=kTt[0:64, 0:512], in1=EK[0:64, 0:512], op=ALU.mult), [t_kT, t_EK], [t_kinv[sl]])
            dve(lambda h: h.tensor_tensor(out=kend[:, 0:256], in0=k_tok[:, 0:256], in1=EE[:, 0:256], op=ALU.mult), [t_kt, t_EE], [t_kend[sl]])

        def stage_att(d, sl):
            qdec, kinv = qdecs[sl], kinvs[sl]
            attm = attms[sl]
            pa, tpa = next_ps(0, 6)

            def mma(h):
                ins = None
                for hd in range(4):
                    ins = h.matmul(pa[:, hd * 128:(hd + 1) * 128], kinv[0:64, hd, :], qdec[0:64, hd, :], start=True, stop=True)
                return ins
            pe(mma, [t_kinv[sl], t_qdec[sl]], [tpa])
            for hd in range(4):
                dve(lambda h, hd=hd: h.tensor_tensor(out=attm[:, hd, :], in0=pa[:, hd * 128:(hd + 1) * 128], in1=mkf[d][:, 0:128], op=ALU.mult), [tpa, t_mk], [t_attm[sl]])

        def stage_b(tt, d, sl):
            tok0 = tt * 128
            bi = 4 if tt >= 16 else tt // 4
            v_tok, kend, EQ, qdec, kinv = v_toks[sl], kends[sl], EQs[sl], qdecs[sl], kinvs[sl]
            EQ3 = EQ.rearrange("p (h t) -> p h t", h=4)
            col = 127 if d == 0 else 0
            attm = attms[sl]
            po, tpo = next_ps(6, 8)
            dve(lambda h: h.tensor_copy(out=Sbf[0:64], in_=S[d][0:64]), t_S[d], [t_Sbf])

            def mmo(h):
                ins = None
                for hd in range(4):
                    reg = po[:, hd * 128:(hd + 1) * 128]
                    h.matmul(reg, v_tok[:, hd * 128:(hd + 1) * 128], attm[:, hd, :], start=True, stop=False)
                    ins = h.matmul(reg, Sbf[0:64, hd, :], qdec[0:64, hd, :], start=False, stop=True)
                return ins
            pe(mmo, [t_v[sl], t_attm[sl], t_Sbf, t_qdec[sl]], [tpo])
            pkv, tpkv = next_ps(0, 6)

            def mmkv(h):
                ins = None
                for hd in range(4):
                    ins = h.matmul(pkv[0:64, hd * 128:(hd + 1) * 128], kend[:, hd * 64:(hd + 1) * 64], v_tok[:, hd * 128:(hd + 1) * 128], start=True, stop=True)
                return ins
            pe(mmkv, [t_kend[sl], t_v[sl]], [tpkv])
            for hd in range(4):
                dve(lambda h, hd=hd: h.scalar_tensor_tensor(
                    out=S[d][0:64, hd, :], in0=S[d][0:64, hd, :], scalar=EQ3[0:64, hd, col:col + 1], in1=pkv[0:64, hd * 128:(hd + 1) * 128],
                    op0=ALU.mult, op1=ALU.add), [t_S[d][hd], t_EQ[sl], tpkv, t_Sbf], [t_S[d][hd]])
            if d == 0:
                act(lambda h: h.activation(out=oacc[:, :, tok0:tok0 + 128], in_=po[:, 0:512].rearrange("p (h t) -> p h t", h=4), func=AF.Copy), [tpo], [t_oacc])
                return
            dve(lambda h: h.tensor_tensor(out=osum[:], in0=po[:, 0:512].rearrange("p (h t) -> p h t", h=4), in1=oacc[:, :, tok0:tok0 + 128], op=ALU.add), [tpo, t_oacc], [t_osum])
            act(lambda h: h.activation(out=sqg[:, 0:512], in_=osum_[:, 0:512], func=AF.Square), [t_osum], [t_sqg])
            pss, tss = next_ps(0, 6)
            pe(lambda h: h.matmul(pss[:, 0:512], ones_bf[:], sqg[:, 0:512], start=True, stop=True), [t_sqg, t_const], [tss])
            rstd_from_ss(pss, tss, 512, 128, Rg, t_Rg)
            pg, tpg = next_ps(0, 6)

            def mmg(h):
                ins = None
                for hd in range(4):
                    for k in range(8):
                        ins = h.matmul(pg[:, hd * 128:(hd + 1) * 128], wB[:, k, 1024 + hd * 128:1024 + (hd + 1) * 128], xnT[:, k, tok0:tok0 + 128], start=(k == 0), stop=(k == 7))
                return ins
            pe(mmg, [t_PW, t_xn[bi]], [tpg])
            act(lambda h: h.activation(out=sg_[:, 0:512], in_=pg[:, 0:512], func=AF.Silu), [tpg], [t_sg])
            dve(lambda h: h.scalar_tensor_tensor(out=osum_[:, 0:512], in0=osum_[:, 0:512], scalar=gns[:, l:l + 1], in1=Rg[:, 0:512], op0=ALU.mult, op1=ALU.mult),
                [t_osum, t_Rg, t_const], [t_osum])
            dve(lambda h: h.tensor_tensor(out=ybT[:, :, tok0:tok0 + 128], in0=osum[:], in1=sg[:], op=ALU.mult), [t_osum, t_sg], [t_yb])

        order_f = [16, 17] + list(range(16))
        order_b = [17, 16] + list(range(15, -1, -1))
        seq = [(tt, 0) for tt in order_f] + [(tt, 1) for tt in order_b]
        stage_a(seq[0][0], seq[0][1], 0)
        for i, (tt, d) in enumerate(seq):
            stage_att(d, i % 2)
            if i + 1 < len(seq):
                stage_a(seq[i + 1][0], seq[i + 1][1], (i + 1) % 2)
            stage_b(tt, d, i % 2)

    def resid_update(e, bi, t0, n, oc, pb, tp, j):
        tf, ttf = next_f(3, 6)
        P.dma("sp", tf[:, 0:n], hcur[e][:, oc, t0:t0 + n], [t_h[e][bi]], [ttf], ttf)
        dve(lambda h: h.scalar_tensor_tensor(out=tf[:, 0:n], in0=pb[:, 0:n], scalar=Gc[:, oc, j:j + 1], in1=tf[:, 0:n], op0=ALU.mult, op1=ALU.add),
            [tp, ttf, t_coef], [ttf])
        P.dma("sp", h_scr[e][:, oc, t0:t0 + n], tf[:, 0:n], [ttf], [t_hs[e][bi]], ttf)

    def merge_phase(l, e):
        mT = BIG[:, 0:8 * T].rearrange("p (k t) -> p k t", k=8)
        macc, t_macc = Fb[0], tF[0]
        sgt, t_sgt = [Fb[1], Fb[2]], [tF[1], tF[2]]
        t_m = P.tok("mT")
        ys = ((yaT, w_a_o), (ybT, w_b_o), (ycT, w_c_o))
        def stage1(oc):
            buf, tb = next_ring()
            wg_ = buf[:, 0:3072].rearrange("p (x k c) -> p x k c", x=3, k=8)
            wo_ = buf[:, 3072:4608].rearrange("p (x k c) -> p x k c", x=3, k=4)
            for x in range(3):
                load_w(wg_[:, x], w_in_cols(l, G0 + x * 1024 + oc * 128, G0 + x * 1024 + (oc + 1) * 128), tb)
                load_w(wo_[:, x], ys[x][1][l].rearrange("(k p) c -> p k c", p=128)[:, :, oc * 128:(oc + 1) * 128], tb)
            for bi, (t0, n) in enumerate(BLOCKS):
                if l == n_layers - 1 and t0 >= SEQ:
                    continue
                stage1b(oc, bi, t0, n, tb, wg_, wo_)

        def stage1b(oc, bi, t0, n, tb, wg_, wo_):
            if True:
                for x in range(3):
                    pg, tpg = next_ps(0, 4)
                    pp, tpp = next_ps(4, 8)

                    def mm(h, pg=pg, pp=pp, x=x):
                        ins = None
                        for k in range(8):
                            ins = h.matmul(pg[:, 0:n], wg_[:, x, k, :], xnT[:, k, t0:t0 + n], start=(k == 0), stop=(k == 7))
                        for k in range(4):
                            ins = h.matmul(pp[:, 0:n], wo_[:, x, k, :], ys[x][0][:, k, t0:t0 + n], start=(k == 0), stop=(k == 3))
                        return ins
                    pe(mm, [tb, t_xn[bi], t_ya[bi], t_yb, t_yc], [tpg, tpp])
                    sg_, tsg = sgt[x % 2], t_sgt[x % 2]
                    act(lambda h, pg=pg, sg_=sg_: h.activation(out=sg_[:, 0:n], in_=pg[:, 0:n], func=AF.Sigmoid), [tpg], [tsg])
                    if x == 0:
                        dve(lambda h, pp=pp, sg_=sg_: h.tensor_tensor(out=macc[:, 0:n], in0=pp[:, 0:n], in1=sg_[:, 0:n], op=ALU.mult), [tpp, tsg], [t_macc])
                    else:
                        dve(lambda h, pp=pp, sg_=sg_: h.tensor_tensor(out=sg_[:, 0:n], in0=pp[:, 0:n], in1=sg_[:, 0:n], op=ALU.mult), [tpp, tsg], [tsg])
                        if x == 1:
                            dve(lambda h, sg_=sg_: h.tensor_tensor(out=macc[:, 0:n], in0=macc[:, 0:n], in1=sg_[:, 0:n], op=ALU.add), [t_macc, tsg], [t_macc])
                        else:
                            dve(lambda h, sg_=sg_: h.tensor_tensor(out=mT[:, oc, t0:t0 + n], in0=macc[:, 0:n], in1=sg_[:, 0:n], op=ALU.add), [t_macc, tsg], [t_m])
        for oc in range(8):
            stage1(oc)

        def stage2(oc):
            buf, tb = next_ring()
            wo = buf[:, 0:1024].rearrange("p (k c) -> p k c", k=8)
            load_w(wo, w_out[l].rearrange("(k p) c -> p k c", p=128)[:, :, oc * 128:(oc + 1) * 128], tb)
            for bi, (t0, n) in enumerate(BLOCKS):
                if l == n_layers - 1 and t0 >= SEQ:
                    continue
                stage2b(oc, bi, t0, n, tb, wo)

        def stage2b(oc, bi, t0, n, tb, wo):
            if True:
                pb, tp = next_ps()

                def mm(h, pb=pb, wo=wo):
                    ins = None
                    for k in range(8):
                        ins = h.matmul(pb[:, 0:n], wo[:, k, :], mT[:, k, t0:t0 + n], start=(k == 0), stop=(k == 7))
                    return ins
                pe(mm, [tb, t_m], [tp])
                resid_update(e, bi, t0, n, oc, pb, tp, 1 if t0 >= SEQ else 0)

        for oc in range(8):
            stage2(oc)

    def ffn_phase(l, e):
        NB_ = 15

        def ff(j):
            if j < NB_:
                return BIG[:, j * T:(j + 1) * T]
            return Y[:, (j - NB_) * T:(j - NB_ + 1) * T]
        sgt, t_sgt = [Fb[0], Fb[1]], [tF[0], tF[1]]
        t_ff = P.tok("ff")
        def stage1(j2):
            buf, tb = next_ring()
            wv = buf[:, 0:4096].rearrange("p (x k c) -> p x k c", x=2, k=8)
            load_w(wv[:, 0], w_ffn_in[l].rearrange("(k p) c -> p k c", p=128)[:, :, j2 * 256:(j2 + 1) * 256], tb)
            load_w(wv[:, 1], w_ffn_in[l].rearrange("(k p) c -> p k c", p=128)[:, :, FFH + j2 * 256:FFH + (j2 + 1) * 256], tb)
            for jj in range(2):
                j = j2 * 2 + jj
                for bi, (t0, n) in enumerate(BLOCKS):
                    if l == n_layers - 1 and t0 >= SEQ:
                        continue
                    stage1b(j, jj, bi, t0, n, tb, wv)

        def stage1b(j, jj, bi, t0, n, tb, wv):
            if True:
                if True:
                    pg, tpg = next_ps(0, 4)
                    pu, tpu = next_ps(4, 8)

                    def mm(h, pg=pg, pu=pu, jj=jj, wv=wv):
                        ins = None
                        for k in range(8):
                            ins = h.matmul(pg[:, 0:n], wv[:, 0, k, jj * 128:(jj + 1) * 128], xnT[:, k, t0:t0 + n], start=(k == 0), stop=(k == 7))
                        for k in range(8):
                            ins = h.matmul(pu[:, 0:n], wv[:, 1, k, jj * 128:(jj + 1) * 128], xnT[:, k, t0:t0 + n], start=(k == 0), stop=(k == 7))
                        return ins
                    pe(mm, [tb, t_xn[bi]], [tpg, tpu])
                    sg_, tsg = sgt[j % 2], t_sgt[j % 2]
                    act(lambda h, pg=pg, sg_=sg_: h.activation(out=sg_[:, 0:n], in_=pg[:, 0:n], func=AF.Silu), [tpg], [tsg])
                    dve(lambda h, pu=pu, sg_=sg_, j=j: h.tensor_tensor(out=ff(j)[:, t0:t0 + n], in0=pu[:, 0:n], in1=sg_[:, 0:n], op=ALU.mult), [tpu, tsg], [t_ff])
        for j2 in range(11):
            stage1(j2)

        def stage2(oc):
            buf, tb = next_ring()
            wo = buf[:, 0:22 * 128].rearrange("p (j c) -> p j c", j=22)
            load_w(wo, w_ffn_out[l].rearrange("(j p) c -> p j c", p=128)[:, :, oc * 128:(oc + 1) * 128], tb)
            for bi, (t0, n) in enumerate(BLOCKS):
                if l == n_layers - 1 and t0 >= SEQ:
                    continue
                stage2b(oc, bi, t0, n, tb, wo)

        def stage2b(oc, bi, t0, n, tb, wo):
            if True:
                pb, tp = next_ps()

                def mm(h, pb=pb, wo=wo):
                    ins = None
                    for j in range(22):
                        ins = h.matmul(pb[:, 0:n], wo[:, j, :], ff(j)[:, t0:t0 + n], start=(j == 0), stop=(j == 21))
                    return ins
                pe(mm, [tb, t_ff], [tp])
                resid_update(e, bi, t0, n, oc, pb, tp, 1 if t0 >= SEQ else 0)

        for oc in range(8):
            stage2(oc)

    def tap(name, src_ap, toks):
        if name in tap_out:
            P.dma("sp", tap_out[name], src_ap, list(toks), [t_out], P.tok("tap"))

    t_out = P.tok("out")
    t_h0 = [P.toks(f"h0_{e}_", len(BLOCKS)) for e in range(2)]
    t_hs = [P.toks(f"hs_{e}_", len(BLOCKS)) for e in range(2)]
    hcur = [h0[0], h0[1]]
    t_h = [t_h0[0], t_h0[1]]
    P.phase = 'prologue'
    prologue()
    P.barrier()
    tap("modT", modT[:].rearrange("p l c e -> p (l c e)"), [t_modT])
    done = False
    for e in range(n_elems):
        if done:
            break
        for l in range(n_layers):
            P.phase = f'norm1_{e}_{l}'
            mla_weights(l)
            coef_cols(l, e, 0)
            norm_phase(e)
            mla_weights_prep(l)
            P.barrier()
            if stop_after == "norm1":
                done = True
                break
            P.phase = f'mla_{e}_{l}'
            mla_phase(l, e)
            na_weights(l)
            P.barrier()
            if stop_after == "mla":
                done = True
                break
            P.phase = f'na_{e}_{l}'
            na_phase(l, e)
            gla_weights(l)
            P.barrier()
            if stop_after == "na":
                done = True
                break
            P.phase = f'gla_{e}_{l}'
            gla_phase(l, e)
            P.barrier()
            if stop_after == "gla":
                done = True
                break
            P.phase = f'merge_{e}_{l}'
            merge_phase(l, e)
            hcur[e] = h_scr[e]
            t_h[e] = t_hs[e]
            P.barrier()
            if stop_after == "merge":
                done = True
                break
            P.phase = f'norm2_{e}_{l}'
            coef_cols(l, e, 1)
            norm_phase(e, skip_ctx=(l == n_layers - 1))
            P.barrier()
            P.phase = f'ffn_{e}_{l}'
            ffn_phase(l, e)
            P.barrier()
        if not done:
            P.phase = f'final_{e}'
            norm_phase(e, final=True)
            P.barrier()
    tap("Y", Y[:], t_ya + [t_yb, t_yc])
    tap("xnT", xnT[:], t_xn)
    tap("hs", h_scr[0], t_hs[0])
    P.add("sp", None, [], [t_out] + t_hs[0] + t_hs[1])
    P.emit()
    nc._prog = P
    return nc, stack


def _col(v, n):
    return np.ascontiguousarray(v.reshape(n, 128).T)


def prepare_shared(inp):
    L = DEPTH
    sh = {}
    for k in ("w_mod", "w_in", "w_a_o", "w_b_o", "w_c_o", "w_out", "w_ffn_in", "w_ffn_out"):
        sh[k] = np.ascontiguousarray(inp[k], dtype=np.float32)
    sh["w_q_up"] = np.ascontiguousarray(inp["mla_w_q_up"], dtype=np.float32)
    sh["w_kv_up"] = np.ascontiguousarray(inp["mla_w_kv_up"], dtype=np.float32)
    sh["b_modT"] = np.ascontiguousarray(np.stack([_col(inp["b_mod"][l], 48) for l in range(L)], 1))
    sh["n1T"] = np.ascontiguousarray(np.stack([_col(inp["norm1_w"][l], 8) for l in range(L)], 1))
    sh["n2T"] = np.ascontiguousarray(np.stack([_col(inp["norm2_w"][l], 8) for l in range(L)], 1))
    sh["fnT"] = _col(inp["final_norm_w"], 8)
    qn = np.stack([_col(inp["mla_q_norm_w"][l], 3) for l in range(L)], 1)
    sh["qnT"] = np.ascontiguousarray(qn)
    kvn = np.stack([_col(inp["mla_kv_norm_w"][l], 2) for l in range(L)], 1)
    sh["kvnT"] = np.ascontiguousarray(kvn)
    sh["gnT"] = np.ascontiguousarray(inp["gla_norm_w"].T)
    gg = np.zeros((L, 2, 17, 256), np.float32)
    gg[:, 0, :16] = inp["gla_w_gate_f"]
    gg[:, 0, 16] = inp["gla_b_gate_f"]
    gg[:, 1, :16] = inp["gla_w_gate_b"]
    gg[:, 1, 16] = inp["gla_b_gate_b"]
    sh["gla_gate"] = gg
    ia, ib, neg = build_na_gather_index(_PATS)
    hidx = np.arange(8)[None, None, :, None]
    tabs = []
    for l in range(L):
        g = inp["na_rpb"][l][hidx, ia, ib]
        tabs.append(np.ascontiguousarray(g.reshape(_NU, 128, 512).transpose(1, 0, 2)))
    sh["na_tab"] = np.ascontiguousarray(np.stack(tabs, 0), dtype=np.float32)
    sh["na_neg"] = np.ascontiguousarray(neg.reshape(_NU, 128, 512).transpose(1, 0, 2)) * np.float32(1.0 / NA_SCALE)
    C, S = rope_tables()
    sh["ropeC"], sh["ropeS"] = C, S
    sh["gmask"] = gla_masks()
    sh["ident"] = np.eye(128, dtype=np.float32)
    return sh


def prepare_core(inp, core):
    b0 = 2 * core
    h0 = np.empty((2, 128, 8, T), np.float32)
    cT = np.empty((128, 8, 3), np.float32)
    for e in range(2):
        full = np.concatenate([inp["x"][b0 + e], inp["ctx"][b0 + e]], 0)
        h0[e] = full.T.reshape(8, 128, T).transpose(1, 0, 2)
        cT[:, :, e] = _col(inp["c"][b0 + e], 8)
    cT[:, :, 2] = _col(inp["c_ctx"], 8)
    return {"h0": h0, "cT": cT}


_CACHE = {}


def kernel(**inputs):
    inp = {k: np.asarray(v) for k, v in inputs.items()}
    if "prog" not in _CACHE:
        _CACHE["prog"] = build_program()
    nc, _ = _CACHE["prog"]
    sh = prepare_shared(inp)
    in_maps = []
    for core in range(8):
        m = dict(sh)
        m.update(prepare_core(inp, core))
        in_maps.append(m)
    res = run_bass_kernel_spmd(nc, in_maps, core_ids=list(range(8)))
    out = np.empty((16, SEQ, D), np.float32)
    for core in range(8):
        o = res.results[core]["outT"]
        for e in range(2):
            out[2 * core + e] = o[e].transpose(2, 1, 0).reshape(SEQ, D)
    return out
```
